# Optimizing a Trainium2 kernel written in Bass

```python
import math
import jax
import jax.numpy as jnp
from jax import lax
import numpy as np

D_MODEL = 1024
BATCH = 16
SEQ = 2048
DEPTH = 4
DEC_BATCH = 8
DEC_SEQ = 16
PAST_LEN = 1024

CHUNK = 64
QBLOCK = 128
N_GROUPS = 4
HEAD_DIM = 64
GW = D_MODEL // N_GROUPS
N_HEADS = GW // HEAD_DIM
D_MIX = N_GROUPS * GW
GDN_CONV = 4
BAND_CHUNKS = 8
BAND_ROWS = BAND_CHUNKS * CHUNK
REL_CLIP = 2 * CHUNK
D_FF = 2816
FFN_CONV = 3
EPS = 1e-6

A_COLS = 4 * GW + 2 * N_HEADS
C_COLS = 3 * GW + N_HEADS
OFF_A = 0
OFF_B = OFF_A + A_COLS
OFF_C = OFF_B + 3 * GW
OFF_D = OFF_C + C_COLS
IN_COLS = OFF_D + 3 * GW

STATE_KEYS = ('gdn_conv', 'gdn_state', 'sb_k', 'sb_v', 'fox_k', 'fox_v', 'fox_logf', 'band_k', 'band_v', 'ffn_conv')

kernel_name = 'hybrid_streaming_encoder_step'

F32 = jnp.float32


def rms_norm(x, g):
    x32 = x.astype(F32)
    y = x32 * lax.rsqrt(jnp.mean(x32 * x32, axis=-1, keepdims=True) + EPS)
    return (y * g.astype(F32)).astype(x.dtype)


def l2_normalize(x):
    return x * lax.rsqrt(jnp.sum(x * x, axis=-1, keepdims=True) + EPS)


def ada_modulation(c, w, b):
    m = jax.nn.silu(c) @ w + b
    return jnp.split(m[:, None, :], 6, axis=-1)


def causal_depthwise_conv(x, w, state):
    width = w.shape[0]
    T = x.shape[1]
    if state is None:
        state = jnp.zeros((x.shape[0], width - 1, x.shape[2]), x.dtype)
    xp = jnp.concatenate([state.astype(x.dtype), x], axis=1)
    y = sum(w[i] * xp[:, i:i + T] for i in range(width))
    return y, xp[:, -(width - 1):]


def split_qkv(a):
    B, T, _ = a.shape
    a = a.reshape(B, T, 3, N_HEADS, HEAD_DIM)
    return a[:, :, 0], a[:, :, 1], a[:, :, 2]


def sweep_query_blocks(fn, q_arrays, q_pos):
    B, Tq = q_arrays[0].shape[:2]
    blk = QBLOCK if Tq % QBLOCK == 0 else Tq
    nb = Tq // blk
    split = lambda a: jnp.moveaxis(a.reshape(B, nb, blk, *a.shape[2:]), 1, 0)
    out = lax.map(fn, (tuple(split(a) for a in q_arrays), q_pos.reshape(nb, blk)))
    return jnp.moveaxis(out, 0, 1).reshape(B, Tq, *out.shape[3:])


def gated_delta_chunked(q, k, v, g, beta, s0):
    B, T, H, _ = q.shape
    L = min(CHUNK, T)
    n = T // L

    def chunks(a):
        a = a.reshape(B, n, L, *a.shape[2:])
        return jnp.moveaxis(jnp.moveaxis(a, 1, 0), 3, 2)

    idx = jnp.arange(L)
    incl = idx[:, None] >= idx[None, :]
    strict = idx[:, None] > idx[None, :]
    eye = jnp.eye(L, dtype=F32)

    def step(S, inp):
        qc, kc, vc, gc, bc = inp
        G = jnp.cumsum(gc, axis=-1)
        decay = jnp.exp(jnp.where(incl, G[..., :, None] - G[..., None, :], -jnp.inf))
        kk = jnp.einsum('bhid,bhjd->bhij', kc, kc)
        m = eye + jnp.where(strict, bc[..., :, None] * kk * decay, 0.0)
        u = lax.linalg.triangular_solve(m, vc * bc[..., None], left_side=True, lower=True)
        wk = lax.linalg.triangular_solve(m, kc * (bc * jnp.exp(G))[..., None], left_side=True, lower=True)
        v_new = u - jnp.einsum('bhik,bhkv->bhiv', wk, S)
        qk = jnp.einsum('bhid,bhjd->bhij', qc, kc) * decay
        o = (jnp.einsum('bhik,bhkv->bhiv', qc * jnp.exp(G)[..., None], S)
             + jnp.einsum('bhij,bhjv->bhiv', qk, v_new))
        g_last = G[..., -1:]
        S = (jnp.exp(g_last)[..., None] * S
             + jnp.einsum('bhik,bhiv->bhkv', kc * jnp.exp(g_last - G)[..., None], v_new))
        return S, o

    S, o = lax.scan(step, s0, (chunks(q), chunks(k), chunks(v), chunks(g), chunks(beta)))
    o = jnp.swapaxes(jnp.moveaxis(o, 0, 1), 2, 3).reshape(B, T, H, -1)
    return o, S


def gdn_mixer(cols, conv_w, a_log, dt_bias, norm_g, conv_state, rec_state):
    B, T, _ = cols.shape
    qkv, conv_new = causal_depthwise_conv(cols[..., :3 * GW], conv_w, conv_state)
    qkv = jax.nn.silu(qkv.astype(F32)).reshape(B, T, 3, N_HEADS, HEAD_DIM)
    q = l2_normalize(qkv[:, :, 0]) * HEAD_DIM ** -0.5
    k = l2_normalize(qkv[:, :, 1])
    v = qkv[:, :, 2]
    z = cols[..., 3 * GW:4 * GW].astype(F32).reshape(B, T, N_HEADS, HEAD_DIM)
    beta = jax.nn.sigmoid(cols[..., 4 * GW:4 * GW + N_HEADS].astype(F32))
    a_in = cols[..., 4 * GW + N_HEADS:A_COLS].astype(F32)
    g = -jnp.exp(a_log.astype(F32)) * jax.nn.softplus(a_in + dt_bias.astype(F32))
    if rec_state is None:
        rec_state = jnp.zeros((B, N_HEADS, HEAD_DIM, HEAD_DIM), F32)
    o, S = gated_delta_chunked(q, k, v, g, beta, rec_state.astype(F32))
    o = rms_norm(o, norm_g) * jax.nn.silu(z)
    return o.reshape(B, T, GW).astype(cols.dtype), conv_new, S.astype(cols.dtype)


def stick_breaking_attend(q, k, v, q_pos, k_pos):
    scale = HEAD_DIM ** -0.5
    k32, v32 = k.astype(F32), v.astype(F32)

    def block(args):
        (qb,), pb = args
        z = jnp.einsum('bqhd,bkhd->bhqk', qb.astype(F32), k32) * scale
        mask = k_pos[None, :] < pb[:, None]
        log_rest = jnp.where(mask, jax.nn.log_sigmoid(-z), 0.0)
        log_w = jax.nn.log_sigmoid(z) + lax.cumsum(log_rest, axis=3, reverse=True) - log_rest
        w = jnp.where(mask, jnp.exp(log_w), 0.0)
        return jnp.einsum('bhqk,bkhd->bqhd', w, v32)

    return sweep_query_blocks(block, (q,), q_pos)


def forgetting_attend(q, k, v, f_q, f_k, q_pos, k_pos):
    scale = HEAD_DIM ** -0.5
    k32, v32 = k.astype(F32), v.astype(F32)
    fk = jnp.swapaxes(f_k, 1, 2)[:, :, None, :]

    def block(args):
        (qb, fb), pb = args
        s = (jnp.einsum('bqhd,bkhd->bhqk', qb.astype(F32), k32) * scale
             + jnp.swapaxes(fb, 1, 2)[..., :, None] - fk)
        mask = k_pos[None, :] <= pb[:, None]
        p = jax.nn.softmax(jnp.where(mask, s, -jnp.inf), axis=-1)
        return jnp.einsum('bhqk,bkhd->bqhd', p, v32)

    return sweep_query_blocks(block, (q, f_q), q_pos)


def band_attend(q, k, v, q_pos, k_pos, valid, rel_table):
    rel = jnp.clip(q_pos[:, :, None] - k_pos[:, None, :], -REL_CLIP, REL_CLIP) + REL_CLIP
    bias = jnp.moveaxis(rel_table.astype(F32)[:, rel], 0, 1)
    s = jnp.einsum('bcqhd,bckhd->bchqk', q.astype(F32), k.astype(F32)) * HEAD_DIM ** -0.5 + bias
    s = jnp.where(valid[None, :, None, None, :], s, -jnp.inf)
    p = jax.nn.softmax(s, axis=-1)
    return jnp.einsum('bchqk,bckhd->bcqhd', p, v.astype(F32))


def trunk_layer(x, c, lp, cache):
    B, T, _ = x.shape
    past = cache is not None
    pos0 = cache['sb_k'].shape[1] if past else 0
    q_pos = pos0 + jnp.arange(T)
    k_pos = jnp.arange(pos0 + T)
    sh1, sc1, ga1, sh2, sc2, ga2 = ada_modulation(c, lp['ada_w'], lp['ada_b'])
    h = rms_norm(x, lp['norm_mix_g']) * (1 + sc1) + sh1
    cols = h @ lp['w_in']
    new = {}

    oa, new['gdn_conv'], new['gdn_state'] = gdn_mixer(
        cols[..., OFF_A:OFF_A + A_COLS], lp['gdn_conv_w'], lp['gdn_a_log'], lp['gdn_dt_bias'],
        lp['gdn_norm_g'], cache['gdn_conv'] if past else None, cache['gdn_state'] if past else None)

    qb, kb, vb = split_qkv(cols[..., OFF_B:OFF_B + 3 * GW])
    new['sb_k'], new['sb_v'] = kb, vb
    if past:
        kb = jnp.concatenate([cache['sb_k'].astype(kb.dtype), kb], axis=1)
        vb = jnp.concatenate([cache['sb_v'].astype(vb.dtype), vb], axis=1)
    ob = stick_breaking_attend(qb, kb, vb, q_pos, k_pos)

    qc, kc, vc = split_qkv(cols[..., OFF_C:OFF_C + 3 * GW])
    qc = rms_norm(qc, lp['fox_q_g'])
    kc = rms_norm(kc, lp['fox_k_g'])
    logf = jax.nn.log_sigmoid(cols[..., OFF_C + 3 * GW:OFF_C + C_COLS].astype(F32)
                              + lp['fox_b_f'].astype(F32))
    new['fox_k'], new['fox_v'], new['fox_logf'] = kc, vc, logf.astype(x.dtype)
    if past:
        kc = jnp.concatenate([cache['fox_k'].astype(kc.dtype), kc], axis=1)
        vc = jnp.concatenate([cache['fox_v'].astype(vc.dtype), vc], axis=1)
        logf = jnp.concatenate([cache['fox_logf'].astype(F32), logf], axis=1)
    f_cum = jnp.cumsum(logf, axis=1)
    oc = forgetting_attend(qc, kc, vc, f_cum[:, -T:], f_cum, q_pos, k_pos)

    qd, kd, vd = split_qkv(cols[..., OFF_D:OFF_D + 3 * GW])
    qd = rms_norm(qd, lp['band_q_g'])
    kd = rms_norm(kd, lp['band_k_g'])
    if past:
        R = cache['band_k'].shape[1]
        kd = jnp.concatenate([cache['band_k'].astype(kd.dtype), kd], axis=1)
        vd = jnp.concatenate([cache['band_v'].astype(vd.dtype), vd], axis=1)
        band_kpos = (pos0 - R + jnp.arange(R + T))[None]
        od = band_attend(qd[:, None], kd[:, None], vd[:, None], q_pos[None], band_kpos,
                         jnp.ones((1, R + T), bool), lp['band_rel_bias'])
        new['band_k'], new['band_v'] = kd[:, -R:], vd[:, -R:]
    else:
        n_chunks = T // CHUNK
        idx = jnp.arange(n_chunks)[:, None] * CHUNK + jnp.arange(BAND_ROWS + CHUNK)[None, :]
        pad = ((0, 0), (BAND_ROWS, 0), (0, 0), (0, 0))
        band_kpos = idx - BAND_ROWS
        od = band_attend(qd.reshape(B, n_chunks, CHUNK, N_HEADS, HEAD_DIM),
                         jnp.pad(kd, pad)[:, idx], jnp.pad(vd, pad)[:, idx],
                         q_pos.reshape(n_chunks, CHUNK), band_kpos, band_kpos >= 0,
                         lp['band_rel_bias'])
        keep = min(BAND_ROWS, T)
        new['band_k'], new['band_v'] = kd[:, -keep:], vd[:, -keep:]
    od = od.reshape(B, T, N_HEADS, HEAD_DIM)

    mg = lp['merge_g']
    mixed = jnp.concatenate(
        [oa.astype(x.dtype)]
        + [rms_norm(o, mg[i]).reshape(B, T, GW).astype(x.dtype) for i, o in enumerate((ob, oc, od))],
        axis=-1)
    x = x + ga1 * (mixed @ lp['w_o'])

    h = rms_norm(x, lp['norm_ffn_g']) * (1 + sc2) + sh2
    u, new['ffn_conv'] = causal_depthwise_conv(h @ lp['w_up'], lp['ffn_conv_w'],
                                               cache['ffn_conv'] if past else None)
    a, b = jnp.split(u, 2, axis=-1)
    x = x + ga2 * ((jax.nn.silu(a) * b) @ lp['w_down'])
    return x, new


def setup_inputs(seed: int = 0) -> dict:
    key = jax.random.key(seed)
    ks = iter(jax.random.split(key, 48))

    def nrm(shape, scale=1.0):
        return jax.random.normal(next(ks), shape, F32) * scale

    def gain(shape):
        return 1.0 + nrm(shape, 0.05)

    band_past = min(BAND_ROWS, PAST_LEN)
    dt = jnp.exp(jax.random.uniform(next(ks), (DEPTH, N_HEADS), F32, math.log(1e-3), math.log(1e-1)))
    a_init = jax.random.uniform(next(ks), (DEPTH, N_HEADS), F32, 1.0, 16.0)
    return {
        'x_prompt': nrm((BATCH, SEQ, D_MODEL)),
        'x_sample': nrm((DEC_BATCH, DEC_SEQ, D_MODEL)),
        'c_prompt': nrm((BATCH, D_MODEL)),
        'c_sample': nrm((DEC_BATCH, D_MODEL)),
        'state_gdn_conv': nrm((DEPTH, DEC_BATCH, GDN_CONV - 1, 3 * GW)),
        'state_gdn': nrm((DEPTH, DEC_BATCH, N_HEADS, HEAD_DIM, HEAD_DIM), 0.2),
        'cache_sb_k': nrm((DEPTH, DEC_BATCH, PAST_LEN, N_HEADS, HEAD_DIM)),
        'cache_sb_v': nrm((DEPTH, DEC_BATCH, PAST_LEN, N_HEADS, HEAD_DIM)),
        'cache_fox_k': nrm((DEPTH, DEC_BATCH, PAST_LEN, N_HEADS, HEAD_DIM)),
        'cache_fox_v': nrm((DEPTH, DEC_BATCH, PAST_LEN, N_HEADS, HEAD_DIM)),
        'cache_fox_logf': jax.nn.log_sigmoid(nrm((DEPTH, DEC_BATCH, PAST_LEN, N_HEADS)) + 2.0),
        'cache_band_k': nrm((DEPTH, DEC_BATCH, band_past, N_HEADS, HEAD_DIM)),
        'cache_band_v': nrm((DEPTH, DEC_BATCH, band_past, N_HEADS, HEAD_DIM)),
        'state_ffn_conv': nrm((DEPTH, DEC_BATCH, FFN_CONV - 1, 2 * D_FF)),
        'ada_w': nrm((DEPTH, D_MODEL, 6 * D_MODEL), 0.5 * D_MODEL ** -0.5),
        'ada_b': nrm((DEPTH, 6 * D_MODEL), 0.02),
        'norm_mix_g': gain((DEPTH, D_MODEL)),
        'w_in': nrm((DEPTH, D_MODEL, IN_COLS), D_MODEL ** -0.5),
        'gdn_conv_w': nrm((DEPTH, GDN_CONV, 3 * GW), GDN_CONV ** -0.5),
        'gdn_a_log': jnp.log(a_init),
        'gdn_dt_bias': dt + jnp.log(-jnp.expm1(-dt)),
        'gdn_norm_g': gain((DEPTH, HEAD_DIM)),
        'fox_q_g': gain((DEPTH, HEAD_DIM)),
        'fox_k_g': gain((DEPTH, HEAD_DIM)),
        'fox_b_f': 2.0 + nrm((DEPTH, N_HEADS), 0.5),
        'band_q_g': gain((DEPTH, HEAD_DIM)),
        'band_k_g': gain((DEPTH, HEAD_DIM)),
        'band_rel_bias': nrm((DEPTH, N_HEADS, 2 * REL_CLIP + 1), 0.2),
        'merge_g': gain((DEPTH, N_GROUPS - 1, N_HEADS, HEAD_DIM)),
        'w_o': nrm((DEPTH, D_MIX, D_MODEL), D_MIX ** -0.5),
        'norm_ffn_g': gain((DEPTH, D_MODEL)),
        'w_up': nrm((DEPTH, D_MODEL, 2 * D_FF), D_MODEL ** -0.5),
        'ffn_conv_w': nrm((DEPTH, FFN_CONV, 2 * D_FF), FFN_CONV ** -0.5),
        'w_down': nrm((DEPTH, D_FF, D_MODEL), D_FF ** -0.5),
    }


def reference(x_prompt, x_sample, c_prompt, c_sample, state_gdn_conv, state_gdn,
              cache_sb_k, cache_sb_v, cache_fox_k, cache_fox_v, cache_fox_logf,
              cache_band_k, cache_band_v, state_ffn_conv, ada_w, ada_b, norm_mix_g, w_in,
              gdn_conv_w, gdn_a_log, gdn_dt_bias, gdn_norm_g, fox_q_g, fox_k_g, fox_b_f,
              band_q_g, band_k_g, band_rel_bias, merge_g, w_o, norm_ffn_g, w_up, ffn_conv_w,
              w_down):
    new_p = {n: [] for n in STATE_KEYS}
    new_s = {n: [] for n in STATE_KEYS}
    y_p, y_s = x_prompt, x_sample
    for l in range(DEPTH):
        lp = {'ada_w': ada_w[l], 'ada_b': ada_b[l], 'norm_mix_g': norm_mix_g[l], 'w_in': w_in[l],
              'gdn_conv_w': gdn_conv_w[l], 'gdn_a_log': gdn_a_log[l], 'gdn_dt_bias': gdn_dt_bias[l],
              'gdn_norm_g': gdn_norm_g[l], 'fox_q_g': fox_q_g[l], 'fox_k_g': fox_k_g[l],
              'fox_b_f': fox_b_f[l], 'band_q_g': band_q_g[l], 'band_k_g': band_k_g[l],
              'band_rel_bias': band_rel_bias[l], 'merge_g': merge_g[l], 'w_o': w_o[l],
              'norm_ffn_g': norm_ffn_g[l], 'w_up': w_up[l], 'ffn_conv_w': ffn_conv_w[l],
              'w_down': w_down[l]}
        cache = {'gdn_conv': state_gdn_conv[l], 'gdn_state': state_gdn[l],
                 'sb_k': cache_sb_k[l], 'sb_v': cache_sb_v[l],
                 'fox_k': cache_fox_k[l], 'fox_v': cache_fox_v[l], 'fox_logf': cache_fox_logf[l],
                 'band_k': cache_band_k[l], 'band_v': cache_band_v[l], 'ffn_conv': state_ffn_conv[l]}
        y_p, st_p = trunk_layer(y_p, c_prompt, lp, None)
        y_s, st_s = trunk_layer(y_s, c_sample, lp, cache)
        for n in STATE_KEYS:
            new_p[n].append(st_p[n])
            new_s[n].append(st_s[n])
    return (y_p, y_s,
            jnp.stack(new_p['gdn_conv']), jnp.stack(new_p['gdn_state']),
            jnp.stack(new_p['sb_k']), jnp.stack(new_p['sb_v']),
            jnp.stack(new_p['fox_k']), jnp.stack(new_p['fox_v']), jnp.stack(new_p['fox_logf']),
            jnp.stack(new_p['band_k']), jnp.stack(new_p['band_v']), jnp.stack(new_p['ffn_conv']),
            jnp.stack(new_s['gdn_conv']), jnp.stack(new_s['gdn_state']),
            jnp.stack(new_s['sb_k']), jnp.stack(new_s['sb_v']),
            jnp.stack(new_s['fox_k']), jnp.stack(new_s['fox_v']), jnp.stack(new_s['fox_logf']),
            jnp.stack(new_s['band_k']), jnp.stack(new_s['band_v']), jnp.stack(new_s['ffn_conv']))
```

```python
import contextlib
import numpy as np
import concourse.bass as bass
import concourse.mybir as mybir
from concourse.bass_utils import run_bass_kernel_spmd

F32 = mybir.dt.float32
BF16 = mybir.dt.bfloat16
AF = mybir.ActivationFunctionType
ALU = mybir.AluOpType
AX = mybir.AxisListType

D = 1024
T = 2048
DEPTH = 4
TS = 16
PAST = 1024
NTOK = 2 * T + TS
INC = 3340
DFF = 2816
EPS = 1e-6


def _box(ap):
    t = ap.tensor
    name = t.name
    dims = [(int(s), int(c)) for s, c in ap.ap]
    off = int(ap.offset)
    if 'DRAM' in str(ap.space).upper():
        lo = hi = off
        for s, c in dims:
            if s >= 0:
                hi += s * (c - 1)
            else:
                lo += s * (c - 1)
        return (name, 0, 1, lo, hi + 1)
    import os
    if 'PSUM' in str(ap.space).upper() and (os.environ.get('KDBG_PSUMBOX', '1') == '1' or (os.environ.get('KDBG_PSUMBOX') == '2' and name.startswith('pbb'))):
        return (name, 0, 128, 0, 1 << 30)
    psize = 1
    for d in list(t.shape)[1:]:
        psize *= int(d)
    p0 = off // psize
    f0 = off % psize
    pstep, pcnt = dims[0]
    if pstep == psize or pcnt == 1:
        np_ = pcnt
    elif pstep == 0:
        np_ = 1
    else:
        np_ = 128 - p0
    lo = hi = f0
    for s, c in dims[1:]:
        if s >= 0:
            hi += s * (c - 1)
        else:
            lo += s * (c - 1)
    return (name, p0, p0 + np_, lo, hi + 1)


class Sync:
    def __init__(self, nc, stack, n_dma_sems=8):
        self.nc = nc
        self.eng = {'pe': nc.tensor, 'act': nc.scalar, 'dve': nc.vector, 'pool': nc.gpsimd, 'sp': nc.sync}
        self.sem = {}
        self.cnt = {}
        for e in ('pe', 'act', 'dve', 'pool'):
            self.sem[e] = stack.enter_context(nc.semaphore('s_' + e))
            self.cnt[e] = 0
        self.dsem = {}
        self.dcnt = {}
        for q in ('sp', 'pool'):
            self.dsem[q] = [stack.enter_context(nc.semaphore('d_%s%d' % (q, i))) for i in range(n_dma_sems)]
            self.dcnt[q] = 0
        self.K = n_dma_sems
        self.semobj = {}
        for e, s in self.sem.items():
            self.semobj[('c', e)] = s
        for q, l in self.dsem.items():
            for i, s in enumerate(l):
                self.semobj[('d', q, i)] = s
        self.waited = {e: {} for e in self.eng}
        self.W = {}
        self.R = {}
        self.n_wait = 0
        self.n_inst = 0

    def _collect(self, reads, writes):
        deps = {}
        for ap in reads:
            b = _box(ap)
            for r in self.W.get(b[0], ()):
                if r[0] < b[2] and b[1] < r[1] and r[2] < b[4] and b[3] < r[3]:
                    if deps.get(r[4], 0) < r[5]:
                        deps[r[4]] = r[5]
        for ap in writes:
            b = _box(ap)
            for r in self.W.get(b[0], ()):
                if r[0] < b[2] and b[1] < r[1] and r[2] < b[4] and b[3] < r[3]:
                    if deps.get(r[4], 0) < r[5]:
                        deps[r[4]] = r[5]
            rd = self.R.get(b[0])
            if rd:
                for k, v in rd.items():
                    if k[0] < b[2] and b[1] < k[1] and k[2] < b[4] and b[3] < k[3]:
                        if deps.get(k[4], 0) < v:
                            deps[k[4]] = v
        return deps

    def _record(self, reads, writes, semkey, val):
        for ap in writes:
            b = _box(ap)
            lst = self.W.setdefault(b[0], [])
            lst[:] = [r for r in lst if not (b[1] <= r[0] and r[1] <= b[2] and b[3] <= r[2] and r[3] <= b[4])]
            lst.append([b[1], b[2], b[3], b[4], semkey, val])
            rd = self.R.get(b[0])
            if rd:
                for k in [k for k in rd if b[1] <= k[0] and k[1] <= b[2] and b[3] <= k[2] and k[3] <= b[4]]:
                    del rd[k]
        for ap in reads:
            b = _box(ap)
            self.R.setdefault(b[0], {})[(b[1], b[2], b[3], b[4], semkey)] = val

    def _emit_waits(self, e, deps):
        w = self.waited[e]
        for k, v in deps.items():
            if k == ('c', 'pe') and e == 'pe':
                continue
            if w.get(k, 0) >= v:
                continue
            self.eng[e].wait_ge(self.semobj[k], v)
            w[k] = v
            self.n_wait += 1

    def op(self, e, fn, reads=(), writes=()):
        px = [a for a in reads if 'PSUM' in str(a.space).upper()]
        if px:
            writes = list(writes) + px
        deps = self._collect(reads, writes)
        self._emit_waits(e, deps)
        inst = fn()
        self.cnt[e] += 1
        inst.then_inc(self.sem[e], 1)
        self._record(reads, writes, ('c', e), self.cnt[e])
        self.n_inst += 1
        return inst

    def dma(self, q, out, in_, **kw):
        deps = self._collect([in_], [out])
        n = self.dcnt[q]
        k = n % self.K
        semkey = ('d', q, k)
        prev = (n // self.K) * 16
        if prev > 0:
            deps[semkey] = max(deps.get(semkey, 0), prev)
        self._emit_waits(q, deps)
        inst = self.eng[q].dma_start(out=out, in_=in_, **kw)
        inst.then_inc(self.semobj[semkey], 16)
        self.dcnt[q] = n + 1
        self._record([in_], [out], semkey, prev + 16)
        self.n_inst += 1
        return inst

    def barrier(self):
        toks = {}
        for e in ('pe', 'act', 'dve', 'pool'):
            if self.cnt[e] > 0:
                toks[('c', e)] = self.cnt[e]
        for q in self.dsem:
            n = self.dcnt[q]
            for k in range(self.K):
                cntk = (n - k + self.K - 1) // self.K if n > k else 0
                if cntk > 0:
                    toks[('d', q, k)] = cntk * 16
        for e in self.eng:
            w = self.waited[e]
            for k, v in toks.items():
                if k == ('c', 'pe') and e == 'pe':
                    continue
                if w.get(k, 0) >= v:
                    continue
                self.eng[e].wait_ge(self.semobj[k], v)
                w[k] = v
                self.n_wait += 1
        self.W = {}
        self.R = {}


class B:
    def __init__(self, nc, st, n_layers=DEPTH, stage=99):
        self.nc = nc
        self.st = st
        self.S = Sync(nc, st)
        self.n_layers = n_layers
        self.stage = stage
        self.uid = 0

    def sb(self, name, shape, dt, st=None):
        self.uid += 1
        return (st or self.st).enter_context(self.nc.sbuf_tensor('%s_%d' % (name, self.uid), shape, dt))

    def ps(self, name, shape, dt, st=None):
        self.uid += 1
        return (st or self.st).enter_context(self.nc.psum_tensor('%s_%d' % (name, self.uid), shape, dt))

    def din(self, name, shape):
        return self.nc.dram_tensor(name, list(shape), F32, kind="ExternalInput").ap()

    def dout(self, name, shape):
        return self.nc.dram_tensor(name, list(shape), F32, kind="ExternalOutput").ap()

    def mm(self, out, lhsT, rhs, start=True, stop=True, **kw):
        nc = self.nc
        return self.S.op('pe', lambda: nc.tensor.matmul(out, lhsT=lhsT, rhs=rhs, start=start, stop=stop, **kw),
                         reads=[lhsT, rhs], writes=[out])

    def tr(self, out, in_, ident):
        nc = self.nc
        return self.S.op('pe', lambda: nc.tensor.transpose(out=out, in_=in_, identity=ident),
                         reads=[in_, ident], writes=[out])

    def act(self, out, in_, func, bias=None, scale=1.0):
        nc = self.nc
        reads = [in_]
        kw = {}
        if bias is not None:
            kw['bias'] = bias
            if not isinstance(bias, (int, float)):
                reads.append(bias)
        if not isinstance(scale, (int, float)):
            reads.append(scale)
        return self.S.op('act', lambda: nc.scalar.activation(out=out, in_=in_, func=func, scale=scale, **kw),
                         reads=reads, writes=[out])

    def _ve(self, e):
        return self.nc.vector if e == 'dve' else self.nc.gpsimd

    def tt(self, out, in0, in1, op, e='dve'):
        eng = self._ve(e)
        return self.S.op(e, lambda: eng.tensor_tensor(out=out, in0=in0, in1=in1, op=op),
                         reads=[in0, in1], writes=[out])

    def ts(self, out, in0, s1, op0, s2=None, op1=None, e='dve'):
        eng = self._ve(e)
        reads = [in0]
        if not isinstance(s1, (int, float)):
            reads.append(s1)
        if s2 is not None and not isinstance(s2, (int, float)):
            reads.append(s2)
        if op1 is None:
            f = lambda: eng.tensor_scalar(out=out, in0=in0, scalar1=s1, scalar2=None, op0=op0)
        else:
            f = lambda: eng.tensor_scalar(out=out, in0=in0, scalar1=s1, scalar2=s2, op0=op0, op1=op1)
        return self.S.op(e, f, reads=reads, writes=[out])

    def stt(self, out, in0, scalar, in1, op0, op1, e='dve'):
        eng = self._ve(e)
        reads = [in0, in1]
        if not isinstance(scalar, (int, float)):
            reads.append(scalar)
        return self.S.op(e, lambda: eng.scalar_tensor_tensor(out=out, in0=in0, scalar=scalar, in1=in1, op0=op0, op1=op1),
                         reads=reads, writes=[out])

    def cp(self, out, in_, e='dve'):
        if e == 'act':
            return self.act(out, in_, AF.Copy)
        eng = self._ve(e)
        return self.S.op(e, lambda: eng.tensor_copy(out=out, in_=in_), reads=[in_], writes=[out])

    def red(self, out, in_, op=ALU.add, e='dve'):
        eng = self._ve(e)
        return self.S.op(e, lambda: eng.tensor_reduce(out=out, in_=in_, axis=AX.X, op=op), reads=[in_], writes=[out])

    def recip(self, out, in_):
        nc = self.nc
        return self.S.op('dve', lambda: nc.vector.reciprocal(out=out, in_=in_), reads=[in_], writes=[out])

    def memset(self, ap, v, e='pool'):
        eng = self._ve(e)
        return self.S.op(e, lambda: eng.memset(ap, v), writes=[ap])

    def asel(self, out, in_, pattern, cmp, fill, base, cm):
        nc = self.nc
        return self.S.op('pool', lambda: nc.gpsimd.affine_select(out=out, in_=in_, pattern=pattern, compare_op=cmp,
                                                               fill=fill, base=base, channel_multiplier=cm),
                         reads=[in_], writes=[out])

    def dma(self, out, in_, q='sp', **kw):
        return self.S.dma(q, out, in_, **kw)

    def rsqrt(self, out, in_, scale, tmp=None):
        t = tmp if tmp is not None else out
        self.act(t, in_, AF.Ln, bias=self.eps_col[0:int(in_.shape[0]), :], scale=scale)
        self.act(out, t, AF.Exp, scale=-0.5)


def bc(ap, shape):
    return ap.unsqueeze(len(ap.shape)).broadcast_to(list(shape))


def pbc(dram_ap_1d, n, parts=128):
    return bass.AP(tensor=dram_ap_1d.tensor, offset=int(dram_ap_1d.offset), ap=[[0, parts], [1, n]])


class Rot:
    def __init__(self, bufs):
        self.bufs = bufs
        self.i = 0

    def next(self):
        t = self.bufs[self.i % len(self.bufs)]
        self.i += 1
        return t


def build(n_layers=DEPTH, stage=99):
    nc = bass.Bass("TRN2", target_bir_lowering=False)
    with contextlib.ExitStack() as st:
        b = B(nc, st, n_layers, stage)
        emit(b)
        b.S.barrier()
        print("built: inst", b.S.n_inst, "waits", b.S.n_wait)
    return nc


def emit(b):
    nc = b.nc
    L = DEPTH
    NL = b.n_layers
    d = {}
    d['xT'] = b.din('xT', [D, NTOK])
    d['cT'] = b.din('cT', [128, 8, 4])
    d['gconvT'] = b.din('gconvT', [L, 64, 12, 3])
    d['gstate'] = b.din('gstate', [L, 4, 64, 64])
    for n in ('sbk', 'sbv', 'fxk', 'fxv'):
        d[n] = b.din(n, [L, PAST, 256])
    d['fxlf'] = b.din('fxlf', [L, PAST, 4])
    d['bdk'] = b.din('bdk', [L, 512, 256])
    d['bdv'] = b.din('bdv', [L, 512, 256])
    d['fconvT'] = b.din('fconvT', [L, 128, 44, 2])
    d['ada_w'] = b.din('ada_w', [L, D, 6 * D])
    d['ada_bT'] = b.din('ada_bT', [128, L, 48])
    d['nmgT'] = b.din('nmgT', [128, L, 8])
    d['nfgT'] = b.din('nfgT', [128, L, 8])
    d['w_in'] = b.din('w_in', [L, D, INC])
    d['gcwT'] = b.din('gcwT', [64, L, 12, 4])
    d['a_log'] = b.din('a_log', [L, 4])
    d['dt_bias'] = b.din('dt_bias', [L, 4])
    d['gn_g'] = b.din('gn_g', [L, 64])
    d['fq_g'] = b.din('fq_g', [L, 64])
    d['fk_g'] = b.din('fk_g', [L, 64])
    d['fb_f'] = b.din('fb_f', [L, 4])
    d['bq_g'] = b.din('bq_g', [L, 64])
    d['bk_g'] = b.din('bk_g', [L, 64])
    d['rel'] = b.din('rel', [L, 4, 257])
    d['mg'] = b.din('mg', [L, 768])
    d['w_o'] = b.din('w_o', [L, D, D])
    d['w_up'] = b.din('w_up', [L, D, 2 * DFF])
    d['fcwT'] = b.din('fcwT', [128, L, 44, 3])
    d['w_down'] = b.din('w_down', [L, DFF, D])
    o = {}
    o['yT'] = b.dout('yT', [D, NTOK])
    o['p_gconv'] = b.dout('p_gconv', [L, 2, 3, 768])
    o['p_gstate'] = b.dout('p_gstate', [L, 2, 4, 64, 64])
    for n in ('p_sbk', 'p_sbv', 'p_fxk', 'p_fxv'):
        o[n] = b.dout(n, [L, 2, T, 256])
    o['p_fxlf'] = b.dout('p_fxlf', [L, 2, T, 4])
    o['p_bdk'] = b.dout('p_bdk', [L, 2, 512, 256])
    o['p_bdv'] = b.dout('p_bdv', [L, 2, 512, 256])
    o['p_fconv'] = b.dout('p_fconv', [L, 2, 2, 2 * DFF])
    o['s_gconv'] = b.dout('s_gconv', [L, 3, 768])
    o['s_gstate'] = b.dout('s_gstate', [L, 4, 64, 64])
    for n in ('s_sbk', 's_sbv', 's_fxk', 's_fxv'):
        o[n] = b.dout(n, [L, TS, 256])
    o['s_fxlf'] = b.dout('s_fxlf', [L, TS, 4])
    o['s_bdk'] = b.dout('s_bdk', [L, 512, 256])
    o['s_bdv'] = b.dout('s_bdv', [L, 512, 256])
    o['s_fconv'] = b.dout('s_fconv', [L, 2, 2 * DFF])
    b.d, b.o = d, o

    identb = b.sb('identb', [128, 128], BF16)
    identf = b.sb('identf', [128, 128], F32)
    onesb = b.sb('onesb', [128, 128], BF16)
    onesf = b.sb('onesf', [128, 128], F32)
    triUb = b.sb('triUb', [128, 128], BF16)
    triIncf = b.sb('triIncf', [128, 128], F32)
    sutf = b.sb('sutf', [64, 64], F32)
    antiJ = b.sb('antiJ', [128, 128], F32)
    blkb = b.sb('blkb', [128, 128], BF16)
    mask0 = b.sb('mask0', [128, 4, 128], BF16)
    mask4 = b.sb('mask4', [128, 4, 128], BF16)
    b.eps_col = b.sb('eps_col', [128, 1], F32)
    b.memset(b.eps_col[:], EPS)
    b.memset(identf[:], 0.0)
    b.asel(identf[:], identf[:], [[-1, 128]], ALU.not_equal, 1.0, 0, 1)
    b.cp(identb[:], identf[:])
    b.memset(onesb[:], 1.0)
    b.memset(onesf[:], 1.0)
    b.asel(triUb[:], onesb[:], [[-1, 128]], ALU.is_ge, 0.0, 0, 1)
    b.asel(triIncf[:], onesf[:], [[1, 128]], ALU.is_ge, 0.0, 0, -1)
    b.asel(sutf[:], onesf[0:64, 0:64], [[1, 64]], ALU.is_gt, 0.0, 0, -1)
    b.memset(antiJ[:], 0.0)
    b.asel(antiJ[:], antiJ[:], [[1, 128]], ALU.not_equal, 1.0, -127, 1)
    b.memset(blkb[:], 0.0)
    b.memset(blkb[0:64, 0:64], 1.0)
    b.memset(blkb[64:128, 64:128], 1.0)
    b.memset(mask0[:], 1.0)
    b.memset(mask0[0:64, :, 64:128], 0.0)
    b.memset(mask4[:], 1.0)
    b.memset(mask4[64:128, :, 0:64], 0.0)
    C = dict(identb=identb, identf=identf, onesb=onesb, onesf=onesf, triUb=triUb, triIncf=triIncf, sutf=sutf,
             antiJ=antiJ, blkb=blkb, mask0=mask0, mask4=mask4)
    b.C = C

    b.P = [b.ps('pb%d' % i, [128, 512], F32) for i in range(7)]
    b.PB = b.ps('pbb', [128, 1024], BF16)

    nmg = b.sb('nmg', [128, L, 8], F32)
    nfg = b.sb('nfg', [128, L, 8], F32)
    adab = b.sb('adab', [128, L, 48], F32)
    gcw = b.sb('gcw', [64, L, 12, 4], F32)
    fcw = b.sb('fcw', [128, L, 44, 3], F32)
    b.dma(nmg[:], d['nmgT'])
    b.dma(nfg[:], d['nfgT'])
    b.dma(adab[:], d['ada_bT'])
    b.dma(gcw[:], d['gcwT'])
    b.dma(fcw[:], d['fcwT'])
    b.gcw, b.fcw = gcw, fcw

    mod = b.sb('mod', [128, L, 48, 4], F32)
    modA = b.sb('modA', [128, L, 2, 8, 4], F32)
    b.mod, b.modA = mod, modA
    with contextlib.ExitStack() as s1:
        cT = b.sb('cT', [128, 8, 4], F32, s1)
        sc = b.sb('sc', [128, 8, 4], F32, s1)
        tmp = b.sb('sctmp', [128, 8, 4], F32, s1)
        b.dma(cT[:], d['cT'])
        b.act(tmp[:], cT[:], AF.Exp, scale=-1.0)
        b.ts(tmp[:], tmp[:], 1.0, ALU.add)
        b.recip(tmp[:], tmp[:])
        b.tt(sc[:], cT[:], tmp[:], ALU.mult)
        awr = Rot([b.sb('aw%d' % i, [128, 8, 768], F32, s1) for i in range(2)])
        for l in range(NL):
            pm = b.P[l % 2]
            for pc in range(8):
                aw = awr.next()
                b.dma(aw[:], d['ada_w'][l, :, pc * 768:(pc + 1) * 768].rearrange("(c p) n -> p c n", p=128))
                for j in range(6):
                    oc = pc * 6 + j
                    for c in range(8):
                        b.mm(pm[:, oc * 4:oc * 4 + 4], aw[:, c, j * 128:(j + 1) * 128], sc[:, c, :],
                             start=(c == 0), stop=(c == 7))
            pmv = pm[:, 0:192].rearrange("p (a s) -> p a s", s=4)
            b.tt(mod[:, l, :, :], pmv, bc(adab[:, l, :], [128, 48, 4]), ALU.add)
            for w, g in ((0, nmg), (1, nfg)):
                scv = mod[:, l, (1 + 3 * w) * 8:(2 + 3 * w) * 8, :]
                b.ts(modA[:, l, w, :, :], scv, 1.0, ALU.add)
                b.tt(modA[:, l, w, :, :], modA[:, l, w, :, :], bc(g[:, l, :], [128, 8, 4]), ALU.mult)
    b.S.barrier()

    for l in range(NL):
        layer(b, l)


def rmsnorm_tok(b, out, src, n, np_, gains, wk, den=None):
    sq, ss, y = wk
    x = src
    if den is not None:
        b.recip(ss[0:np_, 0:n], den)
        b.tt(y[0:np_, 0:n, :], src, bc(ss[0:np_, 0:n], [np_, n, 64]), ALU.mult)
        x = y[0:np_, 0:n, :]
    b.act(sq[0:np_, 0:n, :], x, AF.Square)
    b.red(ss[0:np_, 0:n], sq[0:np_, 0:n, :])
    b.rsqrt(ss[0:np_, 0:n], ss[0:np_, 0:n], 1.0 / 64.0)
    if gains is None:
        b.tt(out, x, bc(ss[0:np_, 0:n], [np_, n, 64]), ALU.mult)
    else:
        b.tt(y[0:np_, 0:n, :], x, bc(ss[0:np_, 0:n], [np_, n, 64]), ALU.mult)
        b.tt(out, y[0:np_, 0:n, :], gains[0:np_, 0:n, :], ALU.mult, e='pool')


def layer(b, l):
    nc = b.nc
    d, o, C, P = b.d, b.o, b.C, b.P
    src_x = d['xT'] if l == 0 else o['yT']
    with contextlib.ExitStack() as s:
        win = b.sb('win', [128, 8, INC], BF16, s)
        wo = b.sb('wo', [128, 8, D], BF16, s)
        for c in range(8):
            for hh in range(2):
                b.dma(win[:, c, hh * 1670:(hh + 1) * 1670], d['w_in'][l, c * 128:(c + 1) * 128, hh * 1670:(hh + 1) * 1670], q='pool')
        for c in range(8):
            b.dma(wo[:, c, :], d['w_o'][l, c * 128:(c + 1) * 128, :], q='pool')
        gC = b.sb('gC', [128, 8, 64], F32, s)
        gD = b.sb('gD', [128, 8, 64], F32, s)
        gG = b.sb('gG', [128, 4, 64], F32, s)
        mg = b.sb('mg', [128, 12, 64], F32, s)
        dtb = b.sb('dtb', [128, 4], F32, s)
        nexpA = b.sb('nexpA', [128, 4], F32, s)
        fbf = b.sb('fbf', [128, 4], F32, s)

        def bcl(t, n, rep):
            return bass.AP(tensor=t.tensor, offset=int(t.offset), ap=[[0, 128], [0, rep], [1, n]])
        b.dma(gC[:, 0:4, :], bcl(d['fq_g'][l], 64, 4))
        b.dma(gC[:, 4:8, :], bcl(d['fk_g'][l], 64, 4))
        b.dma(gD[:, 0:4, :], bcl(d['bq_g'][l], 64, 4))
        b.dma(gD[:, 4:8, :], bcl(d['bk_g'][l], 64, 4))
        b.dma(gG[:], bcl(d['gn_g'][l], 64, 4))
        b.dma(mg[:].rearrange("p a b -> p (a b)"), pbc(d['mg'][l], 768))
        b.dma(dtb[:], pbc(d['dt_bias'][l], 4))
        b.dma(nexpA[:], pbc(d['a_log'][l], 4))
        b.dma(fbf[:], pbc(d['fb_f'][l], 4))
        b.act(nexpA[:], nexpA[:], AF.Exp)
        b.ts(nexpA[:], nexpA[:], -1.0, ALU.mult)
        relx = nc.dram_tensor('relx%d' % l, [4, 520], F32, kind="Internal").ap()
        with contextlib.ExitStack() as s2:
            rl = b.sb('rl', [4, 257], F32, s2)
            rx = b.sb('rx', [4, 520], F32, s2)
            b.dma(rl[:], d['rel'][l])
            b.memset(rx[:], 0.0)
            b.cp(rx[:, 128:385], rl[:])
            b.cp(rx[:, 0:128], rl[:, 0:1].broadcast_to([4, 128]))
            b.cp(rx[:, 385:513], rl[:, 256:257].broadcast_to([4, 128]))
            b.dma(relx, rx[:])
        Bt = b.sb('Bt', [128, 3, 4, 128], F32, s)
        with contextlib.ExitStack() as s2:
            tz = b.sb('tz', [128, 4, 128], F32, s2)
            for kind, delta in ((0, -384), (1, -128), (2, 0)):
                for h in range(4):
                    if delta <= -384:
                        src = bass.AP(tensor=relx.tensor, offset=int(relx[h, 385:386].offset), ap=[[0, 128], [1, 128]])
                    else:
                        src = bass.AP(tensor=relx.tensor, offset=int(relx[h, 0:1].offset) + (129 - delta), ap=[[1, 128], [1, 128]])
                    b.dma(tz[:, h, :], src)
                pz = P[0]
                b.mm(pz[:, 0:512], C['antiJ'][:], tz[:].rearrange("p a b -> p (a b)"))
                b.cp(Bt[:, kind, :, :].rearrange("p a b -> p (a b)"), pz[:, 0:512])
        KT = [b.sb('KT%d' % i, [128, 2, T if i < 2 else 1024], BF16, s) for i in range(3)]
        V = [b.sb('V%d' % i, [128, 16 if i < 2 else 8, 4, 65], BF16, s) for i in range(3)]
        for i in range(3):
            b.memset(V[i][:], 1.0)
        W = dict(win=win, wo=wo, gC=gC, gD=gD, gG=gG, mg=mg, dtb=dtb, nexpA=nexpA, fbf=fbf, Bt=Bt, KT=KT, V=V)
        alloc_work(b, W, s)
        import os
        for si in [int(c) for c in os.environ.get('KDBG_SEQS', '012')]:
            phaseAB(b, l, si, W, src_x)
    b.S.barrier()
    phaseC(b, l)
    b.S.barrier()


GP = 256
SKIP_D = False
OFFS = [1032, 1800, 2572]


def mid_bc(ap2d, n):
    a = [list(x) for x in ap2d.ap]
    return bass.AP(tensor=ap2d.tensor, offset=int(ap2d.offset), ap=[a[0], [0, n], a[1]])


def alloc_work(b, W, s):
    G = GP
    W['xg'] = b.sb('xg', [128, 8, G], F32, s)
    W['sqr'] = Rot([b.sb('sqr%d' % i, [128, G], BF16, s) for i in range(2)])
    W['rstd'] = b.sb('rstd', [128, G], F32, s)
    W['htmp'] = Rot([b.sb('htmp%d' % i, [128, G], F32, s) for i in range(2)])
    W['hT'] = b.sb('hT', [128, 8, G], BF16, s)
    W['QT'] = [b.sb('QT%d' % i, [128, 2, G], BF16, s) for i in range(3)]
    W['stg'] = Rot([b.sb('stg%d' % i, [128, 512], F32, s) for i in range(3)])
    W['qkb'] = Rot([b.sb('qkb%d' % i, [128, 512], BF16, s) for i in range(2)])
    W['nwk'] = (b.sb('nsq', [128, 8, 64], F32, s), b.sb('nss', [128, 8], F32, s), b.sb('ny', [128, 8, 64], F32, s))
    W['lf'] = b.sb('lf', [128, 4], F32, s)
    W['lfc'] = b.sb('lfc', [128, 8, 4], F32, s)
    W['negF'] = b.sb('negF', [128, 17, 4], F32, s)
    W['carry'] = b.sb('carry', [128, 4], F32, s)
    W['fbias'] = b.sb('fbias', [128, 17, 4], F32, s)
    W['mixed'] = b.sb('mixed', [128, G // 128, 1024], BF16, s)
    W['mixedT'] = b.sb('mixedT', [128, 8, G], BF16, s)
    W['E'] = b.sb('E', [128, G], F32, s)
    W['SP'] = Rot([b.sb('SP%d' % i, [128, G], BF16, s) for i in range(2)])
    W['zr'] = Rot([b.sb('zr%d' % i, [128, G], F32, s) for i in range(2)])
    W['Wt'] = Rot([b.sb('Wt%d' % i, [128, 512], BF16, s) for i in range(3)])
    W['Rr'] = b.sb('Rr', [128, G], F32, s)
    W['bt'] = Rot([b.sb('bt%d' % i, [128, 512], F32, s) for i in range(1)])
    alloc_gdn(b, W, s)
    print('sbuf remaining after alloc', b.nc.sbuf_bytes_remaining)


def norm_mod(b, l, si, which, xg, G, W, hT):
    C, P = b.C, b.P
    ss = P[0][:, 0:G]
    for c in range(8):
        sq = W['sqr'].next()
        b.act(sq[:, 0:G], xg[:, c, 0:G], AF.Square)
        b.mm(ss, C['onesb'][:], sq[:, 0:G], start=(c == 0), stop=(c == 7))
    b.rsqrt(W['rstd'][:, 0:G], ss, 1.0 / D)
    for c in range(8):
        ht = W['htmp'].next()
        b.stt(ht[:, 0:G], xg[:, c, 0:G], b.modA[:, l, which, c, si:si + 1], W['rstd'][:, 0:G], ALU.mult, ALU.mult)
        b.act(hT[:, c, 0:G], ht[:, 0:G], AF.Identity, bias=b.mod[:, l, which * 24 + c, si:si + 1])


def phaseAB(b, l, si, W, src_x):
    nc = b.nc
    d, o, C, P, PB = b.d, b.o, b.C, b.P, b.PB
    prompt = si < 2
    Tq = T if prompt else TS
    tok0 = si * T if prompt else 2 * T
    G = GP if prompt else TS
    ngroups = Tq // G
    npast = 0 if prompt else PAST
    npastD = 0 if prompt else 512
    KT, V, win, wo, mg = W['KT'], W['V'], W['win'], W['wo'], W['mg']
    xg, hT, QT, mixed, mixedT = W['xg'], W['hT'], W['QT'], W['mixed'], W['mixedT']
    identb = C['identb']
    pbv = PB[:, 0:1024].rearrange("p (j t) -> p j t", t=128)
    pf = P[3]
    carry, negF, lf = W['carry'], W['negF'], W['lf']
    b.memset(carry[:], 0.0)
    b.memset(negF[:], 0.0)
    if not prompt:
        for i, (kn, vn, nrow) in enumerate((('sbk', 'sbv', 1024), ('fxk', 'fxv', 1024), ('bdk', 'bdv', 512))):
            for tl in range(nrow // 128):
                b.dma(V[i][:, tl, :, 0:64], d[vn][l, tl * 128:(tl + 1) * 128, :].rearrange("p (h e) -> p h e", e=64), q='pool')
                kb = W['qkb'].next()
                b.dma(kb[:, 0:256], d[kn][l, tl * 128:(tl + 1) * 128, :], q='pool')
                for j in range(2):
                    b.tr(pbv[:, j, :], kb[:, j * 128:(j + 1) * 128], identb[:])
                b.cp(KT[i][:, :, tl * 128:(tl + 1) * 128], pbv[:, 0:2, :])
        lfc = W['lfc']
        b.dma(lfc[:], d['fxlf'][l].rearrange("(t p) h -> p t h", p=128))
        for tl in range(8):
            b.mm(pf[:, 0:4], C['triIncf'][:], lfc[:, tl, :])
            b.stt(negF[:, tl, :], pf[:, 0:4], -1.0, carry[:], ALU.mult, ALU.subtract)
            b.mm(pf[:, 4:8], C['onesf'][:], lfc[:, tl, :])
            b.tt(carry[:], carry[:], pf[:, 4:8], ALU.add)
        b.dma(o['s_bdk'][l, 0:496, :], d['bdk'][l, 16:512, :])
        b.dma(o['s_bdv'][l, 0:496, :], d['bdv'][l, 16:512, :])
    gdn_seq_init(b, l, si, W)

    import os
    for g in range(min(ngroups, int(os.environ.get('KDBG_MAXG', '99')))):
        t0 = g * G
        b.dma(xg[:, :, 0:G], src_x[:, tok0 + t0:tok0 + t0 + G].rearrange("(c p) t -> p c t", p=128))
        norm_mod(b, l, si, 0, xg, G, W, hT)
        ntile = max(1, G // 128)
        nt = min(128, G)
        for tt in range(ntile):
            cols = slice(tt * 128, tt * 128 + nt)
            for i in range(3):
                c0 = OFFS[i]
                pqk, pv = P[1], P[2]
                nv = 260 if i == 1 else 256
                for c in range(8):
                    b.mm(pqk[0:nt, 0:512], hT[:, c, cols], win[:, c, c0:c0 + 512], start=(c == 0), stop=(c == 7))
                for c in range(8):
                    b.mm(pv[0:nt, 0:nv], hT[:, c, cols], win[:, c, c0 + 512:c0 + 512 + nv], start=(c == 0), stop=(c == 7))
                kbase = (npast if i < 2 else npastD) + t0 + tt * 128
                kt_i = kbase // 128
                if i == 2:
                    kbase = kbase % 1024
                    kt_i = kt_i % 8
                sv = W['stg'].next()
                b.cp(sv[0:nt, 0:256], pv[0:nt, 0:256], e='act')
                b.cp(V[i][0:nt, kt_i, :, 0:64], sv[0:nt, 0:256].rearrange("p (h e) -> p h e", e=64))
                sk = W['stg'].next()
                if i == 0:
                    b.cp(sk[0:nt, 0:512], pqk[0:nt, 0:512])
                else:
                    rmsnorm_tok(b, sk[0:nt, 0:512].rearrange("p (h e) -> p h e", e=64),
                                pqk[0:nt, 0:512].rearrange("p (h e) -> p h e", e=64), 8, nt,
                                W['gC'] if i == 1 else W['gD'], W['nwk'])
                tloc = t0 + tt * 128
                if prompt:
                    if i == 0:
                        b.dma(o['p_sbk'][l, si, tloc:tloc + nt, :], sk[0:nt, 256:512])
                        b.dma(o['p_sbv'][l, si, tloc:tloc + nt, :], sv[0:nt, 0:256])
                    elif i == 1:
                        b.dma(o['p_fxk'][l, si, tloc:tloc + nt, :], sk[0:nt, 256:512])
                        b.dma(o['p_fxv'][l, si, tloc:tloc + nt, :], sv[0:nt, 0:256])
                    elif tloc >= T - 512:
                        b.dma(o['p_bdk'][l, si, tloc - (T - 512):tloc - (T - 512) + nt, :], sk[0:nt, 256:512])
                        b.dma(o['p_bdv'][l, si, tloc - (T - 512):tloc - (T - 512) + nt, :], sv[0:nt, 0:256])
                else:
                    if i == 0:
                        b.dma(o['s_sbk'][l, 0:nt, :], sk[0:nt, 256:512])
                        b.dma(o['s_sbv'][l, 0:nt, :], sv[0:nt, 0:256])
                    elif i == 1:
                        b.dma(o['s_fxk'][l, 0:nt, :], sk[0:nt, 256:512])
                        b.dma(o['s_fxv'][l, 0:nt, :], sv[0:nt, 0:256])
                    else:
                        b.dma(o['s_bdk'][l, 496:512, :], sk[0:nt, 256:512])
                        b.dma(o['s_bdv'][l, 496:512, :], sv[0:nt, 0:256])
                qb = W['qkb'].next()
                b.cp(qb[0:nt, :], sk[0:nt, 0:512], e='act')
                for j in range(4):
                    b.tr(pbv[:, j, 0:nt], qb[0:nt, j * 128:(j + 1) * 128], identb[0:nt, 0:nt])
                b.cp(QT[i][:, :, tt * 128:tt * 128 + nt], pbv[:, 0:2, 0:nt])
                b.cp(KT[i][:, :, kbase:kbase + nt], pbv[:, 2:4, 0:nt], e='act')
                if i == 1:
                    b.tt(lf[0:nt, :], pv[0:nt, 256:260], W['fbf'][0:nt, :], ALU.add)
                    b.act(lf[0:nt, :], lf[0:nt, :], AF.Exp, scale=-1.0)
                    b.act(lf[0:nt, :], lf[0:nt, :], AF.Ln, bias=1.0)
                    b.ts(lf[0:nt, :], lf[0:nt, :], -1.0, ALU.mult)
                    if prompt:
                        b.dma(o['p_fxlf'][l, si, tloc:tloc + nt, :], lf[0:nt, :])
                    else:
                        b.dma(o['s_fxlf'][l, 0:nt, :], lf[0:nt, :])
                    b.mm(pf[0:nt, 0:4], C['triIncf'][0:nt, 0:nt], lf[0:nt, :])
                    b.stt(negF[0:nt, kt_i, :], pf[0:nt, 0:4], -1.0, carry[0:nt, :], ALU.mult, ALU.subtract)
                    b.mm(pf[:, 4:8], C['onesf'][0:nt, :], lf[0:nt, :])
                    b.tt(carry[:], carry[:], pf[:, 4:8], ALU.add)
        if b.stage < 2:
            continue
        nsb = ntile
        nq = nt
        qpos0 = npast + t0
        jmax = (qpos0 + G - 1) // 128
        Ops = P[4]
        zc = 0
        for h in range(4 if b.stage >= 2 else 0):
            hp, po = h // 2, (h % 2) * 64
            Rr = W['Rr']
            b.memset(Rr[:, 0:G], 0.0)
            first = True
            for j in range(jmax, -1, -1):
                nk = min(128, npast + Tq - j * 128)
                m = j * 128 - qpos0
                diag = m >= 0
                pz = P[5 + zc % 2]
                zc += 1
                b.mm(pz[0:nk, 0:G], KT[0][po:po + 64, hp, j * 128:j * 128 + nk], QT[0][po:po + 64, hp, 0:G])
                lvl = int(os.environ.get('KDBG_B', '9'))
                E = W['E']
                b.act(E[0:nk, 0:G], pz[0:nk, 0:G], AF.Exp, scale=0.125)
                if lvl < 2:
                    continue
                sp = W['SP'].next()
                b.act(sp[0:nk, 0:G], E[0:nk, 0:G], AF.Ln, bias=1.0)
                if diag and lvl >= 3:
                    b.asel(sp[0:nk, 0:G], sp[0:nk, 0:G], [[1, G]], ALU.is_gt, 0.0, -m, -1)
                if lvl < 4:
                    continue
                vv = int(os.environ.get('KDBG_V', '9'))
                b.mm(P[3][0:nk, 0:G], C['triUb'][0:nk, 0:nk], sp[0:nk, 0:G])
                if j > 0 and vv >= 2:
                    b.mm(P[3][:, 256:256 + G], C['onesb'][0:nk, :], sp[0:nk, 0:G])
                zr = W['zr'].next()
                if vv >= 3:
                    b.act(zr[0:nk, 0:G], pz[0:nk, 0:G], AF.Identity, scale=0.125)
                    b.tt(zr[0:nk, 0:G], zr[0:nk, 0:G], Rr[0:nk, 0:G], ALU.subtract)
                if vv >= 4:
                    b.tt(zr[0:nk, 0:G], zr[0:nk, 0:G], P[3][0:nk, 0:G], ALU.subtract)
                if lvl < 5:
                    continue
                wt = W['Wt'].next()
                b.act(wt[0:nk, 0:G], zr[0:nk, 0:G], AF.Exp)
                if diag:
                    b.asel(wt[0:nk, 0:G], wt[0:nk, 0:G], [[1, G]], ALU.is_gt, 0.0, -m, -1)
                if j > 0:
                    b.tt(Rr[:, 0:G], Rr[:, 0:G], P[3][:, 256:256 + G], ALU.add)
                if lvl < 6:
                    continue
                for sb_ in range(nsb):
                    if diag and m >= sb_ * 128 + nq:
                        continue
                    b.mm(Ops[0:nq, sb_ * 64:(sb_ + 1) * 64], wt[0:nk, sb_ * 128:sb_ * 128 + nq], V[0][0:nk, j, h, 0:64],
                         start=first, stop=(j == 0), skip_group_check=True)
                    first = False
            if int(os.environ.get('KDBG_B', '9')) >= 7:
                rmsnorm_tok(b, mixed[0:nq, 0:nsb, 256 + h * 64:256 + (h + 1) * 64],
                            Ops[0:nq, 0:nsb * 64].rearrange("p (s e) -> p s e", e=64), nsb, nq,
                            mid_bc(mg[:, h, :], nsb), W['nwk'])
        nj = jmax + 1
        b.tt(W['fbias'][:, 0:nj, :], negF[:, 0:nj, :], mid_bc(carry[:, :], nj), ALU.add)
        FQ = W['E']
        for h in range(4 if b.stage >= 3 else 0):
            hp, po = h // 2, (h % 2) * 64
            first = True
            pfq = P[3]
            dg = W['nwk'][0][:, 0:2, :].rearrange("p a e -> p (a e)")
            for sb_ in range(nsb):
                jq_ = qpos0 // 128 + sb_
                b.ts(dg[0:nq, 0:nq], C['identf'][0:nq, 0:nq], W['fbias'][0:nq, jq_, h:h + 1], ALU.mult)
                b.mm(pfq[:, sb_ * 128:sb_ * 128 + nq], C['onesf'][0:nq, :], dg[0:nq, 0:nq])
            b.cp(FQ[:, 0:G], pfq[:, 0:G])
            for j in range(0, jmax + 1):
                nk = min(128, npast + Tq - j * 128)
                m = j * 128 - qpos0
                diag = m >= 0
                pz = P[5 + zc % 2]
                zc += 1
                b.mm(pz[0:nk, 0:G], KT[1][po:po + 64, hp, j * 128:j * 128 + nk], QT[1][po:po + 64, hp, 0:G])
                wt = W['Wt'].next()
                s_ = W['zr'].next()
                b.stt(s_[0:nk, 0:G], pz[0:nk, 0:G], 0.125, FQ[0:nk, 0:G], ALU.mult, ALU.subtract)
                if diag:
                    b.ts(s_[0:nk, 0:G], s_[0:nk, 0:G], W['fbias'][0:nk, j, h:h + 1], ALU.add, 30.0, ALU.min)
                    b.act(wt[0:nk, 0:G], s_[0:nk, 0:G], AF.Exp)
                    b.asel(wt[0:nk, 0:G], wt[0:nk, 0:G], [[1, G]], ALU.is_ge, 0.0, -m, -1)
                else:
                    b.act(wt[0:nk, 0:G], s_[0:nk, 0:G], AF.Exp, bias=W['fbias'][0:nk, j, h:h + 1])
                for sb_ in range(nsb):
                    if diag and m >= sb_ * 128 + nq:
                        continue
                    b.mm(Ops[0:nq, sb_ * 65:(sb_ + 1) * 65], wt[0:nk, sb_ * 128:sb_ * 128 + nq], V[1][0:nk, j, h, 0:65],
                         start=first, stop=(j == jmax), skip_group_check=True)
                    first = False
            ov = Ops[0:nq, 0:nsb * 65].rearrange("p (s e) -> p s e", e=65)
            rmsnorm_tok(b, mixed[0:nq, 0:nsb, 512 + h * 64:512 + (h + 1) * 64], ov[:, :, 0:64], nsb, nq,
                        mid_bc(mg[:, 4 + h, :], nsb), W['nwk'], den=ov[:, :, 64])
        if SKIP_D and b.stage >= 5:
            b.memset(mixed[:, :, 768:1024], 0.0)
        for tq in range(ntile if (b.stage >= 4 and not SKIP_D) else 0):
            qk0 = npastD + t0 + tq * 128
            jq = qk0 // 128
            tiles = list(range(max(0, jq - 4), jq + 1))
            first = True
            for j in tiles:
                nk = min(128, npastD + Tq - j * 128)
                kind = 2 if j == jq else (1 if j == jq - 1 else 0)
                bt = W['bt'].next()
                btv = bt[0:nk, 0:512].rearrange("p (h q) -> p h q", q=128)[:, :, 0:nq]
                for h in range(4):
                    hp, po = h // 2, (h % 2) * 64
                    b.mm(P[5 + h % 2][0:nk, hp * 128:hp * 128 + nq], KT[2][po:po + 64, hp, (j % 8) * 128:(j % 8) * 128 + nk],
                         QT[2][po:po + 64, hp, tq * 128:tq * 128 + nq])
                for par in range(2):
                    src = P[5 + par][0:nk, 0:256].rearrange("p (a q) -> p a q", q=128)[:, :, 0:nq]
                    dst = bt[0:nk, 0:512].rearrange("p (a c q) -> p a c q", a=2, c=2, q=128)[:, :, par, 0:nq]
                    b.act(dst, src, AF.Identity, scale=0.125)
                b.tt(btv, btv, W['Bt'][0:nk, kind, :, 0:nq], ALU.add)
                wt = W['Wt'].next()
                wtv = wt[0:nk, 0:512].rearrange("p (h q) -> p h q", q=128)[:, :, 0:nq]
                b.act(wtv, btv, AF.Exp)
                if prompt and j == jq:
                    b.tt(wtv, wtv, C['mask4'][0:nk, :, 0:nq], ALU.mult)
                if prompt and j == jq - 4:
                    b.tt(wtv, wtv, C['mask0'][0:nk, :, 0:nq], ALU.mult)
                for h in range(4):
                    b.mm(Ops[0:nq, h * 65:(h + 1) * 65], wt[0:nk, h * 128:h * 128 + nq], V[2][0:nk, j % 8, h, 0:65],
                         start=first, stop=(j == tiles[-1]), skip_group_check=True)
                    first = False
            ov = Ops[0:nq, 0:260].rearrange("p (s e) -> p s e", e=65)
            rmsnorm_tok(b, mixed[0:nq, tq, 768:1024].rearrange("p (h e) -> p h e", e=64), ov[:, :, 0:64], 4, nq,
                        mg[:, 8:12, :], W['nwk'], den=ov[:, :, 64])
        gdn_group(b, l, si, W, g, G, t0)
        if b.stage < 5:
            continue
        for tt in range(ntile):
            for cc in range(2, 8):
                b.tr(pbv[:, cc, 0:nt], mixed[0:nt, tt, cc * 128:(cc + 1) * 128], identb[0:nt, 0:nt])
            b.cp(mixedT[:, 2:8, tt * 128:tt * 128 + nt], pbv[:, 2:8, 0:nt])
        for oc in range(8):
            pso = P[oc % 2]
            for c in range(8):
                b.mm(pso[:, 0:G], wo[:, c, oc * 128:(oc + 1) * 128], mixedT[:, c, 0:G], start=(c == 0), stop=(c == 7))
            rt = W['htmp'].next()
            b.act(rt[:, 0:G], pso[:, 0:G], AF.Identity, scale=b.mod[:, l, 16 + oc, si:si + 1])
            b.tt(xg[:, oc, 0:G], xg[:, oc, 0:G], rt[:, 0:G], ALU.add)
        b.dma(o['yT'][:, tok0 + t0:tok0 + t0 + G].rearrange("(c p) t -> p c t", p=128), xg[:, :, 0:G])


def phaseC(b, l):
    nc = b.nc
    d, o, C, P = b.d, b.o, b.C, b.P
    if b.stage < 6:
        return
    with contextlib.ExitStack() as s:
        wup = b.sb('wup', [128, 8, 2 * DFF], BF16, s)
        wdn = b.sb('wdn', [128, 22, D], BF16, s)
        for c in range(8):
            for q4 in range(4):
                b.dma(wup[:, c, q4 * 1408:(q4 + 1) * 1408], d['w_up'][l, c * 128:(c + 1) * 128, q4 * 1408:(q4 + 1) * 1408], q='pool')
        for j in range(22):
            b.dma(wdn[:, j, :], d['w_down'][l, j * 128:(j + 1) * 128, :], q='pool')
        GC = 256
        W = {}
        xg = b.sb('cxg', [128, 8, GC], F32, s)
        W['sqr'] = Rot([b.sb('csqr%d' % i, [128, GC], BF16, s) for i in range(2)])
        W['rstd'] = b.sb('crstd', [128, GC], F32, s)
        W['htmp'] = Rot([b.sb('chtmp%d' % i, [128, GC], F32, s) for i in range(2)])
        h2T = b.sb('h2T', [128, 8, GC], BF16, s)
        ur = Rot([b.sb('ur%d' % i, [128, GC + 2], F32, s) for i in range(4)])
        ctr = Rot([b.sb('ctr%d' % i, [128, GC], F32, s) for i in range(4)])
        hist = b.sb('hist', [128, 44, 2], F32, s)
        actT = b.sb('actT', [128, 22, GC], BF16, s)
        fo = b.sb('fo', [2, 2 * DFF], F32, s)
        fcw = b.fcw
        pc_ = 0
        for si in range(3):
            prompt = si < 2
            Tq = T if prompt else TS
            tok0 = si * T if prompt else 2 * T
            G = GC if prompt else TS
            ngroups = Tq // G
            if prompt:
                b.memset(hist[:], 0.0)
            else:
                b.dma(hist[:], d['fconvT'][l])
            for g in range(ngroups):
                t0 = g * G
                ycols = o['yT'][:, tok0 + t0:tok0 + t0 + G].rearrange("(c p) t -> p c t", p=128)
                b.dma(xg[:, :, 0:G], ycols)
                norm_mod(b, l, si, 1, xg, G, W, h2T)
                for j in range(22):
                    cv = []
                    for half in range(2):
                        jj = half * 22 + j
                        colb = half * DFF + j * 128
                        pu = P[1 + pc_ % 4]
                        pc_ += 1
                        for c in range(8):
                            b.mm(pu[:, 0:G], wup[:, c, colb:colb + 128], h2T[:, c, 0:G], start=(c == 0), stop=(c == 7))
                        u = ur.next()
                        b.cp(u[:, 0:2], hist[:, jj, :], e='pool')
                        b.cp(u[:, 2:2 + G], pu[:, 0:G], e='act')
                        b.cp(hist[:, jj, :], u[:, G:G + 2], e='pool')
                        ct = ctr.next()
                        b.ts(ct[:, 0:G], u[:, 0:G], fcw[:, l, jj, 0:1], ALU.mult)
                        b.stt(ct[:, 0:G], u[:, 1:1 + G], fcw[:, l, jj, 1:2], ct[:, 0:G], ALU.mult, ALU.add)
                        b.stt(ct[:, 0:G], u[:, 2:2 + G], fcw[:, l, jj, 2:3], ct[:, 0:G], ALU.mult, ALU.add)
                        cv.append(ct)
                    sil = ctr.next()
                    b.act(sil[:, 0:G], cv[0][:, 0:G], AF.Silu)
                    b.tt(actT[:, j, 0:G], sil[:, 0:G], cv[1][:, 0:G], ALU.mult)
                for oc in range(8):
                    pd = P[5 + oc % 2]
                    for j in range(22):
                        b.mm(pd[:, 0:G], wdn[:, j, oc * 128:(oc + 1) * 128], actT[:, j, 0:G], start=(j == 0), stop=(j == 21))
                    rt = W['htmp'].next()
                    b.act(rt[:, 0:G], pd[:, 0:G], AF.Identity, scale=b.mod[:, l, 40 + oc, si:si + 1])
                    b.tt(xg[:, oc, 0:G], xg[:, oc, 0:G], rt[:, 0:G], ALU.add)
                b.dma(ycols, xg[:, :, 0:G])
                if g == ngroups - 1:
                    for q11 in range(11):
                        pcx = P[0]
                        for c in range(8):
                            b.mm(pcx[0:2, 0:512], h2T[:, c, G - 2:G], wup[:, c, q11 * 512:(q11 + 1) * 512], start=(c == 0), stop=(c == 7))
                        b.cp(fo[0:2, q11 * 512:(q11 + 1) * 512], pcx[0:2, 0:512])
                    if prompt:
                        b.dma(o['p_fconv'][l, si, :, :], fo[:])
                    else:
                        b.dma(o['s_fconv'][l, :, :], fo[:])


def alloc_gdn(b, W, s):
    G = GP
    W['ghist'] = b.sb('ghist', [64, 12, 3], F32, s)
    W['pst'] = Rot([b.sb('pst%d' % i, [64, G + 3], F32, s) for i in range(2)])
    W['cv'] = Rot([b.sb('cv%d' % i, [64, G], F32, s) for i in range(2)])
    W['gsq'] = b.sb('gsq', [64, G], BF16, s)
    W['grn'] = b.sb('grn', [64, G], F32, s)
    W['qkvn'] = b.sb('qkvn', [64, 12, G], BF16, s)
    W['Sf'] = b.sb('Sf', [64, 4, 64], F32, s)
    W['Sb'] = b.sb('Sb', [64, 4, 64], BF16, s)
    W['g3'] = b.sb('g3', [3, 768], F32, s)
    v64 = lambda t, c0: t[0:64, c0:c0 + 256].rearrange("p (h e) -> p h e", e=64)
    hosts32 = [(W['bt'].bufs[0], 0), (W['bt'].bufs[0], 256), (W['zr'].bufs[0], 0), (W['zr'].bufs[1], 0),
               (W['E'], 0), (W['Rr'], 0), (W['htmp'].bufs[0], 0), (W['htmp'].bufs[1], 0)]
    f32n = ['zsil', 'dec', 'decI', 'decS', 'Uf', 'gb', 'vtok', 'of']
    for n, (t, c0) in zip(f32n, hosts32):
        W[n] = v64(t, c0)
    W['on'] = b.sb('on', [64, 4, 64], F32, s)[:]
    hosts16 = [(W['Wt'].bufs[i], c0) for i in range(3) for c0 in (0, 256)] + [(W['SP'].bufs[i], 0) for i in range(2)] + \
              [(W['qkb'].bufs[i], c0) for i in range(2) for c0 in (0, 256)]
    for n, (t, c0) in zip(['ktok', 'kd', 'vn', 'AT', 'om'], hosts16):
        W[n] = v64(t, c0)
    hosts32b = [(W['stg'].bufs[i], c0) for i in range(3) for c0 in (0, 256)]
    for n, (t, c0) in zip(['Pa', 'Qa', 'Qb', 'Ya', 'Yb'], hosts32b):
        W[n] = v64(t, c0)
    W['gsm'] = b.sb('gsm', [64, 10, 4], F32, s)


def gdn_seq_init(b, l, si, W):
    d = b.d
    if si < 2:
        b.memset(W['ghist'][:], 0.0)
        b.memset(W['Sf'][:], 0.0)
        b.memset(W['Sb'][:], 0.0)
    else:
        b.dma(W['ghist'][:], d['gconvT'][l])
        b.dma(W['Sf'][:], d['gstate'][l].rearrange("h k v -> k h v"))
        b.cp(W['Sb'][:], W['Sf'][:])


def gdn_group(b, l, si, W, g, G, t0):
    nc = b.nc
    d, o, C, P, PB = b.d, b.o, b.C, b.P, b.PB
    prompt = si < 2
    Tq = T if prompt else TS
    win, hT = W['win'], W['hT']
    qkvn = W['qkvn']
    gcw = b.gcw
    for hc in range(12):
        pg = P[hc % 2]
        for c in range(8):
            b.mm(pg[0:64, 0:G], win[:, c, hc * 64:(hc + 1) * 64], hT[:, c, 0:G], start=(c == 0), stop=(c == 7))
        pst = W['pst'].next()
        b.cp(pst[:, 0:3], W['ghist'][:, hc, :], e='pool')
        b.cp(pst[:, 3:3 + G], pg[0:64, 0:G], e='act')
        b.cp(W['ghist'][:, hc, :], pst[:, G:G + 3], e='pool')
        cv = W['cv'].next()
        b.ts(cv[:, 0:G], pst[:, 0:G], gcw[:, l, hc, 0:1], ALU.mult)
        for i in range(1, 4):
            b.stt(cv[:, 0:G], pst[:, i:i + G], gcw[:, l, hc, i:i + 1], cv[:, 0:G], ALU.mult, ALU.add)
        b.act(cv[:, 0:G], cv[:, 0:G], AF.Silu)
        if hc < 8:
            b.act(W['gsq'][:, 0:G], cv[:, 0:G], AF.Square)
            pn = P[2]
            b.mm(pn[0:64, 0:G], C['onesb'][0:64, 0:64], W['gsq'][:, 0:G])
            b.rsqrt(W['grn'][:, 0:G], pn[0:64, 0:G], 1.0)
            if hc < 4:
                b.stt(qkvn[:, hc, 0:G], cv[:, 0:G], 0.125, W['grn'][:, 0:G], ALU.mult, ALU.mult)
            else:
                b.tt(qkvn[:, hc, 0:G], cv[:, 0:G], W['grn'][:, 0:G], ALU.mult)
        else:
            b.cp(qkvn[:, hc, 0:G], cv[:, 0:G], e='pool')
    if t0 + G == Tq:
        p3 = P[3]
        for hc in range(8):
            b.tr(p3[0:3, hc * 64:(hc + 1) * 64], W['ghist'][:, hc, :], C['identf'][0:64, 0:64])
        b.cp(W['g3'][:, 0:512], p3[0:3, 0:512])
        p3b = P[2]
        for hc in range(8, 12):
            b.tr(p3b[0:3, (hc - 8) * 64:(hc - 7) * 64], W['ghist'][:, hc, :], C['identf'][0:64, 0:64])
        b.cp(W['g3'][:, 512:768], p3b[0:3, 0:256])
        if prompt:
            b.dma(o['p_gconv'][l, si, :, :], W['g3'][:])
        else:
            b.dma(o['s_gconv'][l, :, :], W['g3'][:])
    Lc = min(64, G)
    nlev = 5 if Lc == 64 else 3
    gsm = W['gsm']
    v3 = lambda t: t[0:Lc, :, 0:Lc]
    pv3 = lambda p, c0: p[0:Lc, c0:c0 + 256].rearrange("p (h e) -> p h e", e=64)
    tri = C['triIncf'][0:Lc, 0:Lc]
    for ch in range(G // Lc):
        cols = slice(ch * Lc, (ch + 1) * Lc)
        pzb = P[2]
        for c in range(8):
            b.mm(pzb[0:Lc, 0:264], hT[:, c, cols], win[:, c, 768:1032], start=(c == 0), stop=(c == 7))
        zsil = W['zsil']
        b.act(zsil[0:Lc, :, :], pzb[0:Lc, 0:256].rearrange("p (h e) -> p h e", e=64), AF.Silu)
        beta, gg, Gc, eG, neG, eGl, ekd, tmp4 = (gsm[:, i, :] for i in range(8))
        b.act(beta[0:Lc, :], pzb[0:Lc, 256:260], AF.Sigmoid)
        b.tt(tmp4[0:Lc, :], pzb[0:Lc, 260:264], W['dtb'][0:Lc, :], ALU.add)
        b.act(tmp4[0:Lc, :], tmp4[0:Lc, :], AF.Exp)
        b.act(tmp4[0:Lc, :], tmp4[0:Lc, :], AF.Ln, bias=1.0)
        b.tt(gg[0:Lc, :], tmp4[0:Lc, :], W['nexpA'][0:Lc, :], ALU.mult)
        pG = P[3]
        b.mm(pG[0:Lc, 0:4], tri, gg[0:Lc, :])
        b.mm(pG[0:64, 4:8], C['onesf'][0:Lc, 0:64], gg[0:Lc, :])
        b.cp(Gc[0:Lc, :], pG[0:Lc, 0:4])
        b.act(eGl[:, :], pG[0:64, 4:8], AF.Exp)
        b.tt(ekd[0:Lc, :], pG[0:Lc, 4:8], Gc[0:Lc, :], ALU.subtract)
        b.act(ekd[0:Lc, :], ekd[0:Lc, :], AF.Exp)
        b.act(eG[0:Lc, :], Gc[0:Lc, :], AF.Exp)
        b.ts(neG[0:Lc, :], eG[0:Lc, :], -1.0, ALU.mult)
        gb = W['gb']
        for h in range(4):
            b.ts(gb[0:Lc, h, 0:Lc], C['onesf'][0:Lc, 0:Lc], gg[0:Lc, h:h + 1], ALU.mult)
        pGr = P[3]
        for h in range(4):
            b.mm(pGr[0:Lc, 256 + h * 64:256 + h * 64 + Lc], gb[0:Lc, h, 0:Lc], tri)
        dec, decI, decS = W['dec'], W['decI'], W['decS']
        for h in range(4):
            b.ts(dec[0:Lc, h, 0:Lc], pGr[0:Lc, 256 + h * 64:256 + h * 64 + Lc], Gc[0:Lc, h:h + 1], ALU.subtract, 0.0, ALU.min)
        b.act(v3(dec), v3(dec), AF.Exp)
        b.tt(v3(decI), v3(dec), mid_bc(tri, 4), ALU.mult)
        b.tt(v3(decS), v3(dec), mid_bc(C['sutf'][0:Lc, 0:Lc], 4), ALU.mult, e='pool')
        pbk = PB[:, 0:512].rearrange("p (a h e) -> p a h e", a=2, e=64)
        for h in range(4):
            b.tr(pbk[0:Lc, 0, h, :], qkvn[:, 4 + h, cols], C['identb'][0:64, 0:64])
            b.tr(pbk[0:Lc, 1, h, :], qkvn[:, 8 + h, cols], C['identb'][0:64, 0:64])
        ktok, vtok, kd = W['ktok'], W['vtok'], W['kd']
        b.cp(ktok[0:Lc, :, :], pbk[0:Lc, 0, :, :])
        b.cp(vtok[0:Lc, :, :], pbk[0:Lc, 1, :, :])
        b.tt(kd[0:Lc, :, :], ktok[0:Lc, :, :], bc(ekd[0:Lc, :], [Lc, 4, 64]), ALU.mult, e='pool')
        pKK = P[4]
        for h in range(4):
            b.mm(pKK[0:Lc, h * 64:h * 64 + Lc], qkvn[:, 4 + h, cols], qkvn[:, 4 + h, cols])
        for h in range(4):
            b.mm(pKK[0:Lc, 256 + h * 64:256 + h * 64 + Lc], qkvn[:, 4 + h, cols], qkvn[:, h, cols])
        Uf = W['Uf']
        b.tt(v3(Uf), pv3(pKK, 0)[:, :, 0:Lc], v3(decS), ALU.mult)
        b.tt(v3(Uf), v3(Uf), bc(beta[0:Lc, :], [Lc, 4, Lc]), ALU.mult)
        AT = W['AT']
        b.tt(v3(AT), pv3(pKK, 256)[:, :, 0:Lc], v3(decI), ALU.mult)
        Pc, Pn_, Qc, Qn_, Yc, Yn_ = Uf, W['Pa'], W['Qa'], W['Qb'], W['Ya'], W['Yb']
        pq = P[5]
        for h in range(4):
            b.tr(pq[0:Lc, h * 64:h * 64 + Lc], Uf[0:Lc, h, 0:Lc], C['identf'][0:Lc, 0:Lc])
        b.cp(v3(Qc), pv3(pq, 0)[:, :, 0:Lc])
        b.tt(v3(Yc), mid_bc(C['identf'][0:Lc, 0:Lc], 4), v3(Uf), ALU.subtract)
        for k in range(1, nlev + 1):
            pI = P[5]
            for h in range(4):
                b.mm(pI[0:Lc, h * 64:h * 64 + Lc], Pc[0:Lc, h, 0:Lc], Qc[0:Lc, h, 0:Lc])
            if k < nlev:
                for h in range(4):
                    b.mm(pI[0:Lc, 256 + h * 64:256 + h * 64 + Lc], Qc[0:Lc, h, 0:Lc], Pc[0:Lc, h, 0:Lc])
            b.cp(v3(Qn_), pv3(pI, 0)[:, :, 0:Lc], e='act')
            if k < nlev:
                b.cp(v3(Pn_), pv3(pI, 256)[:, :, 0:Lc])
            pY = P[6]
            for h in range(4):
                b.mm(pY[0:Lc, h * 64:h * 64 + Lc], Qn_[0:Lc, h, 0:Lc], Yc[0:Lc, h, 0:Lc])
            b.tt(v3(Yn_), v3(Yc), pv3(pY, 0)[:, :, 0:Lc], ALU.add)
            Pc, Pn_ = Pn_, Pc
            Qc, Qn_ = Qn_, Qc
            Yc, Yn_ = Yn_, Yc
        Sf, Sb = W['Sf'], W['Sb']
        pS = P[0]
        for h in range(4):
            b.mm(pS[0:Lc, h * 64:(h + 1) * 64], qkvn[:, 4 + h, cols], Sb[:, h, :])
        for h in range(4):
            b.mm(pS[0:Lc, 256 + h * 64:256 + (h + 1) * 64], qkvn[:, h, cols], Sb[:, h, :])
        Rm, vn, of = W['gb'], W['vn'], W['of']
        b.tt(of[0:Lc, :, :], pv3(pS, 0), bc(neG[0:Lc, :], [Lc, 4, 64]), ALU.mult)
        b.tt(Rm[0:Lc, :, :], of[0:Lc, :, :], vtok[0:Lc, :, :], ALU.add)
        b.tt(of[0:Lc, :, :], pv3(pS, 256), bc(eG[0:Lc, :], [Lc, 4, 64]), ALU.mult)
        pX = P[1]
        for h in range(4):
            b.mm(pX[0:Lc, h * 64:(h + 1) * 64], Yc[0:Lc, h, 0:Lc], Rm[0:Lc, h, :])
        b.tt(vn[0:Lc, :, :], pv3(pX, 0), bc(beta[0:Lc, :], [Lc, 4, 64]), ALU.mult)
        for h in range(4):
            b.mm(pX[0:Lc, 256 + h * 64:256 + (h + 1) * 64], AT[0:Lc, h, 0:Lc], vn[0:Lc, h, :])
        b.tt(of[0:Lc, :, :], of[0:Lc, :, :], pv3(pX, 256), ALU.add)
        pSn = P[4]
        for h in range(4):
            b.mm(pSn[0:64, h * 64:(h + 1) * 64], kd[0:Lc, h, :], vn[0:Lc, h, :])
        b.tt(Sf[:, :, :], Sf[:, :, :], bc(eGl[:, :], [64, 4, 64]), ALU.mult)
        b.tt(Sf[:, :, :], Sf[:, :, :], pSn[0:64, 0:256].rearrange("p (h e) -> p h e", e=64), ALU.add)
        b.cp(Sb[:, :, :], Sf[:, :, :], e='act')
        on, om = W['on'], W['om']
        rmsnorm_tok(b, on[0:Lc, :, :], of[0:Lc, :, :], 4, Lc, W['gG'], W['nwk'])
        b.tt(om[0:Lc, :, :], on[0:Lc, :, :], zsil[0:Lc, :, :], ALU.mult, e='pool')
        pm = PB[:, 768:1024].rearrange("p (a t) -> p a t", t=128)
        omf = om[0:Lc, :, :].rearrange("p h e -> p (h e)")
        for cc in range(2):
            b.tr(pm[:, cc, 0:Lc], omf[:, cc * 128:(cc + 1) * 128], C['identb'][0:Lc, 0:Lc])
        b.cp(W['mixedT'][:, 0:2, ch * Lc:(ch + 1) * Lc], pm[:, 0:2, 0:Lc])
    if t0 + G == Tq:
        if prompt:
            b.dma(o['p_gstate'][l, si].rearrange("h k v -> k h v"), W['Sf'][:])
        else:
            b.dma(o['s_gstate'][l].rearrange("h k v -> k h v"), W['Sf'][:])


_NC_CACHE = {}


def _prep_inputs(inp):
    f = lambda a: np.ascontiguousarray(np.asarray(a, dtype=np.float32))
    L = DEPTH
    shared = {
        'ada_w': f(inp['ada_w']),
        'ada_bT': f(np.asarray(inp['ada_b']).reshape(L, 48, 128).transpose(2, 0, 1)),
        'nmgT': f(np.asarray(inp['norm_mix_g']).reshape(L, 8, 128).transpose(2, 0, 1)),
        'nfgT': f(np.asarray(inp['norm_ffn_g']).reshape(L, 8, 128).transpose(2, 0, 1)),
        'w_in': f(inp['w_in']),
        'gcwT': f(np.asarray(inp['gdn_conv_w']).reshape(L, 4, 12, 64).transpose(3, 0, 2, 1)),
        'a_log': f(inp['gdn_a_log']), 'dt_bias': f(inp['gdn_dt_bias']), 'gn_g': f(inp['gdn_norm_g']),
        'fq_g': f(inp['fox_q_g']), 'fk_g': f(inp['fox_k_g']), 'fb_f': f(inp['fox_b_f']),
        'bq_g': f(inp['band_q_g']), 'bk_g': f(inp['band_k_g']), 'rel': f(inp['band_rel_bias']),
        'mg': f(np.asarray(inp['merge_g']).reshape(L, 768)),
        'w_o': f(inp['w_o']), 'w_up': f(inp['w_up']),
        'fcwT': f(np.asarray(inp['ffn_conv_w']).reshape(L, 3, 44, 128).transpose(3, 0, 2, 1)),
        'w_down': f(inp['w_down']),
    }
    xp = np.asarray(inp['x_prompt'], dtype=np.float32)
    xs = np.asarray(inp['x_sample'], dtype=np.float32)
    cp_ = np.asarray(inp['c_prompt'], dtype=np.float32)
    cs = np.asarray(inp['c_sample'], dtype=np.float32)
    maps = []
    for k in range(8):
        m = dict(shared)
        xT = np.empty((D, NTOK), np.float32)
        xT[:, 0:T] = xp[2 * k].T
        xT[:, T:2 * T] = xp[2 * k + 1].T
        xT[:, 2 * T:] = xs[k].T
        m['xT'] = xT
        cc = np.zeros((4, D), np.float32)
        cc[0], cc[1], cc[2] = cp_[2 * k], cp_[2 * k + 1], cs[k]
        m['cT'] = f(cc.reshape(4, 8, 128).transpose(2, 1, 0))
        m['gconvT'] = f(np.asarray(inp['state_gdn_conv'])[:, k].reshape(L, 3, 12, 64).transpose(0, 3, 2, 1))
        m['gstate'] = f(np.asarray(inp['state_gdn'])[:, k])
        m['sbk'] = f(np.asarray(inp['cache_sb_k'])[:, k].reshape(L, PAST, 256))
        m['sbv'] = f(np.asarray(inp['cache_sb_v'])[:, k].reshape(L, PAST, 256))
        m['fxk'] = f(np.asarray(inp['cache_fox_k'])[:, k].reshape(L, PAST, 256))
        m['fxv'] = f(np.asarray(inp['cache_fox_v'])[:, k].reshape(L, PAST, 256))
        m['fxlf'] = f(np.asarray(inp['cache_fox_logf'])[:, k])
        m['bdk'] = f(np.asarray(inp['cache_band_k'])[:, k].reshape(L, 512, 256))
        m['bdv'] = f(np.asarray(inp['cache_band_v'])[:, k].reshape(L, 512, 256))
        m['fconvT'] = f(np.asarray(inp['state_ffn_conv'])[:, k].reshape(L, 2, 44, 128).transpose(0, 3, 2, 1))
        maps.append(m)
    return maps


def _assemble(res):
    L = DEPTH
    r = res
    yp = np.empty((16, T, D), np.float32)
    ys = np.empty((8, TS, D), np.float32)
    for k in range(8):
        yT = r[k]['yT']
        yp[2 * k] = yT[:, 0:T].T
        yp[2 * k + 1] = yT[:, T:2 * T].T
        ys[k] = yT[:, 2 * T:].T
    cat_p = lambda n, shp: np.concatenate([r[k][n] for k in range(8)], axis=1).reshape(shp)
    stk_s = lambda n, shp: np.stack([r[k][n] for k in range(8)], axis=1).reshape(shp)
    outs = [yp, ys,
            cat_p('p_gconv', (L, 16, 3, 768)), cat_p('p_gstate', (L, 16, 4, 64, 64)),
            cat_p('p_sbk', (L, 16, T, 4, 64)), cat_p('p_sbv', (L, 16, T, 4, 64)),
            cat_p('p_fxk', (L, 16, T, 4, 64)), cat_p('p_fxv', (L, 16, T, 4, 64)),
            cat_p('p_fxlf', (L, 16, T, 4)),
            cat_p('p_bdk', (L, 16, 512, 4, 64)), cat_p('p_bdv', (L, 16, 512, 4, 64)),
            cat_p('p_fconv', (L, 16, 2, 2 * DFF)),
            stk_s('s_gconv', (L, 8, 3, 768)), stk_s('s_gstate', (L, 8, 4, 64, 64)),
            stk_s('s_sbk', (L, 8, TS, 4, 64)), stk_s('s_sbv', (L, 8, TS, 4, 64)),
            stk_s('s_fxk', (L, 8, TS, 4, 64)), stk_s('s_fxv', (L, 8, TS, 4, 64)),
            stk_s('s_fxlf', (L, 8, TS, 4)),
            stk_s('s_bdk', (L, 8, 512, 4, 64)), stk_s('s_bdv', (L, 8, 512, 4, 64)),
            stk_s('s_fconv', (L, 8, 2, 2 * DFF))]
    return tuple(np.ascontiguousarray(a, dtype=np.float32) for a in outs)


def kernel(**inputs):
    maps = _prep_inputs(inputs)
    if 'nc' not in _NC_CACHE:
        _NC_CACHE['nc'] = build()
    res = run_bass_kernel_spmd(_NC_CACHE['nc'], maps, core_ids=list(range(8)))
    return _assemble(res.results)
```

```python
import contextlib
import numpy as np
import concourse.bass as bass
import concourse.mybir as mybir
from concourse.bass_utils import run_bass_kernel_spmd

F32 = mybir.dt.float32
BF16 = mybir.dt.bfloat16
AF = mybir.ActivationFunctionType
ALU = mybir.AluOpType
AX = mybir.AxisListType

D = 1024
T = 2048
DEPTH = 4
TS = 16
PAST = 1024
NTOK = 2 * T + TS
INC = 3340
DFF = 2816
EPS = 1e-6


def _box(ap):
    t = ap.tensor
    name = t.name
    dims = [(int(s), int(c)) for s, c in ap.ap]
    off = int(ap.offset)
    if 'DRAM' in str(ap.space).upper():
        lo = hi = off
        for s, c in dims:
            if s >= 0:
                hi += s * (c - 1)
            else:
                lo += s * (c - 1)
        return (name, 0, 1, lo, hi + 1)
    import os
    if 'PSUM' in str(ap.space).upper() and (os.environ.get('KDBG_PSUMBOX', '1') == '1' or (os.environ.get('KDBG_PSUMBOX') == '2' and name.startswith('pbb'))):
        return (name, 0, 128, 0, 1 << 30)
    psize = 1
    for d in list(t.shape)[1:]:
        psize *= int(d)
    p0 = off // psize
    f0 = off % psize
    pstep, pcnt = dims[0]
    if pstep == psize or pcnt == 1:
        np_ = pcnt
    elif pstep == 0:
        np_ = 1
    else:
        np_ = 128 - p0
    lo = hi = f0
    for s, c in dims[1:]:
        if s >= 0:
            hi += s * (c - 1)
        else:
            lo += s * (c - 1)
    return (name, p0, p0 + np_, lo, hi + 1)


class Sync:
    def __init__(self, nc, stack, n_dma_sems=8):
        self.nc = nc
        self.eng = {'pe': nc.tensor, 'act': nc.scalar, 'dve': nc.vector, 'pool': nc.gpsimd, 'sp': nc.sync}
        self.sem = {}
        self.cnt = {}
        for e in ('pe', 'act', 'dve', 'pool'):
            self.sem[e] = stack.enter_context(nc.semaphore('s_' + e))
            self.cnt[e] = 0
        self.dsem = {}
        self.dcnt = {}
        for q in ('sp', 'pool'):
            self.dsem[q] = [stack.enter_context(nc.semaphore('d_%s%d' % (q, i))) for i in range(n_dma_sems)]
            self.dcnt[q] = 0
        self.K = n_dma_sems
        self.semobj = {}
        for e, s in self.sem.items():
            self.semobj[('c', e)] = s
        for q, l in self.dsem.items():
            for i, s in enumerate(l):
                self.semobj[('d', q, i)] = s
        self.waited = {e: {} for e in self.eng}
        self.W = {}
        self.R = {}
        self.n_wait = 0
        self.n_inst = 0

    def _collect(self, reads, writes):
        deps = {}
        for ap in reads:
            b = _box(ap)
            for r in self.W.get(b[0], ()):
                if r[0] < b[2] and b[1] < r[1] and r[2] < b[4] and b[3] < r[3]:
                    if deps.get(r[4], 0) < r[5]:
                        deps[r[4]] = r[5]
        for ap in writes:
            b = _box(ap)
            for r in self.W.get(b[0], ()):
                if r[0] < b[2] and b[1] < r[1] and r[2] < b[4] and b[3] < r[3]:
                    if deps.get(r[4], 0) < r[5]:
                        deps[r[4]] = r[5]
            rd = self.R.get(b[0])
            if rd:
                for k, v in rd.items():
                    if k[0] < b[2] and b[1] < k[1] and k[2] < b[4] and b[3] < k[3]:
                        if deps.get(k[4], 0) < v:
                            deps[k[4]] = v
        return deps

    def _record(self, reads, writes, semkey, val):
        for ap in writes:
            b = _box(ap)
            lst = self.W.setdefault(b[0], [])
            lst[:] = [r for r in lst if not (b[1] <= r[0] and r[1] <= b[2] and b[3] <= r[2] and r[3] <= b[4])]
            lst.append([b[1], b[2], b[3], b[4], semkey, val])
            rd = self.R.get(b[0])
            if rd:
                for k in [k for k in rd if b[1] <= k[0] and k[1] <= b[2] and b[3] <= k[2] and k[3] <= b[4]]:
                    del rd[k]
        for ap in reads:
            b = _box(ap)
            self.R.setdefault(b[0], {})[(b[1], b[2], b[3], b[4], semkey)] = val

    def _emit_waits(self, e, deps):
        w = self.waited[e]
        for k, v in deps.items():
            if k == ('c', 'pe') and e == 'pe':
                continue
            if w.get(k, 0) >= v:
                continue
            self.eng[e].wait_ge(self.semobj[k], v)
            w[k] = v
            self.n_wait += 1

    def op(self, e, fn, reads=(), writes=()):
        px = [a for a in reads if 'PSUM' in str(a.space).upper()]
        if px:
            writes = list(writes) + px
        deps = self._collect(reads, writes)
        self._emit_waits(e, deps)
        inst = fn()
        self.cnt[e] += 1
        inst.then_inc(self.sem[e], 1)
        self._record(reads, writes, ('c', e), self.cnt[e])
        self.n_inst += 1
        return inst

    def dma(self, q, out, in_, **kw):
        deps = self._collect([in_], [out])
        n = self.dcnt[q]
        k = n % self.K
        semkey = ('d', q, k)
        prev = (n // self.K) * 16
        if prev > 0:
            deps[semkey] = max(deps.get(semkey, 0), prev)
        self._emit_waits(q, deps)
        inst = self.eng[q].dma_start(out=out, in_=in_, **kw)
        inst.then_inc(self.semobj[semkey], 16)
        self.dcnt[q] = n + 1
        self._record([in_], [out], semkey, prev + 16)
        self.n_inst += 1
        return inst

    def barrier(self):
        toks = {}
        for e in ('pe', 'act', 'dve', 'pool'):
            if self.cnt[e] > 0:
                toks[('c', e)] = self.cnt[e]
        for q in self.dsem:
            n = self.dcnt[q]
            for k in range(self.K):
                cntk = (n - k + self.K - 1) // self.K if n > k else 0
                if cntk > 0:
                    toks[('d', q, k)] = cntk * 16
        for e in self.eng:
            w = self.waited[e]
            for k, v in toks.items():
                if k == ('c', 'pe') and e == 'pe':
                    continue
                if w.get(k, 0) >= v:
                    continue
                self.eng[e].wait_ge(self.semobj[k], v)
                w[k] = v
                self.n_wait += 1
        self.W = {}
        self.R = {}


class B:
    def __init__(self, nc, st, n_layers=DEPTH, stage=99):
        self.nc = nc
        self.st = st
        self.S = Sync(nc, st)
        self.n_layers = n_layers
        self.stage = stage
        self.uid = 0

    def sb(self, name, shape, dt, st=None):
        self.uid += 1
        return (st or self.st).enter_context(self.nc.sbuf_tensor('%s_%d' % (name, self.uid), shape, dt))

    def ps(self, name, shape, dt, st=None):
        self.uid += 1
        return (st or self.st).enter_context(self.nc.psum_tensor('%s_%d' % (name, self.uid), shape, dt))

    def din(self, name, shape):
        return self.nc.dram_tensor(name, list(shape), F32, kind="ExternalInput").ap()

    def dout(self, name, shape):
        return self.nc.dram_tensor(name, list(shape), F32, kind="ExternalOutput").ap()

    def mm(self, out, lhsT, rhs, start=True, stop=True, **kw):
        nc = self.nc
        return self.S.op('pe', lambda: nc.tensor.matmul(out, lhsT=lhsT, rhs=rhs, start=start, stop=stop, **kw),
                         reads=[lhsT, rhs], writes=[out])

    def tr(self, out, in_, ident):
        nc = self.nc
        return self.S.op('pe', lambda: nc.tensor.transpose(out=out, in_=in_, identity=ident),
                         reads=[in_, ident], writes=[out])

    def act(self, out, in_, func, bias=None, scale=1.0):
        nc = self.nc
        reads = [in_]
        kw = {}
        if bias is not None:
            kw['bias'] = bias
            if not isinstance(bias, (int, float)):
                reads.append(bias)
        if not isinstance(scale, (int, float)):
            reads.append(scale)
        return self.S.op('act', lambda: nc.scalar.activation(out=out, in_=in_, func=func, scale=scale, **kw),
                         reads=reads, writes=[out])

    def _ve(self, e):
        return self.nc.vector if e == 'dve' else self.nc.gpsimd

    def tt(self, out, in0, in1, op, e='dve'):
        eng = self._ve(e)
        return self.S.op(e, lambda: eng.tensor_tensor(out=out, in0=in0, in1=in1, op=op),
                         reads=[in0, in1], writes=[out])

    def ts(self, out, in0, s1, op0, s2=None, op1=None, e='dve'):
        eng = self._ve(e)
        reads = [in0]
        if not isinstance(s1, (int, float)):
            reads.append(s1)
        if s2 is not None and not isinstance(s2, (int, float)):
            reads.append(s2)
        if op1 is None:
            f = lambda: eng.tensor_scalar(out=out, in0=in0, scalar1=s1, scalar2=None, op0=op0)
        else:
            f = lambda: eng.tensor_scalar(out=out, in0=in0, scalar1=s1, scalar2=s2, op0=op0, op1=op1)
        return self.S.op(e, f, reads=reads, writes=[out])

    def stt(self, out, in0, scalar, in1, op0, op1, e='dve'):
        eng = self._ve(e)
        reads = [in0, in1]
        if not isinstance(scalar, (int, float)):
            reads.append(scalar)
        return self.S.op(e, lambda: eng.scalar_tensor_tensor(out=out, in0=in0, scalar=scalar, in1=in1, op0=op0, op1=op1),
                         reads=reads, writes=[out])

    def cp(self, out, in_, e='dve'):
        if e == 'act':
            return self.act(out, in_, AF.Copy)
        eng = self._ve(e)
        return self.S.op(e, lambda: eng.tensor_copy(out=out, in_=in_), reads=[in_], writes=[out])

    def red(self, out, in_, op=ALU.add, e='dve'):
        eng = self._ve(e)
        return self.S.op(e, lambda: eng.tensor_reduce(out=out, in_=in_, axis=AX.X, op=op), reads=[in_], writes=[out])

    def recip(self, out, in_):
        nc = self.nc
        return self.S.op('dve', lambda: nc.vector.reciprocal(out=out, in_=in_), reads=[in_], writes=[out])

    def memset(self, ap, v, e='pool'):
        eng = self._ve(e)
        return self.S.op(e, lambda: eng.memset(ap, v), writes=[ap])

    def asel(self, out, in_, pattern, cmp, fill, base, cm):
        nc = self.nc
        return self.S.op('pool', lambda: nc.gpsimd.affine_select(out=out, in_=in_, pattern=pattern, compare_op=cmp,
                                                               fill=fill, base=base, channel_multiplier=cm),
                         reads=[in_], writes=[out])

    def dma(self, out, in_, q='sp', **kw):
        return self.S.dma(q, out, in_, **kw)

    def rsqrt(self, out, in_, scale, tmp=None):
        t = tmp if tmp is not None else out
        self.act(t, in_, AF.Ln, bias=self.eps_col[0:int(in_.shape[0]), :], scale=scale)
        self.act(out, t, AF.Exp, scale=-0.5)


def bc(ap, shape):
    return ap.unsqueeze(len(ap.shape)).broadcast_to(list(shape))


def pbc(dram_ap_1d, n, parts=128):
    return bass.AP(tensor=dram_ap_1d.tensor, offset=int(dram_ap_1d.offset), ap=[[0, parts], [1, n]])


class Rot:
    def __init__(self, bufs):
        self.bufs = bufs
        self.i = 0

    def next(self):
        t = self.bufs[self.i % len(self.bufs)]
        self.i += 1
        return t


def build(n_layers=DEPTH, stage=99):
    nc = bass.Bass("TRN2", target_bir_lowering=False)
    with contextlib.ExitStack() as st:
        b = B(nc, st, n_layers, stage)
        emit(b)
        b.S.barrier()
        print("built: inst", b.S.n_inst, "waits", b.S.n_wait)
    return nc


def emit(b):
    nc = b.nc
    L = DEPTH
    NL = b.n_layers
    d = {}
    d['xT'] = b.din('xT', [D, NTOK])
    d['cT'] = b.din('cT', [128, 8, 4])
    d['gconvT'] = b.din('gconvT', [L, 64, 12, 3])
    d['gstate'] = b.din('gstate', [L, 4, 64, 64])
    for n in ('sbk', 'sbv', 'fxk', 'fxv'):
        d[n] = b.din(n, [L, PAST, 256])
    d['fxlf'] = b.din('fxlf', [L, PAST, 4])
    d['bdk'] = b.din('bdk', [L, 512, 256])
    d['bdv'] = b.din('bdv', [L, 512, 256])
    d['fconvT'] = b.din('fconvT', [L, 128, 44, 2])
    d['ada_w'] = b.din('ada_w', [L, D, 6 * D])
    d['ada_bT'] = b.din('ada_bT', [128, L, 48])
    d['nmgT'] = b.din('nmgT', [128, L, 8])
    d['nfgT'] = b.din('nfgT', [128, L, 8])
    d['w_in'] = b.din('w_in', [L, D, INC])
    d['gcwT'] = b.din('gcwT', [64, L, 12, 4])
    d['a_log'] = b.din('a_log', [L, 4])
    d['dt_bias'] = b.din('dt_bias', [L, 4])
    d['gn_g'] = b.din('gn_g', [L, 64])
    d['fq_g'] = b.din('fq_g', [L, 64])
    d['fk_g'] = b.din('fk_g', [L, 64])
    d['fb_f'] = b.din('fb_f', [L, 4])
    d['bq_g'] = b.din('bq_g', [L, 64])
    d['bk_g'] = b.din('bk_g', [L, 64])
    d['rel'] = b.din('rel', [L, 4, 257])
    d['mg'] = b.din('mg', [L, 768])
    d['w_o'] = b.din('w_o', [L, D, D])
    d['w_up'] = b.din('w_up', [L, D, 2 * DFF])
    d['fcwT'] = b.din('fcwT', [128, L, 44, 3])
    d['w_down'] = b.din('w_down', [L, DFF, D])
    o = {}
    o['yT'] = b.dout('yT', [D, NTOK])
    o['p_gconv'] = b.dout('p_gconv', [L, 2, 3, 768])
    o['p_gstate'] = b.dout('p_gstate', [L, 2, 4, 64, 64])
    for n in ('p_sbk', 'p_sbv', 'p_fxk', 'p_fxv'):
        o[n] = b.dout(n, [L, 2, T, 256])
    o['p_fxlf'] = b.dout('p_fxlf', [L, 2, T, 4])
    o['p_bdk'] = b.dout('p_bdk', [L, 2, 512, 256])
    o['p_bdv'] = b.dout('p_bdv', [L, 2, 512, 256])
    o['p_fconv'] = b.dout('p_fconv', [L, 2, 2, 2 * DFF])
    o['s_gconv'] = b.dout('s_gconv', [L, 3, 768])
    o['s_gstate'] = b.dout('s_gstate', [L, 4, 64, 64])
    for n in ('s_sbk', 's_sbv', 's_fxk', 's_fxv'):
        o[n] = b.dout(n, [L, TS, 256])
    o['s_fxlf'] = b.dout('s_fxlf', [L, TS, 4])
    o['s_bdk'] = b.dout('s_bdk', [L, 512, 256])
    o['s_bdv'] = b.dout('s_bdv', [L, 512, 256])
    o['s_fconv'] = b.dout('s_fconv', [L, 2, 2 * DFF])
    b.d, b.o = d, o

    identb = b.sb('identb', [128, 128], BF16)
    identf = b.sb('identf', [128, 128], F32)
    onesb = b.sb('onesb', [128, 128], BF16)
    onesf = b.sb('onesf', [128, 128], F32)
    triUb = b.sb('triUb', [128, 128], BF16)
    triIncf = b.sb('triIncf', [128, 128], F32)
    sutf = b.sb('sutf', [64, 64], F32)
    antiJ = b.sb('antiJ', [128, 128], F32)
    blkb = b.sb('blkb', [128, 128], BF16)
    mask0 = b.sb('mask0', [128, 4, 128], BF16)
    mask4 = b.sb('mask4', [128, 4, 128], BF16)
    b.eps_col = b.sb('eps_col', [128, 1], F32)
    b.memset(b.eps_col[:], EPS)
    b.memset(identf[:], 0.0)
    b.asel(identf[:], identf[:], [[-1, 128]], ALU.not_equal, 1.0, 0, 1)
    b.cp(identb[:], identf[:])
    b.memset(onesb[:], 1.0)
    b.memset(onesf[:], 1.0)
    b.asel(triUb[:], onesb[:], [[-1, 128]], ALU.is_ge, 0.0, 0, 1)
    b.asel(triIncf[:], onesf[:], [[1, 128]], ALU.is_ge, 0.0, 0, -1)
    b.asel(sutf[:], onesf[0:64, 0:64], [[1, 64]], ALU.is_gt, 0.0, 0, -1)
    b.memset(antiJ[:], 0.0)
    b.asel(antiJ[:], antiJ[:], [[1, 128]], ALU.not_equal, 1.0, -127, 1)
    b.memset(blkb[:], 0.0)
    b.memset(blkb[0:64, 0:64], 1.0)
    b.memset(blkb[64:128, 64:128], 1.0)
    b.memset(mask0[:], 1.0)
    b.memset(mask0[0:64, :, 64:128], 0.0)
    b.memset(mask4[:], 1.0)
    b.memset(mask4[64:128, :, 0:64], 0.0)
    C = dict(identb=identb, identf=identf, onesb=onesb, onesf=onesf, triUb=triUb, triIncf=triIncf, sutf=sutf,
             antiJ=antiJ, blkb=blkb, mask0=mask0, mask4=mask4)
    b.C = C

    b.P = [b.ps('pb%d' % i, [128, 512], F32) for i in range(7)]
    b.PB = b.ps('pbb', [128, 1024], BF16)

    nmg = b.sb('nmg', [128, L, 8], F32)
    nfg = b.sb('nfg', [128, L, 8], F32)
    adab = b.sb('adab', [128, L, 48], F32)
    gcw = b.sb('gcw', [64, L, 12, 4], F32)
    fcw = b.sb('fcw', [128, L, 44, 3], F32)
    b.dma(nmg[:], d['nmgT'])
    b.dma(nfg[:], d['nfgT'])
    b.dma(adab[:], d['ada_bT'])
    b.dma(gcw[:], d['gcwT'])
    b.dma(fcw[:], d['fcwT'])
    b.gcw, b.fcw = gcw, fcw

    mod = b.sb('mod', [128, L, 48, 4], F32)
    modA = b.sb('modA', [128, L, 2, 8, 4], F32)
    b.mod, b.modA = mod, modA
    with contextlib.ExitStack() as s1:
        cT = b.sb('cT', [128, 8, 4], F32, s1)
        sc = b.sb('sc', [128, 8, 4], F32, s1)
        tmp = b.sb('sctmp', [128, 8, 4], F32, s1)
        b.dma(cT[:], d['cT'])
        b.act(tmp[:], cT[:], AF.Exp, scale=-1.0)
        b.ts(tmp[:], tmp[:], 1.0, ALU.add)
        b.recip(tmp[:], tmp[:])
        b.tt(sc[:], cT[:], tmp[:], ALU.mult)
        awr = Rot([b.sb('aw%d' % i, [128, 8, 768], F32, s1) for i in range(2)])
        for l in range(NL):
            pm = b.P[l % 2]
            for pc in range(8):
                aw = awr.next()
                b.dma(aw[:], d['ada_w'][l, :, pc * 768:(pc + 1) * 768].rearrange("(c p) n -> p c n", p=128))
                for j in range(6):
                    oc = pc * 6 + j
                    for c in range(8):
                        b.mm(pm[:, oc * 4:oc * 4 + 4], aw[:, c, j * 128:(j + 1) * 128], sc[:, c, :],
                             start=(c == 0), stop=(c == 7))
            pmv = pm[:, 0:192].rearrange("p (a s) -> p a s", s=4)
            b.tt(mod[:, l, :, :], pmv, bc(adab[:, l, :], [128, 48, 4]), ALU.add)
            for w, g in ((0, nmg), (1, nfg)):
                scv = mod[:, l, (1 + 3 * w) * 8:(2 + 3 * w) * 8, :]
                b.ts(modA[:, l, w, :, :], scv, 1.0, ALU.add)
                b.tt(modA[:, l, w, :, :], modA[:, l, w, :, :], bc(g[:, l, :], [128, 8, 4]), ALU.mult)
    b.S.barrier()

    for l in range(NL):
        layer(b, l)


def rmsnorm_tok(b, out, src, n, np_, gains, wk, den=None):
    sq, ss, y = wk
    x = src
    if den is not None:
        b.recip(ss[0:np_, 0:n], den)
        b.tt(y[0:np_, 0:n, :], src, bc(ss[0:np_, 0:n], [np_, n, 64]), ALU.mult)
        x = y[0:np_, 0:n, :]
    b.act(sq[0:np_, 0:n, :], x, AF.Square)
    b.red(ss[0:np_, 0:n], sq[0:np_, 0:n, :])
    b.rsqrt(ss[0:np_, 0:n], ss[0:np_, 0:n], 1.0 / 64.0)
    if gains is None:
        b.tt(out, x, bc(ss[0:np_, 0:n], [np_, n, 64]), ALU.mult)
    else:
        b.tt(y[0:np_, 0:n, :], x, bc(ss[0:np_, 0:n], [np_, n, 64]), ALU.mult)
        b.tt(out, y[0:np_, 0:n, :], gains[0:np_, 0:n, :], ALU.mult, e='pool')


def layer(b, l):
    nc = b.nc
    d, o, C, P = b.d, b.o, b.C, b.P
    src_x = d['xT'] if l == 0 else o['yT']
    with contextlib.ExitStack() as s:
        win = b.sb('win', [128, 8, INC], BF16, s)
        wo = b.sb('wo', [128, 8, D], BF16, s)
        for c in range(8):
            for hh in range(2):
                b.dma(win[:, c, hh * 1670:(hh + 1) * 1670], d['w_in'][l, c * 128:(c + 1) * 128, hh * 1670:(hh + 1) * 1670], q='pool')
        for c in range(8):
            b.dma(wo[:, c, :], d['w_o'][l, c * 128:(c + 1) * 128, :], q='pool')
        gC = b.sb('gC', [128, 8, 64], F32, s)
        gD = b.sb('gD', [128, 8, 64], F32, s)
        gG = b.sb('gG', [128, 4, 64], F32, s)
        mg = b.sb('mg', [128, 12, 64], F32, s)
        dtb = b.sb('dtb', [128, 4], F32, s)
        nexpA = b.sb('nexpA', [128, 4], F32, s)
        fbf = b.sb('fbf', [128, 4], F32, s)

        def bcl(t, n, rep):
            return bass.AP(tensor=t.tensor, offset=int(t.offset), ap=[[0, 128], [0, rep], [1, n]])
        b.dma(gC[:, 0:4, :], bcl(d['fq_g'][l], 64, 4))
        b.dma(gC[:, 4:8, :], bcl(d['fk_g'][l], 64, 4))
        b.dma(gD[:, 0:4, :], bcl(d['bq_g'][l], 64, 4))
        b.dma(gD[:, 4:8, :], bcl(d['bk_g'][l], 64, 4))
        b.dma(gG[:], bcl(d['gn_g'][l], 64, 4))
        b.dma(mg[:].rearrange("p a b -> p (a b)"), pbc(d['mg'][l], 768))
        b.dma(dtb[:], pbc(d['dt_bias'][l], 4))
        b.dma(nexpA[:], pbc(d['a_log'][l], 4))
        b.dma(fbf[:], pbc(d['fb_f'][l], 4))
        b.act(nexpA[:], nexpA[:], AF.Exp)
        b.ts(nexpA[:], nexpA[:], -1.0, ALU.mult)
        relx = nc.dram_tensor('relx%d' % l, [4, 520], F32, kind="Internal").ap()
        with contextlib.ExitStack() as s2:
            rl = b.sb('rl', [4, 257], F32, s2)
            rx = b.sb('rx', [4, 520], F32, s2)
            b.dma(rl[:], d['rel'][l])
            b.memset(rx[:], 0.0)
            b.cp(rx[:, 128:385], rl[:])
            b.cp(rx[:, 0:128], rl[:, 0:1].broadcast_to([4, 128]))
            b.cp(rx[:, 385:513], rl[:, 256:257].broadcast_to([4, 128]))
            b.dma(relx, rx[:])
        Bt = b.sb('Bt', [128, 3, 4, 128], F32, s)
        with contextlib.ExitStack() as s2:
            tz = b.sb('tz', [128, 4, 128], F32, s2)
            for kind, delta in ((0, -384), (1, -128), (2, 0)):
                for h in range(4):
                    if delta <= -384:
                        src = bass.AP(tensor=relx.tensor, offset=int(relx[h, 385:386].offset), ap=[[0, 128], [1, 128]])
                    else:
                        src = bass.AP(tensor=relx.tensor, offset=int(relx[h, 0:1].offset) + (129 - delta), ap=[[1, 128], [1, 128]])
                    b.dma(tz[:, h, :], src)
                pz = P[0]
                b.mm(pz[:, 0:512], C['antiJ'][:], tz[:].rearrange("p a b -> p (a b)"))
                b.cp(Bt[:, kind, :, :].rearrange("p a b -> p (a b)"), pz[:, 0:512])
        KT = [b.sb('KT%d' % i, [128, 2, T if i < 2 else 1024], BF16, s) for i in range(3)]
        V = [b.sb('V%d' % i, [128, 16 if i < 2 else 8, 4, 65], BF16, s) for i in range(3)]
        for i in range(3):
            b.memset(V[i][:], 1.0)
        W = dict(win=win, wo=wo, gC=gC, gD=gD, gG=gG, mg=mg, dtb=dtb, nexpA=nexpA, fbf=fbf, Bt=Bt, KT=KT, V=V)
        alloc_work(b, W, s)
        import os
        for si in [int(c) for c in os.environ.get('KDBG_SEQS', '012')]:
            phaseAB(b, l, si, W, src_x)
    b.S.barrier()
    phaseC(b, l)
    b.S.barrier()


GP = 256
SKIP_D = False
OFFS = [1032, 1800, 2572]


def mid_bc(ap2d, n):
    a = [list(x) for x in ap2d.ap]
    return bass.AP(tensor=ap2d.tensor, offset=int(ap2d.offset), ap=[a[0], [0, n], a[1]])


def alloc_work(b, W, s):
    G = GP
    W['xg'] = b.sb('xg', [128, 8, G], F32, s)
    W['sqr'] = Rot([b.sb('sqr%d' % i, [128, G], BF16, s) for i in range(2)])
    W['rstd'] = b.sb('rstd', [128, G], F32, s)
    W['hT'] = b.sb('hT', [128, 8, G], BF16, s)
    W['QT'] = [b.sb('QT%d' % i, [128, 2, G], BF16, s) for i in range(3)]
    W['stg'] = Rot([b.sb('stg%d' % i, [128, 512], F32, s) for i in range(3)])
    W['qkb'] = Rot([b.sb('qkb%d' % i, [128, 512], BF16, s) for i in range(2)])
    W['nwk'] = (b.sb('nsq', [128, 8, 64], F32, s), b.sb('nss', [128, 8], F32, s), b.sb('ny', [128, 8, 64], F32, s))
    W['lf'] = b.sb('lf', [128, 4], F32, s)
    W['lfc'] = b.sb('lfc', [128, 8, 4], F32, s)
    W['negF'] = b.sb('negF', [128, 17, 4], F32, s)
    W['carry'] = b.sb('carry', [128, 4], F32, s)
    W['fbias'] = b.sb('fbias', [128, 17, 4], F32, s)
    W['mixed'] = b.sb('mixed', [128, G // 128, 1024], BF16, s)
    W['mixedT'] = b.sb('mixedT', [128, 8, G], BF16, s)
    X = [b.sb('X%d' % i, [128, 512], F32, s) for i in range(6)]
    W['X'] = X
    W['E'] = Rot([X[0], X[1]])
    W['zr'] = Rot([X[2], X[3]])
    W['Rr'] = [X[4], X[5]]
    W['htmp'] = Rot([X[1][:, 0:256], X[1][:, 256:512]])
    W['bt'] = Rot([X[4]])
    W['SP'] = Rot([b.sb('SP%d' % i, [128, 512], BF16, s) for i in range(2)])
    W['Wt'] = Rot([b.sb('Wt%d' % i, [128, 512], BF16, s) for i in range(3)])
    alloc_gdn(b, W, s)
    print('sbuf remaining after alloc', b.nc.sbuf_bytes_remaining)


def norm_mod(b, l, si, which, xg, G, W, hT):
    C, P = b.C, b.P
    ss = P[0][:, 0:G]
    for c in range(8):
        sq = W['sqr'].next()
        b.act(sq[:, 0:G], xg[:, c, 0:G], AF.Square)
        b.mm(ss, C['onesb'][:], sq[:, 0:G], start=(c == 0), stop=(c == 7))
    b.rsqrt(W['rstd'][:, 0:G], ss, 1.0 / D)
    for c in range(8):
        ht = W['htmp'].next()
        b.stt(ht[:, 0:G], xg[:, c, 0:G], b.modA[:, l, which, c, si:si + 1], W['rstd'][:, 0:G], ALU.mult, ALU.mult)
        b.act(hT[:, c, 0:G], ht[:, 0:G], AF.Identity, bias=b.mod[:, l, which * 24 + c, si:si + 1])


def phaseAB(b, l, si, W, src_x):
    nc = b.nc
    d, o, C, P, PB = b.d, b.o, b.C, b.P, b.PB
    prompt = si < 2
    Tq = T if prompt else TS
    tok0 = si * T if prompt else 2 * T
    G = GP if prompt else TS
    ngroups = Tq // G
    npast = 0 if prompt else PAST
    npastD = 0 if prompt else 512
    KT, V, win, wo, mg = W['KT'], W['V'], W['win'], W['wo'], W['mg']
    xg, hT, QT, mixed, mixedT = W['xg'], W['hT'], W['QT'], W['mixed'], W['mixedT']
    identb = C['identb']
    pbv = PB[:, 0:1024].rearrange("p (j t) -> p j t", t=128)
    pf = P[3]
    carry, negF, lf = W['carry'], W['negF'], W['lf']
    b.memset(carry[:], 0.0)
    b.memset(negF[:], 0.0)
    if not prompt:
        for i, (kn, vn, nrow) in enumerate((('sbk', 'sbv', 1024), ('fxk', 'fxv', 1024), ('bdk', 'bdv', 512))):
            for tl in range(nrow // 128):
                b.dma(V[i][:, tl, :, 0:64], d[vn][l, tl * 128:(tl + 1) * 128, :].rearrange("p (h e) -> p h e", e=64), q='pool')
                kb = W['qkb'].next()
                b.dma(kb[:, 0:256], d[kn][l, tl * 128:(tl + 1) * 128, :], q='pool')
                for j in range(2):
                    b.tr(pbv[:, j, :], kb[:, j * 128:(j + 1) * 128], identb[:])
                b.cp(KT[i][:, :, tl * 128:(tl + 1) * 128], pbv[:, 0:2, :])
        lfc = W['lfc']
        b.dma(lfc[:], d['fxlf'][l].rearrange("(t p) h -> p t h", p=128))
        for tl in range(8):
            b.mm(pf[:, 0:4], C['triIncf'][:], lfc[:, tl, :])
            b.stt(negF[:, tl, :], pf[:, 0:4], -1.0, carry[:], ALU.mult, ALU.subtract)
            b.mm(pf[:, 4:8], C['onesf'][:], lfc[:, tl, :])
            b.tt(carry[:], carry[:], pf[:, 4:8], ALU.add)
        b.dma(o['s_bdk'][l, 0:496, :], d['bdk'][l, 16:512, :])
        b.dma(o['s_bdv'][l, 0:496, :], d['bdv'][l, 16:512, :])
    gdn_seq_init(b, l, si, W)

    import os
    for g in range(min(ngroups, int(os.environ.get('KDBG_MAXG', '99')))):
        t0 = g * G
        b.dma(xg[:, :, 0:G], src_x[:, tok0 + t0:tok0 + t0 + G].rearrange("(c p) t -> p c t", p=128))
        norm_mod(b, l, si, 0, xg, G, W, hT)
        ntile = max(1, G // 128)
        nt = min(128, G)
        for tt in range(ntile):
            cols = slice(tt * 128, tt * 128 + nt)
            for i in range(3):
                c0 = OFFS[i]
                pqk, pv = P[1], P[2]
                nv = 260 if i == 1 else 256
                for c in range(8):
                    b.mm(pqk[0:nt, 0:512], hT[:, c, cols], win[:, c, c0:c0 + 512], start=(c == 0), stop=(c == 7))
                for c in range(8):
                    b.mm(pv[0:nt, 0:nv], hT[:, c, cols], win[:, c, c0 + 512:c0 + 512 + nv], start=(c == 0), stop=(c == 7))
                kbase = (npast if i < 2 else npastD) + t0 + tt * 128
                kt_i = kbase // 128
                if i == 2:
                    kbase = kbase % 1024
                    kt_i = kt_i % 8
                sv = W['stg'].next()
                b.cp(sv[0:nt, 0:256], pv[0:nt, 0:256], e='act')
                b.cp(V[i][0:nt, kt_i, :, 0:64], sv[0:nt, 0:256].rearrange("p (h e) -> p h e", e=64))
                sk = W['stg'].next()
                if i == 0:
                    b.cp(sk[0:nt, 0:512], pqk[0:nt, 0:512])
                else:
                    rmsnorm_tok(b, sk[0:nt, 0:512].rearrange("p (h e) -> p h e", e=64),
                                pqk[0:nt, 0:512].rearrange("p (h e) -> p h e", e=64), 8, nt,
                                W['gC'] if i == 1 else W['gD'], W['nwk'])
                tloc = t0 + tt * 128
                if prompt:
                    if i == 0:
                        b.dma(o['p_sbk'][l, si, tloc:tloc + nt, :], sk[0:nt, 256:512])
                        b.dma(o['p_sbv'][l, si, tloc:tloc + nt, :], sv[0:nt, 0:256])
                    elif i == 1:
                        b.dma(o['p_fxk'][l, si, tloc:tloc + nt, :], sk[0:nt, 256:512])
                        b.dma(o['p_fxv'][l, si, tloc:tloc + nt, :], sv[0:nt, 0:256])
                    elif tloc >= T - 512:
                        b.dma(o['p_bdk'][l, si, tloc - (T - 512):tloc - (T - 512) + nt, :], sk[0:nt, 256:512])
                        b.dma(o['p_bdv'][l, si, tloc - (T - 512):tloc - (T - 512) + nt, :], sv[0:nt, 0:256])
                else:
                    if i == 0:
                        b.dma(o['s_sbk'][l, 0:nt, :], sk[0:nt, 256:512])
                        b.dma(o['s_sbv'][l, 0:nt, :], sv[0:nt, 0:256])
                    elif i == 1:
                        b.dma(o['s_fxk'][l, 0:nt, :], sk[0:nt, 256:512])
                        b.dma(o['s_fxv'][l, 0:nt, :], sv[0:nt, 0:256])
                    else:
                        b.dma(o['s_bdk'][l, 496:512, :], sk[0:nt, 256:512])
                        b.dma(o['s_bdv'][l, 496:512, :], sv[0:nt, 0:256])
                qb = W['qkb'].next()
                b.cp(qb[0:nt, :], sk[0:nt, 0:512], e='act')
                for j in range(4):
                    b.tr(pbv[:, j, 0:nt], qb[0:nt, j * 128:(j + 1) * 128], identb[0:nt, 0:nt])
                b.cp(QT[i][:, :, tt * 128:tt * 128 + nt], pbv[:, 0:2, 0:nt])
                b.cp(KT[i][:, :, kbase:kbase + nt], pbv[:, 2:4, 0:nt], e='act')
                if i == 1:
                    b.tt(lf[0:nt, :], pv[0:nt, 256:260], W['fbf'][0:nt, :], ALU.add)
                    b.act(lf[0:nt, :], lf[0:nt, :], AF.Exp, scale=-1.0)
                    b.act(lf[0:nt, :], lf[0:nt, :], AF.Ln, bias=1.0)
                    b.ts(lf[0:nt, :], lf[0:nt, :], -1.0, ALU.mult)
                    if prompt:
                        b.dma(o['p_fxlf'][l, si, tloc:tloc + nt, :], lf[0:nt, :])
                    else:
                        b.dma(o['s_fxlf'][l, 0:nt, :], lf[0:nt, :])
                    b.mm(pf[0:nt, 0:4], C['triIncf'][0:nt, 0:nt], lf[0:nt, :])
                    b.stt(negF[0:nt, kt_i, :], pf[0:nt, 0:4], -1.0, carry[0:nt, :], ALU.mult, ALU.subtract)
                    b.mm(pf[:, 4:8], C['onesf'][0:nt, :], lf[0:nt, :])
                    b.tt(carry[:], carry[:], pf[:, 4:8], ALU.add)
        if b.stage < 2:
            continue
        nsb = ntile
        nq = nt
        qpos0 = npast + t0
        jmax = (qpos0 + G - 1) // 128
        Ops = P[4]
        zc = 0
        GW = 2 * G
        if b.stage >= 2 and not os.environ.get('KDBG_NOB'):
            for p in range(2):
                b.memset(W['Rr'][p][:, 0:GW], 0.0)
            firstB = True
            for j in range(jmax, -1, -1):
                nk = min(128, npast + Tq - j * 128)
                m = j * 128 - qpos0
                diag = m >= 0
                for p in range(2):
                    po = p * 64
                    Rr = W['Rr'][p]
                    pz = P[5 + zc % 2]
                    zc += 1
                    for hl in range(2):
                        b.mm(pz[0:nk, hl * G:(hl + 1) * G], KT[0][po:po + 64, hl, j * 128:j * 128 + nk], QT[0][po:po + 64, hl, 0:G])
                    E = W['E'].next()
                    b.act(E[0:nk, 0:GW], pz[0:nk, 0:GW], AF.Exp, scale=0.125)
                    sp = W['SP'].next()
                    b.act(sp[0:nk, 0:GW], E[0:nk, 0:GW], AF.Ln, bias=1.0)
                    if diag:
                        spv = sp[0:nk, 0:GW].rearrange("p (a q) -> p a q", a=2)
                        b.asel(spv, spv, [[0, 2], [1, G]], ALU.is_gt, 0.0, -m, -1)
                    b.mm(P[3][0:nk, 0:GW], C['triUb'][0:nk, 0:nk], sp[0:nk, 0:GW])
                    if j > 0:
                        b.mm(P[2][:, 0:GW], C['onesb'][0:nk, :], sp[0:nk, 0:GW])
                    zr = W['zr'].next()
                    b.act(zr[0:nk, 0:GW], pz[0:nk, 0:GW], AF.Identity, scale=0.125)
                    b.tt(zr[0:nk, 0:GW], zr[0:nk, 0:GW], Rr[0:nk, 0:GW], ALU.subtract)
                    b.tt(zr[0:nk, 0:GW], zr[0:nk, 0:GW], P[3][0:nk, 0:GW], ALU.subtract)
                    wt = W['Wt'].next()
                    b.act(wt[0:nk, 0:GW], zr[0:nk, 0:GW], AF.Exp)
                    if diag:
                        wtv = wt[0:nk, 0:GW].rearrange("p (a q) -> p a q", a=2)
                        b.asel(wtv, wtv, [[0, 2], [1, G]], ALU.is_gt, 0.0, -m, -1)
                    if j > 0:
                        b.tt(Rr[:, 0:GW], Rr[:, 0:GW], P[2][:, 0:GW], ALU.add)
                    for hl in range(2):
                        h = p + 2 * hl
                        for sb_ in range(nsb):
                            if diag and m >= sb_ * 128 + nq:
                                continue
                            oc0 = ((p * 2 + hl) * nsb + sb_) * 64
                            b.mm(Ops[0:nq, oc0:oc0 + 64], wt[0:nk, hl * G + sb_ * 128:hl * G + sb_ * 128 + nq],
                                 V[0][0:nk, j, h, 0:64], start=firstB, stop=(j == 0), skip_group_check=True)
                            firstB = False
            for p in range(2):
                for hl in range(2):
                    h = p + 2 * hl
                    oc0 = (p * 2 + hl) * nsb * 64
                    rmsnorm_tok(b, mixed[0:nq, 0:nsb, 256 + h * 64:256 + (h + 1) * 64],
                                Ops[0:nq, oc0:oc0 + nsb * 64].rearrange("p (s e) -> p s e", e=64), nsb, nq,
                                mid_bc(mg[:, h, :], nsb), W['nwk'])
        nj = jmax + 1
        b.tt(W['fbias'][:, 0:nj, :], negF[:, 0:nj, :], mid_bc(carry[:, :], nj), ALU.add)
        FQ = W['X'][0]
        for h in range(4 if b.stage >= 3 else 0):
            hp, po = h // 2, (h % 2) * 64
            first = True
            pfq = P[3]
            dg = W['nwk'][0][:, 0:2, :].rearrange("p a e -> p (a e)")
            for sb_ in range(nsb):
                jq_ = qpos0 // 128 + sb_
                b.ts(dg[0:nq, 0:nq], C['identf'][0:nq, 0:nq], W['fbias'][0:nq, jq_, h:h + 1], ALU.mult)
                b.mm(pfq[:, sb_ * 128:sb_ * 128 + nq], C['onesf'][0:nq, :], dg[0:nq, 0:nq])
            b.cp(FQ[:, 0:G], pfq[:, 0:G])
            for j in range(0, jmax + 1):
                nk = min(128, npast + Tq - j * 128)
                m = j * 128 - qpos0
                diag = m >= 0
                pz = P[5 + zc % 2]
                zc += 1
                b.mm(pz[0:nk, 0:G], KT[1][po:po + 64, hp, j * 128:j * 128 + nk], QT[1][po:po + 64, hp, 0:G])
                wt = W['Wt'].next()
                s_ = W['zr'].next()
                b.stt(s_[0:nk, 0:G], pz[0:nk, 0:G], 0.125, FQ[0:nk, 0:G], ALU.mult, ALU.subtract)
                if diag:
                    b.ts(s_[0:nk, 0:G], s_[0:nk, 0:G], W['fbias'][0:nk, j, h:h + 1], ALU.add, 30.0, ALU.min)
                    b.act(wt[0:nk, 0:G], s_[0:nk, 0:G], AF.Exp)
                    b.asel(wt[0:nk, 0:G], wt[0:nk, 0:G], [[1, G]], ALU.is_ge, 0.0, -m, -1)
                else:
                    b.act(wt[0:nk, 0:G], s_[0:nk, 0:G], AF.Exp, bias=W['fbias'][0:nk, j, h:h + 1])
                for sb_ in range(nsb):
                    if diag and m >= sb_ * 128 + nq:
                        continue
                    b.mm(Ops[0:nq, sb_ * 65:(sb_ + 1) * 65], wt[0:nk, sb_ * 128:sb_ * 128 + nq], V[1][0:nk, j, h, 0:65],
                         start=first, stop=(j == jmax), skip_group_check=True)
                    first = False
            ov = Ops[0:nq, 0:nsb * 65].rearrange("p (s e) -> p s e", e=65)
            rmsnorm_tok(b, mixed[0:nq, 0:nsb, 512 + h * 64:512 + (h + 1) * 64], ov[:, :, 0:64], nsb, nq,
                        mid_bc(mg[:, 4 + h, :], nsb), W['nwk'], den=ov[:, :, 64])
        if SKIP_D and b.stage >= 5:
            b.memset(mixed[:, :, 768:1024], 0.0)
        for tq in range(ntile if (b.stage >= 4 and not SKIP_D) else 0):
            qk0 = npastD + t0 + tq * 128
            jq = qk0 // 128
            tiles = list(range(max(0, jq - 4), jq + 1))
            first = True
            for j in tiles:
                nk = min(128, npastD + Tq - j * 128)
                kind = 2 if j == jq else (1 if j == jq - 1 else 0)
                bt = W['bt'].next()
                btv = bt[0:nk, 0:512].rearrange("p (h q) -> p h q", q=128)[:, :, 0:nq]
                for h in range(4):
                    hp, po = h // 2, (h % 2) * 64
                    b.mm(P[5 + h % 2][0:nk, hp * 128:hp * 128 + nq], KT[2][po:po + 64, hp, (j % 8) * 128:(j % 8) * 128 + nk],
                         QT[2][po:po + 64, hp, tq * 128:tq * 128 + nq])
                for par in range(2):
                    src = P[5 + par][0:nk, 0:256].rearrange("p (a q) -> p a q", q=128)[:, :, 0:nq]
                    dst = bt[0:nk, 0:512].rearrange("p (a c q) -> p a c q", a=2, c=2, q=128)[:, :, par, 0:nq]
                    b.act(dst, src, AF.Identity, scale=0.125)
                b.tt(btv, btv, W['Bt'][0:nk, kind, :, 0:nq], ALU.add)
                wt = W['Wt'].next()
                wtv = wt[0:nk, 0:512].rearrange("p (h q) -> p h q", q=128)[:, :, 0:nq]
                b.act(wtv, btv, AF.Exp)
                if prompt and j == jq:
                    b.tt(wtv, wtv, C['mask4'][0:nk, :, 0:nq], ALU.mult)
                if prompt and j == jq - 4:
                    b.tt(wtv, wtv, C['mask0'][0:nk, :, 0:nq], ALU.mult)
                for h in range(4):
                    b.mm(Ops[0:nq, h * 65:(h + 1) * 65], wt[0:nk, h * 128:h * 128 + nq], V[2][0:nk, j % 8, h, 0:65],
                         start=first, stop=(j == tiles[-1]), skip_group_check=True)
                    first = False
            ov = Ops[0:nq, 0:260].rearrange("p (s e) -> p s e", e=65)
            rmsnorm_tok(b, mixed[0:nq, tq, 768:1024].rearrange("p (h e) -> p h e", e=64), ov[:, :, 0:64], 4, nq,
                        mg[:, 8:12, :], W['nwk'], den=ov[:, :, 64])
        gdn_group(b, l, si, W, g, G, t0)
        if b.stage < 5:
            continue
        for tt in range(ntile):
            for cc in range(2, 8):
                b.tr(pbv[:, cc, 0:nt], mixed[0:nt, tt, cc * 128:(cc + 1) * 128], identb[0:nt, 0:nt])
            b.cp(mixedT[:, 2:8, tt * 128:tt * 128 + nt], pbv[:, 2:8, 0:nt])
        for oc in range(8):
            pso = P[oc % 2]
            for c in range(8):
                b.mm(pso[:, 0:G], wo[:, c, oc * 128:(oc + 1) * 128], mixedT[:, c, 0:G], start=(c == 0), stop=(c == 7))
            rt = W['htmp'].next()
            b.act(rt[:, 0:G], pso[:, 0:G], AF.Identity, scale=b.mod[:, l, 16 + oc, si:si + 1])
            b.tt(xg[:, oc, 0:G], xg[:, oc, 0:G], rt[:, 0:G], ALU.add)
        b.dma(o['yT'][:, tok0 + t0:tok0 + t0 + G].rearrange("(c p) t -> p c t", p=128), xg[:, :, 0:G])


def phaseC(b, l):
    nc = b.nc
    d, o, C, P = b.d, b.o, b.C, b.P
    if b.stage < 6:
        return
    with contextlib.ExitStack() as s:
        wup = b.sb('wup', [128, 8, 2 * DFF], BF16, s)
        wdn = b.sb('wdn', [128, 22, D], BF16, s)
        for c in range(8):
            for q4 in range(4):
                b.dma(wup[:, c, q4 * 1408:(q4 + 1) * 1408], d['w_up'][l, c * 128:(c + 1) * 128, q4 * 1408:(q4 + 1) * 1408], q='pool')
        for j in range(22):
            b.dma(wdn[:, j, :], d['w_down'][l, j * 128:(j + 1) * 128, :], q='pool')
        GC = 256
        W = {}
        xg = b.sb('cxg', [128, 8, GC], F32, s)
        W['sqr'] = Rot([b.sb('csqr%d' % i, [128, GC], BF16, s) for i in range(2)])
        W['rstd'] = b.sb('crstd', [128, GC], F32, s)
        W['htmp'] = Rot([b.sb('chtmp%d' % i, [128, GC], F32, s) for i in range(2)])
        h2T = b.sb('h2T', [128, 8, GC], BF16, s)
        ur = Rot([b.sb('ur%d' % i, [128, GC + 2], F32, s) for i in range(4)])
        ctr = Rot([b.sb('ctr%d' % i, [128, GC], F32, s) for i in range(4)])
        hist = b.sb('hist', [128, 44, 2], F32, s)
        actT = b.sb('actT', [128, 22, GC], BF16, s)
        fo = b.sb('fo', [2, 2 * DFF], F32, s)
        fcw = b.fcw
        pc_ = 0
        for si in range(3):
            prompt = si < 2
            Tq = T if prompt else TS
            tok0 = si * T if prompt else 2 * T
            G = GC if prompt else TS
            ngroups = Tq // G
            if prompt:
                b.memset(hist[:], 0.0)
            else:
                b.dma(hist[:], d['fconvT'][l])
            for g in range(ngroups):
                t0 = g * G
                ycols = o['yT'][:, tok0 + t0:tok0 + t0 + G].rearrange("(c p) t -> p c t", p=128)
                b.dma(xg[:, :, 0:G], ycols)
                norm_mod(b, l, si, 1, xg, G, W, h2T)
                for j in range(22):
                    cv = []
                    for half in range(2):
                        jj = half * 22 + j
                        colb = half * DFF + j * 128
                        pu = P[1 + pc_ % 4]
                        pc_ += 1
                        for c in range(8):
                            b.mm(pu[:, 0:G], wup[:, c, colb:colb + 128], h2T[:, c, 0:G], start=(c == 0), stop=(c == 7))
                        u = ur.next()
                        b.cp(u[:, 0:2], hist[:, jj, :], e='pool')
                        b.cp(u[:, 2:2 + G], pu[:, 0:G], e='act')
                        b.cp(hist[:, jj, :], u[:, G:G + 2], e='pool')
                        ct = ctr.next()
                        b.ts(ct[:, 0:G], u[:, 0:G], fcw[:, l, jj, 0:1], ALU.mult)
                        b.stt(ct[:, 0:G], u[:, 1:1 + G], fcw[:, l, jj, 1:2], ct[:, 0:G], ALU.mult, ALU.add)
                        b.stt(ct[:, 0:G], u[:, 2:2 + G], fcw[:, l, jj, 2:3], ct[:, 0:G], ALU.mult, ALU.add)
                        cv.append(ct)
                    sil = ctr.next()
                    b.act(sil[:, 0:G], cv[0][:, 0:G], AF.Silu)
                    b.tt(actT[:, j, 0:G], sil[:, 0:G], cv[1][:, 0:G], ALU.mult)
                for oc in range(8):
                    pd = P[5 + oc % 2]
                    for j in range(22):
                        b.mm(pd[:, 0:G], wdn[:, j, oc * 128:(oc + 1) * 128], actT[:, j, 0:G], start=(j == 0), stop=(j == 21))
                    rt = W['htmp'].next()
                    b.act(rt[:, 0:G], pd[:, 0:G], AF.Identity, scale=b.mod[:, l, 40 + oc, si:si + 1])
                    b.tt(xg[:, oc, 0:G], xg[:, oc, 0:G], rt[:, 0:G], ALU.add)
                b.dma(ycols, xg[:, :, 0:G])
                if g == ngroups - 1:
                    for q11 in range(11):
                        pcx = P[0]
                        for c in range(8):
                            b.mm(pcx[0:2, 0:512], h2T[:, c, G - 2:G], wup[:, c, q11 * 512:(q11 + 1) * 512], start=(c == 0), stop=(c == 7))
                        b.cp(fo[0:2, q11 * 512:(q11 + 1) * 512], pcx[0:2, 0:512])
                    if prompt:
                        b.dma(o['p_fconv'][l, si, :, :], fo[:])
                    else:
                        b.dma(o['s_fconv'][l, :, :], fo[:])


def alloc_gdn(b, W, s):
    G = GP
    W['ghist'] = b.sb('ghist', [64, 12, 3], F32, s)
    W['pst'] = Rot([b.sb('pst%d' % i, [64, G + 3], F32, s) for i in range(2)])
    W['cv'] = Rot([b.sb('cv%d' % i, [64, G], F32, s) for i in range(2)])
    W['gsq'] = b.sb('gsq', [64, G], BF16, s)
    W['grn'] = b.sb('grn', [64, G], F32, s)
    W['qkvn'] = b.sb('qkvn', [64, 12, G], BF16, s)
    W['Sf'] = b.sb('Sf', [64, 4, 64], F32, s)
    W['Sb'] = b.sb('Sb', [64, 4, 64], BF16, s)
    W['g3'] = b.sb('g3', [3, 768], F32, s)
    v64 = lambda t, c0: t[0:64, c0:c0 + 256].rearrange("p (h e) -> p h e", e=64)
    hosts32 = [(W['X'][i], c0) for i in range(4) for c0 in (0, 256)]
    f32n = ['zsil', 'dec', 'decI', 'decS', 'Uf', 'gb', 'vtok', 'of']
    for n, (t, c0) in zip(f32n, hosts32):
        W[n] = v64(t, c0)
    W['on'] = b.sb('on', [64, 4, 64], F32, s)[:]
    hosts16 = [(W['Wt'].bufs[i], c0) for i in range(3) for c0 in (0, 256)] + [(W['SP'].bufs[i], 0) for i in range(2)] + \
              [(W['qkb'].bufs[i], c0) for i in range(2) for c0 in (0, 256)]
    for n, (t, c0) in zip(['ktok', 'kd', 'vn', 'AT', 'om'], hosts16):
        W[n] = v64(t, c0)
    hosts32b = [(W['stg'].bufs[i], c0) for i in range(3) for c0 in (0, 256)]
    for n, (t, c0) in zip(['Pa', 'Qa', 'Qb', 'Ya', 'Yb'], hosts32b):
        W[n] = v64(t, c0)
    W['gsm'] = b.sb('gsm', [64, 10, 4], F32, s)


def gdn_seq_init(b, l, si, W):
    d = b.d
    if si < 2:
        b.memset(W['ghist'][:], 0.0)
        b.memset(W['Sf'][:], 0.0)
        b.memset(W['Sb'][:], 0.0)
    else:
        b.dma(W['ghist'][:], d['gconvT'][l])
        b.dma(W['Sf'][:], d['gstate'][l].rearrange("h k v -> k h v"))
        b.cp(W['Sb'][:], W['Sf'][:])


def gdn_group(b, l, si, W, g, G, t0):
    import os
    if os.environ.get('KDBG_NOGDN'):
        b.memset(W['mixedT'][:, 0:2, 0:G], 0.0)
        return
    nc = b.nc
    d, o, C, P, PB = b.d, b.o, b.C, b.P, b.PB
    prompt = si < 2
    Tq = T if prompt else TS
    win, hT = W['win'], W['hT']
    qkvn = W['qkvn']
    gcw = b.gcw
    for hc in range(12):
        pg = P[hc % 2]
        for c in range(8):
            b.mm(pg[0:64, 0:G], win[:, c, hc * 64:(hc + 1) * 64], hT[:, c, 0:G], start=(c == 0), stop=(c == 7))
        pst = W['pst'].next()
        b.cp(pst[:, 0:3], W['ghist'][:, hc, :], e='pool')
        b.cp(pst[:, 3:3 + G], pg[0:64, 0:G], e='act')
        b.cp(W['ghist'][:, hc, :], pst[:, G:G + 3], e='pool')
        cv = W['cv'].next()
        b.ts(cv[:, 0:G], pst[:, 0:G], gcw[:, l, hc, 0:1], ALU.mult)
        for i in range(1, 4):
            b.stt(cv[:, 0:G], pst[:, i:i + G], gcw[:, l, hc, i:i + 1], cv[:, 0:G], ALU.mult, ALU.add)
        b.act(cv[:, 0:G], cv[:, 0:G], AF.Silu)
        if hc < 8:
            b.act(W['gsq'][:, 0:G], cv[:, 0:G], AF.Square)
            pn = P[2]
            b.mm(pn[0:64, 0:G], C['onesb'][0:64, 0:64], W['gsq'][:, 0:G])
            b.rsqrt(W['grn'][:, 0:G], pn[0:64, 0:G], 1.0)
            if hc < 4:
                b.stt(qkvn[:, hc, 0:G], cv[:, 0:G], 0.125, W['grn'][:, 0:G], ALU.mult, ALU.mult)
            else:
                b.tt(qkvn[:, hc, 0:G], cv[:, 0:G], W['grn'][:, 0:G], ALU.mult)
        else:
            b.cp(qkvn[:, hc, 0:G], cv[:, 0:G], e='pool')
    if t0 + G == Tq:
        p3 = P[3]
        for hc in range(8):
            b.tr(p3[0:3, hc * 64:(hc + 1) * 64], W['ghist'][:, hc, :], C['identf'][0:64, 0:64])
        b.cp(W['g3'][:, 0:512], p3[0:3, 0:512])
        p3b = P[2]
        for hc in range(8, 12):
            b.tr(p3b[0:3, (hc - 8) * 64:(hc - 7) * 64], W['ghist'][:, hc, :], C['identf'][0:64, 0:64])
        b.cp(W['g3'][:, 512:768], p3b[0:3, 0:256])
        if prompt:
            b.dma(o['p_gconv'][l, si, :, :], W['g3'][:])
        else:
            b.dma(o['s_gconv'][l, :, :], W['g3'][:])
    Lc = min(64, G)
    nlev = 5 if Lc == 64 else 3
    gsm = W['gsm']
    v3 = lambda t: t[0:Lc, :, 0:Lc]
    pv3 = lambda p, c0: p[0:Lc, c0:c0 + 256].rearrange("p (h e) -> p h e", e=64)
    tri = C['triIncf'][0:Lc, 0:Lc]
    for ch in range(G // Lc):
        cols = slice(ch * Lc, (ch + 1) * Lc)
        pzb = P[2]
        for c in range(8):
            b.mm(pzb[0:Lc, 0:264], hT[:, c, cols], win[:, c, 768:1032], start=(c == 0), stop=(c == 7))
        beta, gg, Gc, eG, neG, eGl, ekd, tmp4 = (gsm[:, i, :] for i in range(8))
        Sf, Sb = W['Sf'], W['Sb']
        pS = P[0]
        for h in range(4):
            b.mm(pS[0:Lc, h * 64:(h + 1) * 64], qkvn[:, 4 + h, cols], Sb[:, h, :])
        for h in range(4):
            b.mm(pS[0:Lc, 256 + h * 64:256 + (h + 1) * 64], qkvn[:, h, cols], Sb[:, h, :])
        pbk = PB[:, 0:512].rearrange("p (a h e) -> p a h e", a=2, e=64)
        for h in range(4):
            b.tr(pbk[0:Lc, 0, h, :], qkvn[:, 4 + h, cols], C['identb'][0:64, 0:64])
            b.tr(pbk[0:Lc, 1, h, :], qkvn[:, 8 + h, cols], C['identb'][0:64, 0:64])
        ktok, vtok, kd = W['ktok'], W['vtok'], W['kd']
        b.cp(ktok[0:Lc, :, :], pbk[0:Lc, 0, :, :])
        b.cp(vtok[0:Lc, :, :], pbk[0:Lc, 1, :, :])
        pKK = P[4]
        for h in range(4):
            b.mm(pKK[0:Lc, h * 64:h * 64 + Lc], qkvn[:, 4 + h, cols], qkvn[:, 4 + h, cols])
        for h in range(4):
            b.mm(pKK[0:Lc, 256 + h * 64:256 + h * 64 + Lc], qkvn[:, 4 + h, cols], qkvn[:, h, cols])
        zsil = W['zsil']
        pzv = pzb[0:Lc, 0:256].rearrange("p (h e) -> p h e", e=64)
        b.act(zsil[0:Lc, :, :], pzv, AF.Exp, scale=-1.0)
        b.act(beta[0:Lc, :], pzb[0:Lc, 256:260], AF.Exp, scale=-1.0)
        b.ts(zsil[0:Lc, :, :], zsil[0:Lc, :, :], 1.0, ALU.add)
        b.recip(zsil[0:Lc, :, :], zsil[0:Lc, :, :])
        b.tt(zsil[0:Lc, :, :], zsil[0:Lc, :, :], pzv, ALU.mult)
        b.ts(beta[0:Lc, :], beta[0:Lc, :], 1.0, ALU.add)
        b.recip(beta[0:Lc, :], beta[0:Lc, :])
        b.tt(tmp4[0:Lc, :], pzb[0:Lc, 260:264], W['dtb'][0:Lc, :], ALU.add)
        b.act(tmp4[0:Lc, :], tmp4[0:Lc, :], AF.Exp)
        b.act(tmp4[0:Lc, :], tmp4[0:Lc, :], AF.Ln, bias=1.0)
        b.tt(gg[0:Lc, :], tmp4[0:Lc, :], W['nexpA'][0:Lc, :], ALU.mult)
        pG = P[3]
        b.mm(pG[0:Lc, 0:4], tri, gg[0:Lc, :])
        b.mm(pG[0:64, 4:8], C['onesf'][0:Lc, 0:64], gg[0:Lc, :])
        b.cp(Gc[0:Lc, :], pG[0:Lc, 0:4])
        b.act(eGl[:, :], pG[0:64, 4:8], AF.Exp)
        b.tt(ekd[0:Lc, :], pG[0:Lc, 4:8], Gc[0:Lc, :], ALU.subtract)
        b.act(ekd[0:Lc, :], ekd[0:Lc, :], AF.Exp)
        b.act(eG[0:Lc, :], Gc[0:Lc, :], AF.Exp)
        b.tt(W['kd'][0:Lc, :, :], W['ktok'][0:Lc, :, :], bc(ekd[0:Lc, :], [Lc, 4, 64]), ALU.mult, e='pool')
        b.ts(neG[0:Lc, :], eG[0:Lc, :], -1.0, ALU.mult)
        gb = W['gb']
        for h in range(4):
            b.ts(gb[0:Lc, h, 0:Lc], C['onesf'][0:Lc, 0:Lc], gg[0:Lc, h:h + 1], ALU.mult)
        pGr = P[3]
        for h in range(4):
            b.mm(pGr[0:Lc, 256 + h * 64:256 + h * 64 + Lc], gb[0:Lc, h, 0:Lc], tri)
        dec, decI, decS = W['dec'], W['decI'], W['decS']
        for h in range(4):
            b.ts(dec[0:Lc, h, 0:Lc], pGr[0:Lc, 256 + h * 64:256 + h * 64 + Lc], Gc[0:Lc, h:h + 1], ALU.subtract, 0.0, ALU.min)
        b.act(v3(dec), v3(dec), AF.Exp)
        b.tt(v3(decI), v3(dec), mid_bc(tri, 4), ALU.mult)
        b.tt(v3(decS), v3(dec), mid_bc(C['sutf'][0:Lc, 0:Lc], 4), ALU.mult, e='pool')
        Uf = W['Uf']
        b.tt(v3(Uf), pv3(pKK, 0)[:, :, 0:Lc], v3(decS), ALU.mult)
        b.tt(v3(Uf), v3(Uf), bc(beta[0:Lc, :], [Lc, 4, Lc]), ALU.mult)
        AT = W['AT']
        b.tt(v3(AT), pv3(pKK, 256)[:, :, 0:Lc], v3(decI), ALU.mult)
        Pc, Pn_, Qc, Qn_, Yc, Yn_ = Uf, W['Pa'], W['Qa'], W['Qb'], W['Ya'], W['Yb']
        pq = P[5]
        for h in range(4):
            b.tr(pq[0:Lc, h * 64:h * 64 + Lc], Uf[0:Lc, h, 0:Lc], C['identf'][0:Lc, 0:Lc])
        b.cp(v3(Qc), pv3(pq, 0)[:, :, 0:Lc])
        b.tt(v3(Yc), mid_bc(C['identf'][0:Lc, 0:Lc], 4), v3(Uf), ALU.subtract)
        for k in range(1, nlev + 1):
            pI = P[5]
            for h in range(4):
                b.mm(pI[0:Lc, h * 64:h * 64 + Lc], Pc[0:Lc, h, 0:Lc], Qc[0:Lc, h, 0:Lc])
            if k < nlev:
                for h in range(4):
                    b.mm(pI[0:Lc, 256 + h * 64:256 + h * 64 + Lc], Qc[0:Lc, h, 0:Lc], Pc[0:Lc, h, 0:Lc])
            b.cp(v3(Qn_), pv3(pI, 0)[:, :, 0:Lc], e='act')
            if k < nlev:
                b.cp(v3(Pn_), pv3(pI, 256)[:, :, 0:Lc])
            pY = P[6]
            for h in range(4):
                b.mm(pY[0:Lc, h * 64:h * 64 + Lc], Qn_[0:Lc, h, 0:Lc], Yc[0:Lc, h, 0:Lc])
            b.tt(v3(Yn_), v3(Yc), pv3(pY, 0)[:, :, 0:Lc], ALU.add)
            Pc, Pn_ = Pn_, Pc
            Qc, Qn_ = Qn_, Qc
            Yc, Yn_ = Yn_, Yc
        Rm, vn, of = W['gb'], W['vn'], W['of']
        b.tt(of[0:Lc, :, :], pv3(pS, 0), bc(neG[0:Lc, :], [Lc, 4, 64]), ALU.mult)
        b.tt(Rm[0:Lc, :, :], of[0:Lc, :, :], vtok[0:Lc, :, :], ALU.add)
        b.tt(of[0:Lc, :, :], pv3(pS, 256), bc(eG[0:Lc, :], [Lc, 4, 64]), ALU.mult)
        pX = P[1]
        for h in range(4):
            b.mm(pX[0:Lc, h * 64:(h + 1) * 64], Yc[0:Lc, h, 0:Lc], Rm[0:Lc, h, :])
        b.tt(vn[0:Lc, :, :], pv3(pX, 0), bc(beta[0:Lc, :], [Lc, 4, 64]), ALU.mult)
        for h in range(4):
            b.mm(pX[0:Lc, 256 + h * 64:256 + (h + 1) * 64], AT[0:Lc, h, 0:Lc], vn[0:Lc, h, :])
        b.tt(of[0:Lc, :, :], of[0:Lc, :, :], pv3(pX, 256), ALU.add)
        pSn = P[4]
        for h in range(4):
            b.mm(pSn[0:64, h * 64:(h + 1) * 64], kd[0:Lc, h, :], vn[0:Lc, h, :])
        b.tt(Sf[:, :, :], Sf[:, :, :], bc(eGl[:, :], [64, 4, 64]), ALU.mult)
        b.tt(Sf[:, :, :], Sf[:, :, :], pSn[0:64, 0:256].rearrange("p (h e) -> p h e", e=64), ALU.add)
        b.cp(Sb[:, :, :], Sf[:, :, :], e='act')
        on, om = W['on'], W['om']
        rmsnorm_tok(b, on[0:Lc, :, :], of[0:Lc, :, :], 4, Lc, W['gG'], W['nwk'])
        b.tt(om[0:Lc, :, :], on[0:Lc, :, :], zsil[0:Lc, :, :], ALU.mult, e='pool')
        pm = PB[:, 768:1024].rearrange("p (a t) -> p a t", t=128)
        omf = om[0:Lc, :, :].rearrange("p h e -> p (h e)")
        for cc in range(2):
            b.tr(pm[:, cc, 0:Lc], omf[:, cc * 128:(cc + 1) * 128], C['identb'][0:Lc, 0:Lc])
        b.cp(W['mixedT'][:, 0:2, ch * Lc:(ch + 1) * Lc], pm[:, 0:2, 0:Lc])
    if t0 + G == Tq:
        if prompt:
            b.dma(o['p_gstate'][l, si].rearrange("h k v -> k h v"), W['Sf'][:])
        else:
            b.dma(o['s_gstate'][l].rearrange("h k v -> k h v"), W['Sf'][:])


_NC_CACHE = {}


def _prep_inputs(inp):
    f = lambda a: np.ascontiguousarray(np.asarray(a, dtype=np.float32))
    L = DEPTH
    shared = {
        'ada_w': f(inp['ada_w']),
        'ada_bT': f(np.asarray(inp['ada_b']).reshape(L, 48, 128).transpose(2, 0, 1)),
        'nmgT': f(np.asarray(inp['norm_mix_g']).reshape(L, 8, 128).transpose(2, 0, 1)),
        'nfgT': f(np.asarray(inp['norm_ffn_g']).reshape(L, 8, 128).transpose(2, 0, 1)),
        'w_in': f(inp['w_in']),
        'gcwT': f(np.asarray(inp['gdn_conv_w']).reshape(L, 4, 12, 64).transpose(3, 0, 2, 1)),
        'a_log': f(inp['gdn_a_log']), 'dt_bias': f(inp['gdn_dt_bias']), 'gn_g': f(inp['gdn_norm_g']),
        'fq_g': f(inp['fox_q_g']), 'fk_g': f(inp['fox_k_g']), 'fb_f': f(inp['fox_b_f']),
        'bq_g': f(inp['band_q_g']), 'bk_g': f(inp['band_k_g']), 'rel': f(inp['band_rel_bias']),
        'mg': f(np.asarray(inp['merge_g']).reshape(L, 768)),
        'w_o': f(inp['w_o']), 'w_up': f(inp['w_up']),
        'fcwT': f(np.asarray(inp['ffn_conv_w']).reshape(L, 3, 44, 128).transpose(3, 0, 2, 1)),
        'w_down': f(inp['w_down']),
    }
    xp = np.asarray(inp['x_prompt'], dtype=np.float32)
    xs = np.asarray(inp['x_sample'], dtype=np.float32)
    cp_ = np.asarray(inp['c_prompt'], dtype=np.float32)
    cs = np.asarray(inp['c_sample'], dtype=np.float32)
    maps = []
    for k in range(8):
        m = dict(shared)
        xT = np.empty((D, NTOK), np.float32)
        xT[:, 0:T] = xp[2 * k].T
        xT[:, T:2 * T] = xp[2 * k + 1].T
        xT[:, 2 * T:] = xs[k].T
        m['xT'] = xT
        cc = np.zeros((4, D), np.float32)
        cc[0], cc[1], cc[2] = cp_[2 * k], cp_[2 * k + 1], cs[k]
        m['cT'] = f(cc.reshape(4, 8, 128).transpose(2, 1, 0))
        m['gconvT'] = f(np.asarray(inp['state_gdn_conv'])[:, k].reshape(L, 3, 12, 64).transpose(0, 3, 2, 1))
        m['gstate'] = f(np.asarray(inp['state_gdn'])[:, k])
        m['sbk'] = f(np.asarray(inp['cache_sb_k'])[:, k].reshape(L, PAST, 256))
        m['sbv'] = f(np.asarray(inp['cache_sb_v'])[:, k].reshape(L, PAST, 256))
        m['fxk'] = f(np.asarray(inp['cache_fox_k'])[:, k].reshape(L, PAST, 256))
        m['fxv'] = f(np.asarray(inp['cache_fox_v'])[:, k].reshape(L, PAST, 256))
        m['fxlf'] = f(np.asarray(inp['cache_fox_logf'])[:, k])
        m['bdk'] = f(np.asarray(inp['cache_band_k'])[:, k].reshape(L, 512, 256))
        m['bdv'] = f(np.asarray(inp['cache_band_v'])[:, k].reshape(L, 512, 256))
        m['fconvT'] = f(np.asarray(inp['state_ffn_conv'])[:, k].reshape(L, 2, 44, 128).transpose(0, 3, 2, 1))
        maps.append(m)
    return maps


def _assemble(res):
    L = DEPTH
    r = res
    yp = np.empty((16, T, D), np.float32)
    ys = np.empty((8, TS, D), np.float32)
    for k in range(8):
        yT = r[k]['yT']
        yp[2 * k] = yT[:, 0:T].T
        yp[2 * k + 1] = yT[:, T:2 * T].T
        ys[k] = yT[:, 2 * T:].T
    cat_p = lambda n, shp: np.concatenate([r[k][n] for k in range(8)], axis=1).reshape(shp)
    stk_s = lambda n, shp: np.stack([r[k][n] for k in range(8)], axis=1).reshape(shp)
    outs = [yp, ys,
            cat_p('p_gconv', (L, 16, 3, 768)), cat_p('p_gstate', (L, 16, 4, 64, 64)),
            cat_p('p_sbk', (L, 16, T, 4, 64)), cat_p('p_sbv', (L, 16, T, 4, 64)),
            cat_p('p_fxk', (L, 16, T, 4, 64)), cat_p('p_fxv', (L, 16, T, 4, 64)),
            cat_p('p_fxlf', (L, 16, T, 4)),
            cat_p('p_bdk', (L, 16, 512, 4, 64)), cat_p('p_bdv', (L, 16, 512, 4, 64)),
            cat_p('p_fconv', (L, 16, 2, 2 * DFF)),
            stk_s('s_gconv', (L, 8, 3, 768)), stk_s('s_gstate', (L, 8, 4, 64, 64)),
            stk_s('s_sbk', (L, 8, TS, 4, 64)), stk_s('s_sbv', (L, 8, TS, 4, 64)),
            stk_s('s_fxk', (L, 8, TS, 4, 64)), stk_s('s_fxv', (L, 8, TS, 4, 64)),
            stk_s('s_fxlf', (L, 8, TS, 4)),
            stk_s('s_bdk', (L, 8, 512, 4, 64)), stk_s('s_bdv', (L, 8, 512, 4, 64)),
            stk_s('s_fconv', (L, 8, 2, 2 * DFF))]
    return tuple(np.ascontiguousarray(a, dtype=np.float32) for a in outs)


def kernel(**inputs):
    maps = _prep_inputs(inputs)
    if 'nc' not in _NC_CACHE:
        _NC_CACHE['nc'] = build()
    res = run_bass_kernel_spmd(_NC_CACHE['nc'], maps, core_ids=list(range(8)))
    return _assemble(res.results)
```

```python
import contextlib
import numpy as np
import concourse.bass as bass
import concourse.mybir as mybir
from concourse.bass_utils import run_bass_kernel_spmd

F32 = mybir.dt.float32
BF16 = mybir.dt.bfloat16
AF = mybir.ActivationFunctionType
ALU = mybir.AluOpType
AX = mybir.AxisListType

D = 1024
T = 2048
DEPTH = 4
TS = 16
PAST = 1024
NTOK = 2 * T + TS
INC = 3340
DFF = 2816
EPS = 1e-6


def _box(ap):
    t = ap.tensor
    name = t.name
    dims = [(int(s), int(c)) for s, c in ap.ap]
    off = int(ap.offset)
    if 'DRAM' in str(ap.space).upper():
        lo = hi = off
        for s, c in dims:
            if s >= 0:
                hi += s * (c - 1)
            else:
                lo += s * (c - 1)
        return (name, 0, 1, lo, hi + 1)
    import os
    if 'PSUM' in str(ap.space).upper() and (os.environ.get('KDBG_PSUMBOX', '1') == '1' or (os.environ.get('KDBG_PSUMBOX') == '2' and name.startswith('pbb'))):
        return (name, 0, 128, 0, 1 << 30)
    psize = 1
    for d in list(t.shape)[1:]:
        psize *= int(d)
    p0 = off // psize
    f0 = off % psize
    pstep, pcnt = dims[0]
    if pstep == psize or pcnt == 1:
        np_ = pcnt
    elif pstep == 0:
        np_ = 1
    else:
        np_ = 128 - p0
    lo = hi = f0
    for s, c in dims[1:]:
        if s >= 0:
            hi += s * (c - 1)
        else:
            lo += s * (c - 1)
    return (name, p0, p0 + np_, lo, hi + 1)


class Sync:
    def __init__(self, nc, stack, n_dma_sems=8):
        self.nc = nc
        self.eng = {'pe': nc.tensor, 'act': nc.scalar, 'dve': nc.vector, 'pool': nc.gpsimd, 'sp': nc.sync}
        self.sem = {}
        self.cnt = {}
        for e in ('pe', 'act', 'dve', 'pool'):
            self.sem[e] = stack.enter_context(nc.semaphore('s_' + e))
            self.cnt[e] = 0
        self.dsem = {}
        self.dcnt = {}
        for q in ('sp', 'pool'):
            self.dsem[q] = [stack.enter_context(nc.semaphore('d_%s%d' % (q, i))) for i in range(n_dma_sems)]
            self.dcnt[q] = 0
        self.K = n_dma_sems
        self.semobj = {}
        for e, s in self.sem.items():
            self.semobj[('c', e)] = s
        for q, l in self.dsem.items():
            for i, s in enumerate(l):
                self.semobj[('d', q, i)] = s
        self.waited = {e: {} for e in self.eng}
        self.W = {}
        self.R = {}
        self.n_wait = 0
        self.n_inst = 0

    def _collect(self, reads, writes):
        deps = {}
        for ap in reads:
            b = _box(ap)
            for r in self.W.get(b[0], ()):
                if r[0] < b[2] and b[1] < r[1] and r[2] < b[4] and b[3] < r[3]:
                    if deps.get(r[4], 0) < r[5]:
                        deps[r[4]] = r[5]
        for ap in writes:
            b = _box(ap)
            for r in self.W.get(b[0], ()):
                if r[0] < b[2] and b[1] < r[1] and r[2] < b[4] and b[3] < r[3]:
                    if deps.get(r[4], 0) < r[5]:
                        deps[r[4]] = r[5]
            rd = self.R.get(b[0])
            if rd:
                for k, v in rd.items():
                    if k[0] < b[2] and b[1] < k[1] and k[2] < b[4] and b[3] < k[3]:
                        if deps.get(k[4], 0) < v:
                            deps[k[4]] = v
        return deps

    def _record(self, reads, writes, semkey, val):
        for ap in writes:
            b = _box(ap)
            lst = self.W.setdefault(b[0], [])
            lst[:] = [r for r in lst if not (b[1] <= r[0] and r[1] <= b[2] and b[3] <= r[2] and r[3] <= b[4])]
            lst.append([b[1], b[2], b[3], b[4], semkey, val])
            rd = self.R.get(b[0])
            if rd:
                for k in [k for k in rd if b[1] <= k[0] and k[1] <= b[2] and b[3] <= k[2] and k[3] <= b[4]]:
                    del rd[k]
        for ap in reads:
            b = _box(ap)
            self.R.setdefault(b[0], {})[(b[1], b[2], b[3], b[4], semkey)] = val

    def _emit_waits(self, e, deps):
        w = self.waited[e]
        for k, v in deps.items():
            if k == ('c', 'pe') and e == 'pe':
                continue
            if w.get(k, 0) >= v:
                continue
            self.eng[e].wait_ge(self.semobj[k], v)
            w[k] = v
            self.n_wait += 1

    def op(self, e, fn, reads=(), writes=()):
        px = [a for a in reads if 'PSUM' in str(a.space).upper()]
        if px:
            writes = list(writes) + px
        deps = self._collect(reads, writes)
        self._emit_waits(e, deps)
        inst = fn()
        self.cnt[e] += 1
        inst.then_inc(self.sem[e], 1)
        self._record(reads, writes, ('c', e), self.cnt[e])
        self.n_inst += 1
        return inst

    def dma(self, q, out, in_, **kw):
        deps = self._collect([in_], [out])
        n = self.dcnt[q]
        k = n % self.K
        semkey = ('d', q, k)
        prev = (n // self.K) * 16
        if prev > 0:
            deps[semkey] = max(deps.get(semkey, 0), prev)
        self._emit_waits(q, deps)
        inst = self.eng[q].dma_start(out=out, in_=in_, **kw)
        inst.then_inc(self.semobj[semkey], 16)
        self.dcnt[q] = n + 1
        self._record([in_], [out], semkey, prev + 16)
        self.n_inst += 1
        return inst

    def barrier(self):
        toks = {}
        for e in ('pe', 'act', 'dve', 'pool'):
            if self.cnt[e] > 0:
                toks[('c', e)] = self.cnt[e]
        for q in self.dsem:
            n = self.dcnt[q]
            for k in range(self.K):
                cntk = (n - k + self.K - 1) // self.K if n > k else 0
                if cntk > 0:
                    toks[('d', q, k)] = cntk * 16
        for e in self.eng:
            w = self.waited[e]
            for k, v in toks.items():
                if k == ('c', 'pe') and e == 'pe':
                    continue
                if w.get(k, 0) >= v:
                    continue
                self.eng[e].wait_ge(self.semobj[k], v)
                w[k] = v
                self.n_wait += 1
        self.W = {}
        self.R = {}


class B:
    def __init__(self, nc, st, n_layers=DEPTH, stage=99):
        self.nc = nc
        self.st = st
        self.S = Sync(nc, st)
        self.n_layers = n_layers
        self.stage = stage
        self.uid = 0

    def sb(self, name, shape, dt, st=None):
        self.uid += 1
        return (st or self.st).enter_context(self.nc.sbuf_tensor('%s_%d' % (name, self.uid), shape, dt))

    def ps(self, name, shape, dt, st=None):
        self.uid += 1
        return (st or self.st).enter_context(self.nc.psum_tensor('%s_%d' % (name, self.uid), shape, dt))

    def din(self, name, shape):
        return self.nc.dram_tensor(name, list(shape), F32, kind="ExternalInput").ap()

    def dout(self, name, shape):
        return self.nc.dram_tensor(name, list(shape), F32, kind="ExternalOutput").ap()

    def mm(self, out, lhsT, rhs, start=True, stop=True, **kw):
        nc = self.nc
        return self.S.op('pe', lambda: nc.tensor.matmul(out, lhsT=lhsT, rhs=rhs, start=start, stop=stop, **kw),
                         reads=[lhsT, rhs], writes=[out])

    def tr(self, out, in_, ident):
        nc = self.nc
        return self.S.op('pe', lambda: nc.tensor.transpose(out=out, in_=in_, identity=ident),
                         reads=[in_, ident], writes=[out])

    def act(self, out, in_, func, bias=None, scale=1.0):
        nc = self.nc
        reads = [in_]
        kw = {}
        if bias is not None:
            kw['bias'] = bias
            if not isinstance(bias, (int, float)):
                reads.append(bias)
        if not isinstance(scale, (int, float)):
            reads.append(scale)
        return self.S.op('act', lambda: nc.scalar.activation(out=out, in_=in_, func=func, scale=scale, **kw),
                         reads=reads, writes=[out])

    def _ve(self, e):
        return self.nc.vector if e == 'dve' else self.nc.gpsimd

    def tt(self, out, in0, in1, op, e='dve'):
        eng = self._ve(e)
        return self.S.op(e, lambda: eng.tensor_tensor(out=out, in0=in0, in1=in1, op=op),
                         reads=[in0, in1], writes=[out])

    def ts(self, out, in0, s1, op0, s2=None, op1=None, e='dve'):
        eng = self._ve(e)
        reads = [in0]
        if not isinstance(s1, (int, float)):
            reads.append(s1)
        if s2 is not None and not isinstance(s2, (int, float)):
            reads.append(s2)
        if op1 is None:
            f = lambda: eng.tensor_scalar(out=out, in0=in0, scalar1=s1, scalar2=None, op0=op0)
        else:
            f = lambda: eng.tensor_scalar(out=out, in0=in0, scalar1=s1, scalar2=s2, op0=op0, op1=op1)
        return self.S.op(e, f, reads=reads, writes=[out])

    def stt(self, out, in0, scalar, in1, op0, op1, e='dve'):
        eng = self._ve(e)
        reads = [in0, in1]
        if not isinstance(scalar, (int, float)):
            reads.append(scalar)
        return self.S.op(e, lambda: eng.scalar_tensor_tensor(out=out, in0=in0, scalar=scalar, in1=in1, op0=op0, op1=op1),
                         reads=reads, writes=[out])

    def cp(self, out, in_, e='dve'):
        if e == 'act':
            return self.act(out, in_, AF.Copy)
        eng = self._ve(e)
        return self.S.op(e, lambda: eng.tensor_copy(out=out, in_=in_), reads=[in_], writes=[out])

    def red(self, out, in_, op=ALU.add, e='dve'):
        eng = self._ve(e)
        return self.S.op(e, lambda: eng.tensor_reduce(out=out, in_=in_, axis=AX.X, op=op), reads=[in_], writes=[out])

    def recip(self, out, in_):
        nc = self.nc
        return self.S.op('dve', lambda: nc.vector.reciprocal(out=out, in_=in_), reads=[in_], writes=[out])

    def memset(self, ap, v, e='pool'):
        eng = self._ve(e)
        return self.S.op(e, lambda: eng.memset(ap, v), writes=[ap])

    def asel(self, out, in_, pattern, cmp, fill, base, cm):
        nc = self.nc
        return self.S.op('pool', lambda: nc.gpsimd.affine_select(out=out, in_=in_, pattern=pattern, compare_op=cmp,
                                                               fill=fill, base=base, channel_multiplier=cm),
                         reads=[in_], writes=[out])

    def dma(self, out, in_, q='sp', **kw):
        return self.S.dma(q, out, in_, **kw)

    def rsqrt(self, out, in_, scale, tmp=None):
        t = tmp if tmp is not None else out
        self.act(t, in_, AF.Ln, bias=self.eps_col[0:int(in_.shape[0]), :], scale=scale)
        self.act(out, t, AF.Exp, scale=-0.5)


def bc(ap, shape):
    return ap.unsqueeze(len(ap.shape)).broadcast_to(list(shape))


def pbc(dram_ap_1d, n, parts=128):
    return bass.AP(tensor=dram_ap_1d.tensor, offset=int(dram_ap_1d.offset), ap=[[0, parts], [1, n]])


class Rot:
    def __init__(self, bufs):
        self.bufs = bufs
        self.i = 0

    def next(self):
        t = self.bufs[self.i % len(self.bufs)]
        self.i += 1
        return t


def build(n_layers=DEPTH, stage=99):
    nc = bass.Bass("TRN2", target_bir_lowering=False)
    with contextlib.ExitStack() as st:
        b = B(nc, st, n_layers, stage)
        emit(b)
        b.S.barrier()
        print("built: inst", b.S.n_inst, "waits", b.S.n_wait)
    return nc


def emit(b):
    nc = b.nc
    L = DEPTH
    NL = b.n_layers
    d = {}
    d['xT'] = b.din('xT', [D, NTOK])
    d['cT'] = b.din('cT', [128, 8, 4])
    d['gconvT'] = b.din('gconvT', [L, 64, 12, 3])
    d['gstate'] = b.din('gstate', [L, 4, 64, 64])
    for n in ('sbk', 'sbv', 'fxk', 'fxv'):
        d[n] = b.din(n, [L, PAST, 256])
    d['fxlf'] = b.din('fxlf', [L, PAST, 4])
    d['bdk'] = b.din('bdk', [L, 512, 256])
    d['bdv'] = b.din('bdv', [L, 512, 256])
    d['fconvT'] = b.din('fconvT', [L, 128, 44, 2])
    d['ada_w'] = b.din('ada_w', [L, D, 6 * D])
    d['ada_bT'] = b.din('ada_bT', [128, L, 48])
    d['nmgT'] = b.din('nmgT', [128, L, 8])
    d['nfgT'] = b.din('nfgT', [128, L, 8])
    d['w_in'] = b.din('w_in', [L, D, INC])
    d['gcwT'] = b.din('gcwT', [64, L, 12, 4])
    d['a_log'] = b.din('a_log', [L, 4])
    d['dt_bias'] = b.din('dt_bias', [L, 4])
    d['gn_g'] = b.din('gn_g', [L, 64])
    d['fq_g'] = b.din('fq_g', [L, 64])
    d['fk_g'] = b.din('fk_g', [L, 64])
    d['fb_f'] = b.din('fb_f', [L, 4])
    d['bq_g'] = b.din('bq_g', [L, 64])
    d['bk_g'] = b.din('bk_g', [L, 64])
    d['rel'] = b.din('rel', [L, 4, 257])
    d['mg'] = b.din('mg', [L, 768])
    d['w_o'] = b.din('w_o', [L, D, D])
    d['w_up'] = b.din('w_up', [L, D, 2 * DFF])
    d['fcwT'] = b.din('fcwT', [128, L, 44, 3])
    d['w_down'] = b.din('w_down', [L, DFF, D])
    o = {}
    o['yT'] = b.dout('yT', [D, NTOK])
    o['p_gconv'] = b.dout('p_gconv', [L, 2, 3, 768])
    o['p_gstate'] = b.dout('p_gstate', [L, 2, 4, 64, 64])
    for n in ('p_sbk', 'p_sbv', 'p_fxk', 'p_fxv'):
        o[n] = b.dout(n, [L, 2, T, 256])
    o['p_fxlf'] = b.dout('p_fxlf', [L, 2, T, 4])
    o['p_bdk'] = b.dout('p_bdk', [L, 2, 512, 256])
    o['p_bdv'] = b.dout('p_bdv', [L, 2, 512, 256])
    o['p_fconv'] = b.dout('p_fconv', [L, 2, 2, 2 * DFF])
    o['s_gconv'] = b.dout('s_gconv', [L, 3, 768])
    o['s_gstate'] = b.dout('s_gstate', [L, 4, 64, 64])
    for n in ('s_sbk', 's_sbv', 's_fxk', 's_fxv'):
        o[n] = b.dout(n, [L, TS, 256])
    o['s_fxlf'] = b.dout('s_fxlf', [L, TS, 4])
    o['s_bdk'] = b.dout('s_bdk', [L, 512, 256])
    o['s_bdv'] = b.dout('s_bdv', [L, 512, 256])
    o['s_fconv'] = b.dout('s_fconv', [L, 2, 2 * DFF])
    b.d, b.o = d, o

    identb = b.sb('identb', [128, 128], BF16)
    identf = b.sb('identf', [128, 128], F32)
    onesb = b.sb('onesb', [128, 128], BF16)
    onesf = b.sb('onesf', [128, 128], F32)
    triUb = b.sb('triUb', [128, 128], BF16)
    triIncf = b.sb('triIncf', [128, 128], F32)
    sutf = b.sb('sutf', [64, 64], F32)
    antiJ = b.sb('antiJ', [128, 128], F32)
    blkb = b.sb('blkb', [128, 128], BF16)
    mask0 = b.sb('mask0', [128, 4, 128], BF16)
    mask4 = b.sb('mask4', [128, 4, 128], BF16)
    b.eps_col = b.sb('eps_col', [128, 1], F32)
    b.memset(b.eps_col[:], EPS)
    b.memset(identf[:], 0.0)
    b.asel(identf[:], identf[:], [[-1, 128]], ALU.not_equal, 1.0, 0, 1)
    b.cp(identb[:], identf[:])
    b.memset(onesb[:], 1.0)
    b.memset(onesf[:], 1.0)
    b.asel(triUb[:], onesb[:], [[-1, 128]], ALU.is_ge, 0.0, 0, 1)
    b.asel(triIncf[:], onesf[:], [[1, 128]], ALU.is_ge, 0.0, 0, -1)
    b.asel(sutf[:], onesf[0:64, 0:64], [[1, 64]], ALU.is_gt, 0.0, 0, -1)
    b.memset(antiJ[:], 0.0)
    b.asel(antiJ[:], antiJ[:], [[1, 128]], ALU.not_equal, 1.0, -127, 1)
    b.memset(blkb[:], 0.0)
    b.memset(blkb[0:64, 0:64], 1.0)
    b.memset(blkb[64:128, 64:128], 1.0)
    b.memset(mask0[:], 1.0)
    b.memset(mask0[0:64, :, 64:128], 0.0)
    b.memset(mask4[:], 1.0)
    b.memset(mask4[64:128, :, 0:64], 0.0)
    C = dict(identb=identb, identf=identf, onesb=onesb, onesf=onesf, triUb=triUb, triIncf=triIncf, sutf=sutf,
             antiJ=antiJ, blkb=blkb, mask0=mask0, mask4=mask4)
    b.C = C

    b.P = [b.ps('pb%d' % i, [128, 512], F32) for i in range(7)]
    b.PB = b.ps('pbb', [128, 1024], BF16)

    nmg = b.sb('nmg', [128, L, 8], F32)
    nfg = b.sb('nfg', [128, L, 8], F32)
    adab = b.sb('adab', [128, L, 48], F32)
    gcw = b.sb('gcw', [64, L, 12, 4], F32)
    fcw = b.sb('fcw', [128, L, 44, 3], F32)
    b.dma(nmg[:], d['nmgT'])
    b.dma(nfg[:], d['nfgT'])
    b.dma(adab[:], d['ada_bT'])
    b.dma(gcw[:], d['gcwT'])
    b.dma(fcw[:], d['fcwT'])
    b.gcw, b.fcw = gcw, fcw

    mod = b.sb('mod', [128, L, 48, 4], F32)
    modA = b.sb('modA', [128, L, 2, 8, 4], F32)
    b.mod, b.modA = mod, modA
    with contextlib.ExitStack() as s1:
        cT = b.sb('cT', [128, 8, 4], F32, s1)
        sc = b.sb('sc', [128, 8, 4], F32, s1)
        tmp = b.sb('sctmp', [128, 8, 4], F32, s1)
        b.dma(cT[:], d['cT'])
        b.act(tmp[:], cT[:], AF.Exp, scale=-1.0)
        b.ts(tmp[:], tmp[:], 1.0, ALU.add)
        b.recip(tmp[:], tmp[:])
        b.tt(sc[:], cT[:], tmp[:], ALU.mult)
        awr = Rot([b.sb('aw%d' % i, [128, 8, 768], F32, s1) for i in range(2)])
        for l in range(NL):
            pm = b.P[l % 2]
            for pc in range(8):
                aw = awr.next()
                b.dma(aw[:], d['ada_w'][l, :, pc * 768:(pc + 1) * 768].rearrange("(c p) n -> p c n", p=128))
                for j in range(6):
                    oc = pc * 6 + j
                    for c in range(8):
                        b.mm(pm[:, oc * 4:oc * 4 + 4], aw[:, c, j * 128:(j + 1) * 128], sc[:, c, :],
                             start=(c == 0), stop=(c == 7))
            pmv = pm[:, 0:192].rearrange("p (a s) -> p a s", s=4)
            b.tt(mod[:, l, :, :], pmv, bc(adab[:, l, :], [128, 48, 4]), ALU.add)
            for w, g in ((0, nmg), (1, nfg)):
                scv = mod[:, l, (1 + 3 * w) * 8:(2 + 3 * w) * 8, :]
                b.ts(modA[:, l, w, :, :], scv, 1.0, ALU.add)
                b.tt(modA[:, l, w, :, :], modA[:, l, w, :, :], bc(g[:, l, :], [128, 8, 4]), ALU.mult)
    b.S.barrier()

    for l in range(NL):
        layer(b, l)


def rmsnorm_tok(b, out, src, n, np_, gains, wk, den=None):
    sq, ss, y = wk
    x = src
    if den is not None:
        b.recip(ss[0:np_, 0:n], den)
        b.tt(y[0:np_, 0:n, :], src, bc(ss[0:np_, 0:n], [np_, n, 64]), ALU.mult)
        x = y[0:np_, 0:n, :]
    b.act(sq[0:np_, 0:n, :], x, AF.Square)
    b.red(ss[0:np_, 0:n], sq[0:np_, 0:n, :])
    b.rsqrt(ss[0:np_, 0:n], ss[0:np_, 0:n], 1.0 / 64.0)
    if gains is None:
        b.tt(out, x, bc(ss[0:np_, 0:n], [np_, n, 64]), ALU.mult)
    else:
        b.tt(y[0:np_, 0:n, :], x, bc(ss[0:np_, 0:n], [np_, n, 64]), ALU.mult)
        b.tt(out, y[0:np_, 0:n, :], gains[0:np_, 0:n, :], ALU.mult)


def layer(b, l):
    nc = b.nc
    d, o, C, P = b.d, b.o, b.C, b.P
    src_x = d['xT'] if l == 0 else o['yT']
    with contextlib.ExitStack() as s:
        win = b.sb('win', [128, 8, INC], BF16, s)
        wo = b.sb('wo', [128, 8, D], BF16, s)
        for c in range(8):
            for hh in range(2):
                b.dma(win[:, c, hh * 1670:(hh + 1) * 1670], d['w_in'][l, c * 128:(c + 1) * 128, hh * 1670:(hh + 1) * 1670], q='pool')
        for c in range(8):
            b.dma(wo[:, c, :], d['w_o'][l, c * 128:(c + 1) * 128, :], q='pool')
        gC = b.sb('gC', [128, 8, 64], F32, s)
        gD = b.sb('gD', [128, 8, 64], F32, s)
        gG = b.sb('gG', [128, 4, 64], F32, s)
        mg = b.sb('mg', [128, 12, 64], F32, s)
        dtb = b.sb('dtb', [128, 4], F32, s)
        nexpA = b.sb('nexpA', [128, 4], F32, s)
        fbf = b.sb('fbf', [128, 4], F32, s)

        def bcl(t, n, rep):
            return bass.AP(tensor=t.tensor, offset=int(t.offset), ap=[[0, 128], [0, rep], [1, n]])
        b.dma(gC[:, 0:4, :], bcl(d['fq_g'][l], 64, 4))
        b.dma(gC[:, 4:8, :], bcl(d['fk_g'][l], 64, 4))
        b.dma(gD[:, 0:4, :], bcl(d['bq_g'][l], 64, 4))
        b.dma(gD[:, 4:8, :], bcl(d['bk_g'][l], 64, 4))
        b.dma(gG[:], bcl(d['gn_g'][l], 64, 4))
        b.dma(mg[:].rearrange("p a b -> p (a b)"), pbc(d['mg'][l], 768))
        b.dma(dtb[:], pbc(d['dt_bias'][l], 4))
        b.dma(nexpA[:], pbc(d['a_log'][l], 4))
        b.dma(fbf[:], pbc(d['fb_f'][l], 4))
        b.act(nexpA[:], nexpA[:], AF.Exp)
        b.ts(nexpA[:], nexpA[:], -1.0, ALU.mult)
        relx = nc.dram_tensor('relx%d' % l, [4, 520], F32, kind="Internal").ap()
        with contextlib.ExitStack() as s2:
            rl = b.sb('rl', [4, 257], F32, s2)
            rx = b.sb('rx', [4, 520], F32, s2)
            b.dma(rl[:], d['rel'][l])
            b.memset(rx[:], 0.0)
            b.cp(rx[:, 128:385], rl[:])
            b.cp(rx[:, 0:128], rl[:, 0:1].broadcast_to([4, 128]))
            b.cp(rx[:, 385:513], rl[:, 256:257].broadcast_to([4, 128]))
            b.dma(relx, rx[:])
        Bt = b.sb('Bt', [128, 3, 4, 128], F32, s)
        with contextlib.ExitStack() as s2:
            tz = b.sb('tz', [128, 4, 128], F32, s2)
            for kind, delta in ((0, -384), (1, -128), (2, 0)):
                for h in range(4):
                    if delta <= -384:
                        src = bass.AP(tensor=relx.tensor, offset=int(relx[h, 385:386].offset), ap=[[0, 128], [1, 128]])
                    else:
                        src = bass.AP(tensor=relx.tensor, offset=int(relx[h, 0:1].offset) + (129 - delta), ap=[[1, 128], [1, 128]])
                    b.dma(tz[:, h, :], src)
                pz = P[0]
                b.mm(pz[:, 0:512], C['antiJ'][:], tz[:].rearrange("p a b -> p (a b)"))
                b.cp(Bt[:, kind, :, :].rearrange("p a b -> p (a b)"), pz[:, 0:512])
        KT = [b.sb('KT%d' % i, [128, 2, T if i < 2 else 1024], BF16, s) for i in range(3)]
        V = [b.sb('V%d' % i, [128, 16 if i < 2 else 8, 4, 65], BF16, s) for i in range(3)]
        for i in range(3):
            b.memset(V[i][:], 1.0)
        W = dict(win=win, wo=wo, gC=gC, gD=gD, gG=gG, mg=mg, dtb=dtb, nexpA=nexpA, fbf=fbf, Bt=Bt, KT=KT, V=V)
        alloc_work(b, W, s)
        import os
        for si in [int(c) for c in os.environ.get('KDBG_SEQS', '012')]:
            phaseAB(b, l, si, W, src_x)
    b.S.barrier()
    phaseC(b, l)
    b.S.barrier()


GP = 256
SKIP_D = False
OFFS = [1032, 1800, 2572]


def mid_bc(ap2d, n):
    a = [list(x) for x in ap2d.ap]
    return bass.AP(tensor=ap2d.tensor, offset=int(ap2d.offset), ap=[a[0], [0, n], a[1]])


def alloc_work(b, W, s):
    G = GP
    W['xg'] = b.sb('xg', [128, 8, G], F32, s)
    W['sqr'] = Rot([b.sb('sqr%d' % i, [128, G], BF16, s) for i in range(2)])
    W['rstd'] = b.sb('rstd', [128, G], F32, s)
    W['hT'] = b.sb('hT', [128, 8, G], BF16, s)
    W['QT'] = [b.sb('QT%d' % i, [128, 2, G], BF16, s) for i in range(3)]
    W['stg'] = Rot([b.sb('stg%d' % i, [128, 512], F32, s) for i in range(3)])
    W['qkb'] = Rot([b.sb('qkb%d' % i, [128, 512], BF16, s) for i in range(2)])
    W['nwk'] = (b.sb('nsq', [128, 8, 64], F32, s), b.sb('nss', [128, 8], F32, s), b.sb('ny', [128, 8, 64], F32, s))
    W['lf'] = b.sb('lf', [128, 4], F32, s)
    W['lfc'] = b.sb('lfc', [128, 8, 4], F32, s)
    W['negF'] = b.sb('negF', [128, 17, 4], F32, s)
    W['carry'] = b.sb('carry', [128, 4], F32, s)
    W['fbias'] = b.sb('fbias', [128, 17, 4], F32, s)
    W['mixed'] = b.sb('mixed', [128, G // 128, 1024], BF16, s)
    W['mixedT'] = b.sb('mixedT', [128, 8, G], BF16, s)
    X = [b.sb('X%d' % i, [128, 512], F32, s) for i in range(6)]
    W['X'] = X
    W['E'] = Rot([X[0], X[1]])
    W['zr'] = Rot([X[2], X[3]])
    W['Rr'] = [X[4], X[5]]
    W['htmp'] = Rot([X[1][:, 0:256], X[1][:, 256:512]])
    W['bt'] = Rot([X[4]])
    W['SP'] = Rot([b.sb('SP%d' % i, [128, 512], BF16, s) for i in range(2)])
    W['Wt'] = Rot([b.sb('Wt%d' % i, [128, 512], BF16, s) for i in range(3)])
    alloc_gdn(b, W, s)
    print('sbuf remaining after alloc', b.nc.sbuf_bytes_remaining)


def norm_mod(b, l, si, which, xg, G, W, hT):
    C, P = b.C, b.P
    ss = P[0][:, 0:G]
    for c in range(8):
        sq = W['sqr'].next()
        b.act(sq[:, 0:G], xg[:, c, 0:G], AF.Square)
        b.mm(ss, C['onesb'][:], sq[:, 0:G], start=(c == 0), stop=(c == 7))
    b.rsqrt(W['rstd'][:, 0:G], ss, 1.0 / D)
    for c in range(8):
        ht = W['htmp'].next()
        b.stt(ht[:, 0:G], xg[:, c, 0:G], b.modA[:, l, which, c, si:si + 1], W['rstd'][:, 0:G], ALU.mult, ALU.mult)
        b.act(hT[:, c, 0:G], ht[:, 0:G], AF.Identity, bias=b.mod[:, l, which * 24 + c, si:si + 1])


def phaseAB(b, l, si, W, src_x):
    nc = b.nc
    d, o, C, P, PB = b.d, b.o, b.C, b.P, b.PB
    prompt = si < 2
    Tq = T if prompt else TS
    tok0 = si * T if prompt else 2 * T
    G = GP if prompt else TS
    ngroups = Tq // G
    npast = 0 if prompt else PAST
    npastD = 0 if prompt else 512
    KT, V, win, wo, mg = W['KT'], W['V'], W['win'], W['wo'], W['mg']
    xg, hT, QT, mixed, mixedT = W['xg'], W['hT'], W['QT'], W['mixed'], W['mixedT']
    identb = C['identb']
    pbv = PB[:, 0:1024].rearrange("p (j t) -> p j t", t=128)
    pf = P[3]
    carry, negF, lf = W['carry'], W['negF'], W['lf']
    b.memset(carry[:], 0.0)
    b.memset(negF[:], 0.0)
    if not prompt:
        for i, (kn, vn, nrow) in enumerate((('sbk', 'sbv', 1024), ('fxk', 'fxv', 1024), ('bdk', 'bdv', 512))):
            for tl in range(nrow // 128):
                b.dma(V[i][:, tl, :, 0:64], d[vn][l, tl * 128:(tl + 1) * 128, :].rearrange("p (h e) -> p h e", e=64), q='pool')
                kb = W['qkb'].next()
                b.dma(kb[:, 0:256], d[kn][l, tl * 128:(tl + 1) * 128, :], q='pool')
                for j in range(2):
                    b.tr(pbv[:, j, :], kb[:, j * 128:(j + 1) * 128], identb[:])
                b.cp(KT[i][:, :, tl * 128:(tl + 1) * 128], pbv[:, 0:2, :])
        lfc = W['lfc']
        b.dma(lfc[:], d['fxlf'][l].rearrange("(t p) h -> p t h", p=128))
        for tl in range(8):
            b.mm(pf[:, 0:4], C['triIncf'][:], lfc[:, tl, :])
            b.stt(negF[:, tl, :], pf[:, 0:4], -1.0, carry[:], ALU.mult, ALU.subtract)
            b.mm(pf[:, 4:8], C['onesf'][:], lfc[:, tl, :])
            b.tt(carry[:], carry[:], pf[:, 4:8], ALU.add)
        b.dma(o['s_bdk'][l, 0:496, :], d['bdk'][l, 16:512, :])
        b.dma(o['s_bdv'][l, 0:496, :], d['bdv'][l, 16:512, :])
    gdn_seq_init(b, l, si, W)

    import os
    for g in range(min(ngroups, int(os.environ.get('KDBG_MAXG', '99')))):
        t0 = g * G
        b.dma(xg[:, :, 0:G], src_x[:, tok0 + t0:tok0 + t0 + G].rearrange("(c p) t -> p c t", p=128))
        norm_mod(b, l, si, 0, xg, G, W, hT)
        ntile = max(1, G // 128)
        nt = min(128, G)
        for tt in range(ntile):
            cols = slice(tt * 128, tt * 128 + nt)
            for i in range(3):
                c0 = OFFS[i]
                pqk, pv = P[1], P[2]
                nv = 260 if i == 1 else 256
                for c in range(8):
                    b.mm(pqk[0:nt, 0:512], hT[:, c, cols], win[:, c, c0:c0 + 512], start=(c == 0), stop=(c == 7))
                for c in range(8):
                    b.mm(pv[0:nt, 0:nv], hT[:, c, cols], win[:, c, c0 + 512:c0 + 512 + nv], start=(c == 0), stop=(c == 7))
                kbase = (npast if i < 2 else npastD) + t0 + tt * 128
                kt_i = kbase // 128
                if i == 2:
                    kbase = kbase % 1024
                    kt_i = kt_i % 8
                sv = W['stg'].next()
                b.cp(sv[0:nt, 0:256], pv[0:nt, 0:256], e='act')
                b.cp(V[i][0:nt, kt_i, :, 0:64], sv[0:nt, 0:256].rearrange("p (h e) -> p h e", e=64))
                sk = W['stg'].next()
                if i == 0:
                    b.cp(sk[0:nt, 0:512], pqk[0:nt, 0:512])
                else:
                    rmsnorm_tok(b, sk[0:nt, 0:512].rearrange("p (h e) -> p h e", e=64),
                                pqk[0:nt, 0:512].rearrange("p (h e) -> p h e", e=64), 8, nt,
                                W['gC'] if i == 1 else W['gD'], W['nwk'])
                tloc = t0 + tt * 128
                if prompt:
                    if i == 0:
                        b.dma(o['p_sbk'][l, si, tloc:tloc + nt, :], sk[0:nt, 256:512])
                        b.dma(o['p_sbv'][l, si, tloc:tloc + nt, :], sv[0:nt, 0:256])
                    elif i == 1:
                        b.dma(o['p_fxk'][l, si, tloc:tloc + nt, :], sk[0:nt, 256:512])
                        b.dma(o['p_fxv'][l, si, tloc:tloc + nt, :], sv[0:nt, 0:256])
                    elif tloc >= T - 512:
                        b.dma(o['p_bdk'][l, si, tloc - (T - 512):tloc - (T - 512) + nt, :], sk[0:nt, 256:512])
                        b.dma(o['p_bdv'][l, si, tloc - (T - 512):tloc - (T - 512) + nt, :], sv[0:nt, 0:256])
                else:
                    if i == 0:
                        b.dma(o['s_sbk'][l, 0:nt, :], sk[0:nt, 256:512])
                        b.dma(o['s_sbv'][l, 0:nt, :], sv[0:nt, 0:256])
                    elif i == 1:
                        b.dma(o['s_fxk'][l, 0:nt, :], sk[0:nt, 256:512])
                        b.dma(o['s_fxv'][l, 0:nt, :], sv[0:nt, 0:256])
                    else:
                        b.dma(o['s_bdk'][l, 496:512, :], sk[0:nt, 256:512])
                        b.dma(o['s_bdv'][l, 496:512, :], sv[0:nt, 0:256])
                qb = W['qkb'].next()
                b.cp(qb[0:nt, :], sk[0:nt, 0:512], e='act')
                for j in range(4):
                    b.tr(pbv[:, j, 0:nt], qb[0:nt, j * 128:(j + 1) * 128], identb[0:nt, 0:nt])
                b.cp(QT[i][:, :, tt * 128:tt * 128 + nt], pbv[:, 0:2, 0:nt])
                b.cp(KT[i][:, :, kbase:kbase + nt], pbv[:, 2:4, 0:nt], e='act')
                if i == 1:
                    b.tt(lf[0:nt, :], pv[0:nt, 256:260], W['fbf'][0:nt, :], ALU.add)
                    b.act(lf[0:nt, :], lf[0:nt, :], AF.Exp, scale=-1.0)
                    b.act(lf[0:nt, :], lf[0:nt, :], AF.Ln, bias=1.0)
                    b.ts(lf[0:nt, :], lf[0:nt, :], -1.0, ALU.mult)
                    if prompt:
                        b.dma(o['p_fxlf'][l, si, tloc:tloc + nt, :], lf[0:nt, :])
                    else:
                        b.dma(o['s_fxlf'][l, 0:nt, :], lf[0:nt, :])
                    b.mm(pf[0:nt, 0:4], C['triIncf'][0:nt, 0:nt], lf[0:nt, :])
                    b.stt(negF[0:nt, kt_i, :], pf[0:nt, 0:4], -1.0, carry[0:nt, :], ALU.mult, ALU.subtract)
                    b.mm(pf[:, 4:8], C['onesf'][0:nt, :], lf[0:nt, :])
                    b.tt(carry[:], carry[:], pf[:, 4:8], ALU.add)
        if b.stage < 2:
            continue
        nsb = ntile
        nq = nt
        qpos0 = npast + t0
        jmax = (qpos0 + G - 1) // 128
        Ops = P[4]
        zc = 0
        GW = 2 * G
        if b.stage >= 2 and not os.environ.get('KDBG_NOB'):
            for p in range(2):
                b.memset(W['Rr'][p][:, 0:GW], 0.0)
            firstB = True
            for j in range(jmax, -1, -1):
                nk = min(128, npast + Tq - j * 128)
                m = j * 128 - qpos0
                diag = m >= 0
                for p in range(2):
                    po = p * 64
                    Rr = W['Rr'][p]
                    pz = P[5 + zc % 2]
                    zc += 1
                    for hl in range(2):
                        b.mm(pz[0:nk, hl * G:(hl + 1) * G], KT[0][po:po + 64, hl, j * 128:j * 128 + nk], QT[0][po:po + 64, hl, 0:G])
                    E = W['E'].next()
                    b.act(E[0:nk, 0:GW], pz[0:nk, 0:GW], AF.Exp, scale=0.125)
                    sp = W['SP'].next()
                    b.act(sp[0:nk, 0:GW], E[0:nk, 0:GW], AF.Ln, bias=1.0)
                    if diag:
                        spv = sp[0:nk, 0:GW].rearrange("p (a q) -> p a q", a=2)
                        b.asel(spv, spv, [[0, 2], [1, G]], ALU.is_gt, 0.0, -m, -1)
                    b.mm(P[3][0:nk, 0:GW], C['triUb'][0:nk, 0:nk], sp[0:nk, 0:GW])
                    if j > 0:
                        b.mm(P[2][:, 0:GW], C['onesb'][0:nk, :], sp[0:nk, 0:GW])
                    zr = W['zr'].next()
                    b.act(zr[0:nk, 0:GW], pz[0:nk, 0:GW], AF.Identity, scale=0.125)
                    b.tt(zr[0:nk, 0:GW], zr[0:nk, 0:GW], Rr[0:nk, 0:GW], ALU.subtract)
                    b.tt(zr[0:nk, 0:GW], zr[0:nk, 0:GW], P[3][0:nk, 0:GW], ALU.subtract)
                    wt = W['Wt'].next()
                    b.act(wt[0:nk, 0:GW], zr[0:nk, 0:GW], AF.Exp)
                    if diag:
                        wtv = wt[0:nk, 0:GW].rearrange("p (a q) -> p a q", a=2)
                        b.asel(wtv, wtv, [[0, 2], [1, G]], ALU.is_gt, 0.0, -m, -1)
                    if j > 0:
                        b.tt(Rr[:, 0:GW], Rr[:, 0:GW], P[2][:, 0:GW], ALU.add)
                    for hl in range(2):
                        h = p + 2 * hl
                        for sb_ in range(nsb):
                            if diag and m >= sb_ * 128 + nq:
                                continue
                            oc0 = ((p * 2 + hl) * nsb + sb_) * 64
                            b.mm(Ops[0:nq, oc0:oc0 + 64], wt[0:nk, hl * G + sb_ * 128:hl * G + sb_ * 128 + nq],
                                 V[0][0:nk, j, h, 0:64], start=firstB, stop=(j == 0), skip_group_check=True)
                            firstB = False
            for p in range(2):
                for hl in range(2):
                    h = p + 2 * hl
                    oc0 = (p * 2 + hl) * nsb * 64
                    rmsnorm_tok(b, mixed[0:nq, 0:nsb, 256 + h * 64:256 + (h + 1) * 64],
                                Ops[0:nq, oc0:oc0 + nsb * 64].rearrange("p (s e) -> p s e", e=64), nsb, nq,
                                mid_bc(mg[:, h, :], nsb), W['nwk'])
        nj = jmax + 1
        b.tt(W['fbias'][:, 0:nj, :], negF[:, 0:nj, :], mid_bc(carry[:, :], nj), ALU.add)
        FQ = W['X'][0]
        for h in range(4 if b.stage >= 3 else 0):
            hp, po = h // 2, (h % 2) * 64
            first = True
            pfq = P[3]
            dg = W['nwk'][0][:, 0:2, :].rearrange("p a e -> p (a e)")
            for sb_ in range(nsb):
                jq_ = qpos0 // 128 + sb_
                b.ts(dg[0:nq, 0:nq], C['identf'][0:nq, 0:nq], W['fbias'][0:nq, jq_, h:h + 1], ALU.mult)
                b.mm(pfq[:, sb_ * 128:sb_ * 128 + nq], C['onesf'][0:nq, :], dg[0:nq, 0:nq])
            b.cp(FQ[:, 0:G], pfq[:, 0:G])
            for j in range(0, jmax + 1):
                nk = min(128, npast + Tq - j * 128)
                m = j * 128 - qpos0
                diag = m >= 0
                pz = P[5 + zc % 2]
                zc += 1
                b.mm(pz[0:nk, 0:G], KT[1][po:po + 64, hp, j * 128:j * 128 + nk], QT[1][po:po + 64, hp, 0:G])
                wt = W['Wt'].next()
                s_ = W['zr'].next()
                b.stt(s_[0:nk, 0:G], pz[0:nk, 0:G], 0.125, FQ[0:nk, 0:G], ALU.mult, ALU.subtract)
                if diag:
                    b.ts(s_[0:nk, 0:G], s_[0:nk, 0:G], W['fbias'][0:nk, j, h:h + 1], ALU.add, 30.0, ALU.min)
                    b.act(wt[0:nk, 0:G], s_[0:nk, 0:G], AF.Exp)
                    b.asel(wt[0:nk, 0:G], wt[0:nk, 0:G], [[1, G]], ALU.is_ge, 0.0, -m, -1)
                else:
                    b.act(wt[0:nk, 0:G], s_[0:nk, 0:G], AF.Exp, bias=W['fbias'][0:nk, j, h:h + 1])
                for sb_ in range(nsb):
                    if diag and m >= sb_ * 128 + nq:
                        continue
                    b.mm(Ops[0:nq, sb_ * 65:(sb_ + 1) * 65], wt[0:nk, sb_ * 128:sb_ * 128 + nq], V[1][0:nk, j, h, 0:65],
                         start=first, stop=(j == jmax), skip_group_check=True)
                    first = False
            ov = Ops[0:nq, 0:nsb * 65].rearrange("p (s e) -> p s e", e=65)
            rmsnorm_tok(b, mixed[0:nq, 0:nsb, 512 + h * 64:512 + (h + 1) * 64], ov[:, :, 0:64], nsb, nq,
                        mid_bc(mg[:, 4 + h, :], nsb), W['nwk'], den=ov[:, :, 64])
        if SKIP_D and b.stage >= 5:
            b.memset(mixed[:, :, 768:1024], 0.0)
        for tq in range(ntile if (b.stage >= 4 and not SKIP_D) else 0):
            qk0 = npastD + t0 + tq * 128
            jq = qk0 // 128
            tiles = list(range(max(0, jq - 4), jq + 1))
            first = True
            for j in tiles:
                nk = min(128, npastD + Tq - j * 128)
                kind = 2 if j == jq else (1 if j == jq - 1 else 0)
                bt = W['bt'].next()
                btv = bt[0:nk, 0:512].rearrange("p (h q) -> p h q", q=128)[:, :, 0:nq]
                for h in range(4):
                    hp, po = h // 2, (h % 2) * 64
                    b.mm(P[5 + h % 2][0:nk, hp * 128:hp * 128 + nq], KT[2][po:po + 64, hp, (j % 8) * 128:(j % 8) * 128 + nk],
                         QT[2][po:po + 64, hp, tq * 128:tq * 128 + nq])
                for par in range(2):
                    src = P[5 + par][0:nk, 0:256].rearrange("p (a q) -> p a q", q=128)[:, :, 0:nq]
                    dst = bt[0:nk, 0:512].rearrange("p (a c q) -> p a c q", a=2, c=2, q=128)[:, :, par, 0:nq]
                    b.act(dst, src, AF.Identity, scale=0.125)
                b.tt(btv, btv, W['Bt'][0:nk, kind, :, 0:nq], ALU.add)
                wt = W['Wt'].next()
                wtv = wt[0:nk, 0:512].rearrange("p (h q) -> p h q", q=128)[:, :, 0:nq]
                b.act(wtv, btv, AF.Exp)
                if prompt and j == jq:
                    b.tt(wtv, wtv, C['mask4'][0:nk, :, 0:nq], ALU.mult)
                if prompt and j == jq - 4:
                    b.tt(wtv, wtv, C['mask0'][0:nk, :, 0:nq], ALU.mult)
                for h in range(4):
                    b.mm(Ops[0:nq, h * 65:(h + 1) * 65], wt[0:nk, h * 128:h * 128 + nq], V[2][0:nk, j % 8, h, 0:65],
                         start=first, stop=(j == tiles[-1]), skip_group_check=True)
                    first = False
            ov = Ops[0:nq, 0:260].rearrange("p (s e) -> p s e", e=65)
            rmsnorm_tok(b, mixed[0:nq, tq, 768:1024].rearrange("p (h e) -> p h e", e=64), ov[:, :, 0:64], 4, nq,
                        mg[:, 8:12, :], W['nwk'], den=ov[:, :, 64])
        gdn_group(b, l, si, W, g, G, t0)
        if b.stage < 5:
            continue
        for tt in range(ntile):
            for cc in range(2, 8):
                b.tr(pbv[:, cc, 0:nt], mixed[0:nt, tt, cc * 128:(cc + 1) * 128], identb[0:nt, 0:nt])
            b.cp(mixedT[:, 2:8, tt * 128:tt * 128 + nt], pbv[:, 2:8, 0:nt])
        for oc in range(8):
            pso = P[oc % 2]
            for c in range(8):
                b.mm(pso[:, 0:G], wo[:, c, oc * 128:(oc + 1) * 128], mixedT[:, c, 0:G], start=(c == 0), stop=(c == 7))
            rt = W['htmp'].next()
            b.act(rt[:, 0:G], pso[:, 0:G], AF.Identity, scale=b.mod[:, l, 16 + oc, si:si + 1])
            b.tt(xg[:, oc, 0:G], xg[:, oc, 0:G], rt[:, 0:G], ALU.add)
        b.dma(o['yT'][:, tok0 + t0:tok0 + t0 + G].rearrange("(c p) t -> p c t", p=128), xg[:, :, 0:G])


def phaseC(b, l):
    nc = b.nc
    d, o, C, P = b.d, b.o, b.C, b.P
    if b.stage < 6:
        return
    with contextlib.ExitStack() as s:
        wup = b.sb('wup', [128, 8, 2 * DFF], BF16, s)
        wdn = b.sb('wdn', [128, 22, D], BF16, s)
        for c in range(8):
            for q4 in range(4):
                b.dma(wup[:, c, q4 * 1408:(q4 + 1) * 1408], d['w_up'][l, c * 128:(c + 1) * 128, q4 * 1408:(q4 + 1) * 1408], q='pool')
        for j in range(22):
            b.dma(wdn[:, j, :], d['w_down'][l, j * 128:(j + 1) * 128, :], q='pool')
        GC = 256
        W = {}
        xg = b.sb('cxg', [128, 8, GC], F32, s)
        W['sqr'] = Rot([b.sb('csqr%d' % i, [128, GC], BF16, s) for i in range(2)])
        W['rstd'] = b.sb('crstd', [128, GC], F32, s)
        W['htmp'] = Rot([b.sb('chtmp%d' % i, [128, GC], F32, s) for i in range(2)])
        h2T = b.sb('h2T', [128, 8, GC], BF16, s)
        ur = Rot([b.sb('ur%d' % i, [128, GC + 2], F32, s) for i in range(4)])
        ctr = Rot([b.sb('ctr%d' % i, [128, GC], F32, s) for i in range(4)])
        hist = b.sb('hist', [128, 44, 2], F32, s)
        actT = b.sb('actT', [128, 22, GC], BF16, s)
        fo = b.sb('fo', [2, 2 * DFF], F32, s)
        fcw = b.fcw
        pc_ = 0
        for si in range(3):
            prompt = si < 2
            Tq = T if prompt else TS
            tok0 = si * T if prompt else 2 * T
            G = GC if prompt else TS
            ngroups = Tq // G
            if prompt:
                b.memset(hist[:], 0.0)
            else:
                b.dma(hist[:], d['fconvT'][l])
            for g in range(ngroups):
                t0 = g * G
                ycols = o['yT'][:, tok0 + t0:tok0 + t0 + G].rearrange("(c p) t -> p c t", p=128)
                b.dma(xg[:, :, 0:G], ycols)
                norm_mod(b, l, si, 1, xg, G, W, h2T)
                for j in range(22):
                    cv = []
                    for half in range(2):
                        jj = half * 22 + j
                        colb = half * DFF + j * 128
                        pu = P[1 + pc_ % 4]
                        pc_ += 1
                        for c in range(8):
                            b.mm(pu[:, 0:G], wup[:, c, colb:colb + 128], h2T[:, c, 0:G], start=(c == 0), stop=(c == 7))
                        u = ur.next()
                        b.cp(u[:, 0:2], hist[:, jj, :], e='act')
                        b.cp(u[:, 2:2 + G], pu[:, 0:G], e='act')
                        b.cp(hist[:, jj, :], u[:, G:G + 2], e='act')
                        ct = ctr.next()
                        b.ts(ct[:, 0:G], u[:, 0:G], fcw[:, l, jj, 0:1], ALU.mult)
                        b.stt(ct[:, 0:G], u[:, 1:1 + G], fcw[:, l, jj, 1:2], ct[:, 0:G], ALU.mult, ALU.add)
                        b.stt(ct[:, 0:G], u[:, 2:2 + G], fcw[:, l, jj, 2:3], ct[:, 0:G], ALU.mult, ALU.add)
                        cv.append(ct)
                    sil = ctr.next()
                    b.act(sil[:, 0:G], cv[0][:, 0:G], AF.Silu)
                    b.tt(actT[:, j, 0:G], sil[:, 0:G], cv[1][:, 0:G], ALU.mult)
                for oc in range(8):
                    pd = P[5 + oc % 2]
                    for j in range(22):
                        b.mm(pd[:, 0:G], wdn[:, j, oc * 128:(oc + 1) * 128], actT[:, j, 0:G], start=(j == 0), stop=(j == 21))
                    rt = W['htmp'].next()
                    b.act(rt[:, 0:G], pd[:, 0:G], AF.Identity, scale=b.mod[:, l, 40 + oc, si:si + 1])
                    b.tt(xg[:, oc, 0:G], xg[:, oc, 0:G], rt[:, 0:G], ALU.add)
                b.dma(ycols, xg[:, :, 0:G])
                if g == ngroups - 1:
                    for q11 in range(11):
                        pcx = P[0]
                        for c in range(8):
                            b.mm(pcx[0:2, 0:512], h2T[:, c, G - 2:G], wup[:, c, q11 * 512:(q11 + 1) * 512], start=(c == 0), stop=(c == 7))
                        b.cp(fo[0:2, q11 * 512:(q11 + 1) * 512], pcx[0:2, 0:512])
                    if prompt:
                        b.dma(o['p_fconv'][l, si, :, :], fo[:])
                    else:
                        b.dma(o['s_fconv'][l, :, :], fo[:])


def alloc_gdn(b, W, s):
    G = GP
    W['ghist'] = b.sb('ghist', [64, 12, 3], F32, s)
    W['pst'] = Rot([b.sb('pst%d' % i, [64, G + 3], F32, s) for i in range(2)])
    W['cv'] = Rot([b.sb('cv%d' % i, [64, G], F32, s) for i in range(2)])
    W['gsq'] = b.sb('gsq', [64, G], BF16, s)
    W['grn'] = b.sb('grn', [64, G], F32, s)
    W['qkvn'] = b.sb('qkvn', [64, 12, G], BF16, s)
    W['Sf'] = b.sb('Sf', [64, 4, 64], F32, s)
    W['Sb'] = b.sb('Sb', [64, 4, 64], BF16, s)
    W['g3'] = b.sb('g3', [3, 768], F32, s)
    v64 = lambda t, c0: t[0:64, c0:c0 + 256].rearrange("p (h e) -> p h e", e=64)
    hosts32 = [(W['X'][i], c0) for i in range(4) for c0 in (0, 256)]
    f32n = ['zsil', 'dec', 'decI', 'decS', 'Uf', 'gb', 'vtok', 'of']
    for n, (t, c0) in zip(f32n, hosts32):
        W[n] = v64(t, c0)
    W['on'] = b.sb('on', [64, 4, 64], F32, s)[:]
    hosts16 = [(W['Wt'].bufs[i], c0) for i in range(3) for c0 in (0, 256)] + [(W['SP'].bufs[i], 0) for i in range(2)] + \
              [(W['qkb'].bufs[i], c0) for i in range(2) for c0 in (0, 256)]
    for n, (t, c0) in zip(['ktok', 'kd', 'vn', 'AT', 'om'], hosts16):
        W[n] = v64(t, c0)
    hosts32b = [(W['stg'].bufs[i], c0) for i in range(3) for c0 in (0, 256)]
    for n, (t, c0) in zip(['Pa', 'Qa', 'Qb', 'Ya', 'Yb'], hosts32b):
        W[n] = v64(t, c0)
    W['gsm'] = b.sb('gsm', [64, 10, 4], F32, s)


def gdn_seq_init(b, l, si, W):
    d = b.d
    if si < 2:
        b.memset(W['ghist'][:], 0.0)
        b.memset(W['Sf'][:], 0.0)
        b.memset(W['Sb'][:], 0.0)
    else:
        b.dma(W['ghist'][:], d['gconvT'][l])
        b.dma(W['Sf'][:], d['gstate'][l].rearrange("h k v -> k h v"))
        b.cp(W['Sb'][:], W['Sf'][:])


def gdn_group(b, l, si, W, g, G, t0):
    import os
    if os.environ.get('KDBG_NOGDN'):
        b.memset(W['mixedT'][:, 0:2, 0:G], 0.0)
        return
    nc = b.nc
    d, o, C, P, PB = b.d, b.o, b.C, b.P, b.PB
    prompt = si < 2
    Tq = T if prompt else TS
    win, hT = W['win'], W['hT']
    qkvn = W['qkvn']
    gcw = b.gcw
    for hc in range(12):
        pg = P[hc % 2]
        for c in range(8):
            b.mm(pg[0:64, 0:G], win[:, c, hc * 64:(hc + 1) * 64], hT[:, c, 0:G], start=(c == 0), stop=(c == 7))
        pst = W['pst'].next()
        b.cp(pst[:, 0:3], W['ghist'][:, hc, :], e='act')
        b.cp(pst[:, 3:3 + G], pg[0:64, 0:G], e='act')
        b.cp(W['ghist'][:, hc, :], pst[:, G:G + 3], e='act')
        cv = W['cv'].next()
        b.ts(cv[:, 0:G], pst[:, 0:G], gcw[:, l, hc, 0:1], ALU.mult)
        for i in range(1, 4):
            b.stt(cv[:, 0:G], pst[:, i:i + G], gcw[:, l, hc, i:i + 1], cv[:, 0:G], ALU.mult, ALU.add)
        b.act(cv[:, 0:G], cv[:, 0:G], AF.Silu)
        if hc < 8:
            b.act(W['gsq'][:, 0:G], cv[:, 0:G], AF.Square)
            pn = P[2]
            b.mm(pn[0:64, 0:G], C['onesb'][0:64, 0:64], W['gsq'][:, 0:G])
            b.rsqrt(W['grn'][:, 0:G], pn[0:64, 0:G], 1.0)
            if hc < 4:
                b.stt(qkvn[:, hc, 0:G], cv[:, 0:G], 0.125, W['grn'][:, 0:G], ALU.mult, ALU.mult)
            else:
                b.tt(qkvn[:, hc, 0:G], cv[:, 0:G], W['grn'][:, 0:G], ALU.mult)
        else:
            b.cp(qkvn[:, hc, 0:G], cv[:, 0:G], e='act')
    if t0 + G == Tq:
        p3 = P[3]
        for hc in range(8):
            b.tr(p3[0:3, hc * 64:(hc + 1) * 64], W['ghist'][:, hc, :], C['identf'][0:64, 0:64])
        b.cp(W['g3'][:, 0:512], p3[0:3, 0:512])
        p3b = P[2]
        for hc in range(8, 12):
            b.tr(p3b[0:3, (hc - 8) * 64:(hc - 7) * 64], W['ghist'][:, hc, :], C['identf'][0:64, 0:64])
        b.cp(W['g3'][:, 512:768], p3b[0:3, 0:256])
        if prompt:
            b.dma(o['p_gconv'][l, si, :, :], W['g3'][:])
        else:
            b.dma(o['s_gconv'][l, :, :], W['g3'][:])
    Lc = min(64, G)
    nlev = 5 if Lc == 64 else 3
    gsm = W['gsm']
    v3 = lambda t: t[0:Lc, :, 0:Lc]
    pv3 = lambda p, c0: p[0:Lc, c0:c0 + 256].rearrange("p (h e) -> p h e", e=64)
    tri = C['triIncf'][0:Lc, 0:Lc]
    for ch in range(G // Lc):
        cols = slice(ch * Lc, (ch + 1) * Lc)
        pzb = P[2]
        for c in range(8):
            b.mm(pzb[0:Lc, 0:264], hT[:, c, cols], win[:, c, 768:1032], start=(c == 0), stop=(c == 7))
        beta, gg, Gc, eG, neG, eGl, ekd, tmp4 = (gsm[:, i, :] for i in range(8))
        Sf, Sb = W['Sf'], W['Sb']
        pS = P[0]
        for h in range(4):
            b.mm(pS[0:Lc, h * 64:(h + 1) * 64], qkvn[:, 4 + h, cols], Sb[:, h, :])
        for h in range(4):
            b.mm(pS[0:Lc, 256 + h * 64:256 + (h + 1) * 64], qkvn[:, h, cols], Sb[:, h, :])
        pbk = PB[:, 0:512].rearrange("p (a h e) -> p a h e", a=2, e=64)
        for h in range(4):
            b.tr(pbk[0:Lc, 0, h, :], qkvn[:, 4 + h, cols], C['identb'][0:64, 0:64])
            b.tr(pbk[0:Lc, 1, h, :], qkvn[:, 8 + h, cols], C['identb'][0:64, 0:64])
        ktok, vtok, kd = W['ktok'], W['vtok'], W['kd']
        b.cp(ktok[0:Lc, :, :], pbk[0:Lc, 0, :, :])
        b.cp(vtok[0:Lc, :, :], pbk[0:Lc, 1, :, :])
        pKK = P[4]
        for h in range(4):
            b.mm(pKK[0:Lc, h * 64:h * 64 + Lc], qkvn[:, 4 + h, cols], qkvn[:, 4 + h, cols])
        for h in range(4):
            b.mm(pKK[0:Lc, 256 + h * 64:256 + h * 64 + Lc], qkvn[:, 4 + h, cols], qkvn[:, h, cols])
        zsil = W['zsil']
        pzv = pzb[0:Lc, 0:256].rearrange("p (h e) -> p h e", e=64)
        b.act(zsil[0:Lc, :, :], pzv, AF.Exp, scale=-1.0)
        b.act(beta[0:Lc, :], pzb[0:Lc, 256:260], AF.Exp, scale=-1.0)
        b.ts(zsil[0:Lc, :, :], zsil[0:Lc, :, :], 1.0, ALU.add)
        b.recip(zsil[0:Lc, :, :], zsil[0:Lc, :, :])
        b.tt(zsil[0:Lc, :, :], zsil[0:Lc, :, :], pzv, ALU.mult)
        b.ts(beta[0:Lc, :], beta[0:Lc, :], 1.0, ALU.add)
        b.recip(beta[0:Lc, :], beta[0:Lc, :])
        b.tt(tmp4[0:Lc, :], pzb[0:Lc, 260:264], W['dtb'][0:Lc, :], ALU.add)
        b.act(tmp4[0:Lc, :], tmp4[0:Lc, :], AF.Exp)
        b.act(tmp4[0:Lc, :], tmp4[0:Lc, :], AF.Ln, bias=1.0)
        b.tt(gg[0:Lc, :], tmp4[0:Lc, :], W['nexpA'][0:Lc, :], ALU.mult)
        pG = P[3]
        b.mm(pG[0:Lc, 0:4], tri, gg[0:Lc, :])
        b.mm(pG[0:64, 4:8], C['onesf'][0:Lc, 0:64], gg[0:Lc, :])
        b.cp(Gc[0:Lc, :], pG[0:Lc, 0:4])
        b.act(eGl[:, :], pG[0:64, 4:8], AF.Exp)
        b.tt(ekd[0:Lc, :], pG[0:Lc, 4:8], Gc[0:Lc, :], ALU.subtract)
        b.act(ekd[0:Lc, :], ekd[0:Lc, :], AF.Exp)
        b.act(eG[0:Lc, :], Gc[0:Lc, :], AF.Exp)
        b.tt(W['kd'][0:Lc, :, :], W['ktok'][0:Lc, :, :], bc(ekd[0:Lc, :], [Lc, 4, 64]), ALU.mult)
        b.ts(neG[0:Lc, :], eG[0:Lc, :], -1.0, ALU.mult)
        gb = W['gb']
        for h in range(4):
            b.ts(gb[0:Lc, h, 0:Lc], C['onesf'][0:Lc, 0:Lc], gg[0:Lc, h:h + 1], ALU.mult)
        pGr = P[3]
        for h in range(4):
            b.mm(pGr[0:Lc, 256 + h * 64:256 + h * 64 + Lc], gb[0:Lc, h, 0:Lc], tri)
        dec, decI, decS = W['dec'], W['decI'], W['decS']
        for h in range(4):
            b.ts(dec[0:Lc, h, 0:Lc], pGr[0:Lc, 256 + h * 64:256 + h * 64 + Lc], Gc[0:Lc, h:h + 1], ALU.subtract, 0.0, ALU.min)
        b.act(v3(dec), v3(dec), AF.Exp)
        b.tt(v3(decI), v3(dec), mid_bc(tri, 4), ALU.mult)
        b.tt(v3(decS), v3(dec), mid_bc(C['sutf'][0:Lc, 0:Lc], 4), ALU.mult)
        Uf = W['Uf']
        b.tt(v3(Uf), pv3(pKK, 0)[:, :, 0:Lc], v3(decS), ALU.mult)
        b.tt(v3(Uf), v3(Uf), bc(beta[0:Lc, :], [Lc, 4, Lc]), ALU.mult)
        AT = W['AT']
        b.tt(v3(AT), pv3(pKK, 256)[:, :, 0:Lc], v3(decI), ALU.mult)
        Pc, Pn_, Qc, Qn_, Yc, Yn_ = Uf, W['Pa'], W['Qa'], W['Qb'], W['Ya'], W['Yb']
        pq = P[5]
        for h in range(4):
            b.tr(pq[0:Lc, h * 64:h * 64 + Lc], Uf[0:Lc, h, 0:Lc], C['identf'][0:Lc, 0:Lc])
        b.cp(v3(Qc), pv3(pq, 0)[:, :, 0:Lc])
        b.tt(v3(Yc), mid_bc(C['identf'][0:Lc, 0:Lc], 4), v3(Uf), ALU.subtract)
        for k in range(1, nlev + 1):
            pI = P[5]
            for h in range(4):
                b.mm(pI[0:Lc, h * 64:h * 64 + Lc], Pc[0:Lc, h, 0:Lc], Qc[0:Lc, h, 0:Lc])
            if k < nlev:
                for h in range(4):
                    b.mm(pI[0:Lc, 256 + h * 64:256 + h * 64 + Lc], Qc[0:Lc, h, 0:Lc], Pc[0:Lc, h, 0:Lc])
            b.cp(v3(Qn_), pv3(pI, 0)[:, :, 0:Lc], e='act')
            if k < nlev:
                b.cp(v3(Pn_), pv3(pI, 256)[:, :, 0:Lc])
            pY = P[6]
            if k > 1:
                for h in range(4):
                    b.mm(pY[0:Lc, h * 64:h * 64 + Lc], Qc[0:Lc, h, 0:Lc], Yc[0:Lc, h, 0:Lc])
                b.tt(v3(Yn_), v3(Yc), pv3(pY, 0)[:, :, 0:Lc], ALU.add)
                Yc, Yn_ = Yn_, Yc
            Pc, Pn_ = Pn_, Pc
            Qc, Qn_ = Qn_, Qc
        pY = P[6]
        for h in range(4):
            b.mm(pY[0:Lc, h * 64:h * 64 + Lc], Qc[0:Lc, h, 0:Lc], Yc[0:Lc, h, 0:Lc])
        b.tt(v3(Yn_), v3(Yc), pv3(pY, 0)[:, :, 0:Lc], ALU.add)
        Yc, Yn_ = Yn_, Yc
        Rm, vn, of = W['gb'], W['vn'], W['of']
        b.tt(of[0:Lc, :, :], pv3(pS, 0), bc(neG[0:Lc, :], [Lc, 4, 64]), ALU.mult)
        b.tt(Rm[0:Lc, :, :], of[0:Lc, :, :], vtok[0:Lc, :, :], ALU.add)
        b.tt(of[0:Lc, :, :], pv3(pS, 256), bc(eG[0:Lc, :], [Lc, 4, 64]), ALU.mult)
        pX = P[1]
        for h in range(4):
            b.mm(pX[0:Lc, h * 64:(h + 1) * 64], Yc[0:Lc, h, 0:Lc], Rm[0:Lc, h, :])
        b.tt(vn[0:Lc, :, :], pv3(pX, 0), bc(beta[0:Lc, :], [Lc, 4, 64]), ALU.mult)
        for h in range(4):
            b.mm(pX[0:Lc, 256 + h * 64:256 + (h + 1) * 64], AT[0:Lc, h, 0:Lc], vn[0:Lc, h, :])
        b.tt(of[0:Lc, :, :], of[0:Lc, :, :], pv3(pX, 256), ALU.add)
        pSn = P[4]
        for h in range(4):
            b.mm(pSn[0:64, h * 64:(h + 1) * 64], kd[0:Lc, h, :], vn[0:Lc, h, :])
        b.tt(Sf[:, :, :], Sf[:, :, :], bc(eGl[:, :], [64, 4, 64]), ALU.mult)
        b.tt(Sf[:, :, :], Sf[:, :, :], pSn[0:64, 0:256].rearrange("p (h e) -> p h e", e=64), ALU.add)
        b.cp(Sb[:, :, :], Sf[:, :, :], e='act')
        on, om = W['on'], W['om']
        rmsnorm_tok(b, on[0:Lc, :, :], of[0:Lc, :, :], 4, Lc, W['gG'], W['nwk'])
        b.tt(om[0:Lc, :, :], on[0:Lc, :, :], zsil[0:Lc, :, :], ALU.mult)
        pm = PB[:, 768:1024].rearrange("p (a t) -> p a t", t=128)
        omf = om[0:Lc, :, :].rearrange("p h e -> p (h e)")
        for cc in range(2):
            b.tr(pm[:, cc, 0:Lc], omf[:, cc * 128:(cc + 1) * 128], C['identb'][0:Lc, 0:Lc])
        b.cp(W['mixedT'][:, 0:2, ch * Lc:(ch + 1) * Lc], pm[:, 0:2, 0:Lc])
    if t0 + G == Tq:
        if prompt:
            b.dma(o['p_gstate'][l, si].rearrange("h k v -> k h v"), W['Sf'][:])
        else:
            b.dma(o['s_gstate'][l].rearrange("h k v -> k h v"), W['Sf'][:])


_NC_CACHE = {}


def _prep_inputs(inp):
    f = lambda a: np.ascontiguousarray(np.asarray(a, dtype=np.float32))
    L = DEPTH
    shared = {
        'ada_w': f(inp['ada_w']),
        'ada_bT': f(np.asarray(inp['ada_b']).reshape(L, 48, 128).transpose(2, 0, 1)),
        'nmgT': f(np.asarray(inp['norm_mix_g']).reshape(L, 8, 128).transpose(2, 0, 1)),
        'nfgT': f(np.asarray(inp['norm_ffn_g']).reshape(L, 8, 128).transpose(2, 0, 1)),
        'w_in': f(inp['w_in']),
        'gcwT': f(np.asarray(inp['gdn_conv_w']).reshape(L, 4, 12, 64).transpose(3, 0, 2, 1)),
        'a_log': f(inp['gdn_a_log']), 'dt_bias': f(inp['gdn_dt_bias']), 'gn_g': f(inp['gdn_norm_g']),
        'fq_g': f(inp['fox_q_g']), 'fk_g': f(inp['fox_k_g']), 'fb_f': f(inp['fox_b_f']),
        'bq_g': f(inp['band_q_g']), 'bk_g': f(inp['band_k_g']), 'rel': f(inp['band_rel_bias']),
        'mg': f(np.asarray(inp['merge_g']).reshape(L, 768)),
        'w_o': f(inp['w_o']), 'w_up': f(inp['w_up']),
        'fcwT': f(np.asarray(inp['ffn_conv_w']).reshape(L, 3, 44, 128).transpose(3, 0, 2, 1)),
        'w_down': f(inp['w_down']),
    }
    xp = np.asarray(inp['x_prompt'], dtype=np.float32)
    xs = np.asarray(inp['x_sample'], dtype=np.float32)
    cp_ = np.asarray(inp['c_prompt'], dtype=np.float32)
    cs = np.asarray(inp['c_sample'], dtype=np.float32)
    maps = []
    for k in range(8):
        m = dict(shared)
        xT = np.empty((D, NTOK), np.float32)
        xT[:, 0:T] = xp[2 * k].T
        xT[:, T:2 * T] = xp[2 * k + 1].T
        xT[:, 2 * T:] = xs[k].T
        m['xT'] = xT
        cc = np.zeros((4, D), np.float32)
        cc[0], cc[1], cc[2] = cp_[2 * k], cp_[2 * k + 1], cs[k]
        m['cT'] = f(cc.reshape(4, 8, 128).transpose(2, 1, 0))
        m['gconvT'] = f(np.asarray(inp['state_gdn_conv'])[:, k].reshape(L, 3, 12, 64).transpose(0, 3, 2, 1))
        m['gstate'] = f(np.asarray(inp['state_gdn'])[:, k])
        m['sbk'] = f(np.asarray(inp['cache_sb_k'])[:, k].reshape(L, PAST, 256))
        m['sbv'] = f(np.asarray(inp['cache_sb_v'])[:, k].reshape(L, PAST, 256))
        m['fxk'] = f(np.asarray(inp['cache_fox_k'])[:, k].reshape(L, PAST, 256))
        m['fxv'] = f(np.asarray(inp['cache_fox_v'])[:, k].reshape(L, PAST, 256))
        m['fxlf'] = f(np.asarray(inp['cache_fox_logf'])[:, k])
        m['bdk'] = f(np.asarray(inp['cache_band_k'])[:, k].reshape(L, 512, 256))
        m['bdv'] = f(np.asarray(inp['cache_band_v'])[:, k].reshape(L, 512, 256))
        m['fconvT'] = f(np.asarray(inp['state_ffn_conv'])[:, k].reshape(L, 2, 44, 128).transpose(0, 3, 2, 1))
        maps.append(m)
    return maps


def _assemble(res):
    L = DEPTH
    r = res
    yp = np.empty((16, T, D), np.float32)
    ys = np.empty((8, TS, D), np.float32)
    for k in range(8):
        yT = r[k]['yT']
        yp[2 * k] = yT[:, 0:T].T
        yp[2 * k + 1] = yT[:, T:2 * T].T
        ys[k] = yT[:, 2 * T:].T
    cat_p = lambda n, shp: np.concatenate([r[k][n] for k in range(8)], axis=1).reshape(shp)
    stk_s = lambda n, shp: np.stack([r[k][n] for k in range(8)], axis=1).reshape(shp)
    outs = [yp, ys,
            cat_p('p_gconv', (L, 16, 3, 768)), cat_p('p_gstate', (L, 16, 4, 64, 64)),
            cat_p('p_sbk', (L, 16, T, 4, 64)), cat_p('p_sbv', (L, 16, T, 4, 64)),
            cat_p('p_fxk', (L, 16, T, 4, 64)), cat_p('p_fxv', (L, 16, T, 4, 64)),
            cat_p('p_fxlf', (L, 16, T, 4)),
            cat_p('p_bdk', (L, 16, 512, 4, 64)), cat_p('p_bdv', (L, 16, 512, 4, 64)),
            cat_p('p_fconv', (L, 16, 2, 2 * DFF)),
            stk_s('s_gconv', (L, 8, 3, 768)), stk_s('s_gstate', (L, 8, 4, 64, 64)),
            stk_s('s_sbk', (L, 8, TS, 4, 64)), stk_s('s_sbv', (L, 8, TS, 4, 64)),
            stk_s('s_fxk', (L, 8, TS, 4, 64)), stk_s('s_fxv', (L, 8, TS, 4, 64)),
            stk_s('s_fxlf', (L, 8, TS, 4)),
            stk_s('s_bdk', (L, 8, 512, 4, 64)), stk_s('s_bdv', (L, 8, 512, 4, 64)),
            stk_s('s_fconv', (L, 8, 2, 2 * DFF))]
    return tuple(np.ascontiguousarray(a, dtype=np.float32) for a in outs)


def kernel(**inputs):
    maps = _prep_inputs(inputs)
    if 'nc' not in _NC_CACHE:
        _NC_CACHE['nc'] = build()
    res = run_bass_kernel_spmd(_NC_CACHE['nc'], maps, core_ids=list(range(8)))
    return _assemble(res.results)
```

```python
import contextlib
import numpy as np
import concourse.bass as bass
import concourse.mybir as mybir
from concourse.bass_utils import run_bass_kernel_spmd

F32 = mybir.dt.float32
BF16 = mybir.dt.bfloat16
AF = mybir.ActivationFunctionType
ALU = mybir.AluOpType
AX = mybir.AxisListType

D = 1024
T = 2048
DEPTH = 4
TS = 16
PAST = 1024
NTOK = 2 * T + TS
INC = 3340
DFF = 2816
EPS = 1e-6


def _box(ap):
    t = ap.tensor
    name = t.name
    dims = [(int(s), int(c)) for s, c in ap.ap]
    off = int(ap.offset)
    if 'DRAM' in str(ap.space).upper():
        lo = hi = off
        for s, c in dims:
            if s >= 0:
                hi += s * (c - 1)
            else:
                lo += s * (c - 1)
        return (name, 0, 1, lo, hi + 1)
    import os
    if 'PSUM' in str(ap.space).upper() and (os.environ.get('KDBG_PSUMBOX', '1') == '1' or (os.environ.get('KDBG_PSUMBOX') == '2' and name.startswith('pbb'))):
        return (name, 0, 128, 0, 1 << 30)
    psize = 1
    for d in list(t.shape)[1:]:
        psize *= int(d)
    p0 = off // psize
    f0 = off % psize
    pstep, pcnt = dims[0]
    if pstep == psize or pcnt == 1:
        np_ = pcnt
    elif pstep == 0:
        np_ = 1
    else:
        np_ = 128 - p0
    lo = hi = f0
    for s, c in dims[1:]:
        if s >= 0:
            hi += s * (c - 1)
        else:
            lo += s * (c - 1)
    return (name, p0, p0 + np_, lo, hi + 1)


class Sync:
    def __init__(self, nc, stack, n_dma_sems=8):
        self.nc = nc
        self.eng = {'pe': nc.tensor, 'act': nc.scalar, 'dve': nc.vector, 'pool': nc.gpsimd, 'sp': nc.sync}
        self.sem = {}
        self.cnt = {}
        for e in ('pe', 'act', 'dve', 'pool'):
            self.sem[e] = stack.enter_context(nc.semaphore('s_' + e))
            self.cnt[e] = 0
        self.dsem = {}
        self.dcnt = {}
        for q in ('sp', 'pool'):
            self.dsem[q] = [stack.enter_context(nc.semaphore('d_%s%d' % (q, i))) for i in range(n_dma_sems)]
            self.dcnt[q] = 0
        self.K = n_dma_sems
        self.semobj = {}
        for e, s in self.sem.items():
            self.semobj[('c', e)] = s
        for q, l in self.dsem.items():
            for i, s in enumerate(l):
                self.semobj[('d', q, i)] = s
        self.waited = {e: {} for e in self.eng}
        self.W = {}
        self.R = {}
        self.n_wait = 0
        self.n_inst = 0

    def _collect(self, reads, writes):
        deps = {}
        for ap in reads:
            b = _box(ap)
            for r in self.W.get(b[0], ()):
                if r[0] < b[2] and b[1] < r[1] and r[2] < b[4] and b[3] < r[3]:
                    if deps.get(r[4], 0) < r[5]:
                        deps[r[4]] = r[5]
        for ap in writes:
            b = _box(ap)
            for r in self.W.get(b[0], ()):
                if r[0] < b[2] and b[1] < r[1] and r[2] < b[4] and b[3] < r[3]:
                    if deps.get(r[4], 0) < r[5]:
                        deps[r[4]] = r[5]
            rd = self.R.get(b[0])
            if rd:
                for k, v in rd.items():
                    if k[0] < b[2] and b[1] < k[1] and k[2] < b[4] and b[3] < k[3]:
                        if deps.get(k[4], 0) < v:
                            deps[k[4]] = v
        return deps

    def _record(self, reads, writes, semkey, val):
        for ap in writes:
            b = _box(ap)
            lst = self.W.setdefault(b[0], [])
            lst[:] = [r for r in lst if not (b[1] <= r[0] and r[1] <= b[2] and b[3] <= r[2] and r[3] <= b[4])]
            lst.append([b[1], b[2], b[3], b[4], semkey, val])
            rd = self.R.get(b[0])
            if rd:
                for k in [k for k in rd if b[1] <= k[0] and k[1] <= b[2] and b[3] <= k[2] and k[3] <= b[4]]:
                    del rd[k]
        for ap in reads:
            b = _box(ap)
            self.R.setdefault(b[0], {})[(b[1], b[2], b[3], b[4], semkey)] = val

    def _emit_waits(self, e, deps):
        w = self.waited[e]
        for k, v in deps.items():
            if k == ('c', 'pe') and e == 'pe':
                continue
            if w.get(k, 0) >= v:
                continue
            self.eng[e].wait_ge(self.semobj[k], v)
            w[k] = v
            self.n_wait += 1

    def op(self, e, fn, reads=(), writes=()):
        px = [a for a in reads if 'PSUM' in str(a.space).upper()]
        if px:
            writes = list(writes) + px
        deps = self._collect(reads, writes)
        self._emit_waits(e, deps)
        inst = fn()
        self.cnt[e] += 1
        inst.then_inc(self.sem[e], 1)
        self._record(reads, writes, ('c', e), self.cnt[e])
        self.n_inst += 1
        return inst

    def dma(self, q, out, in_, **kw):
        deps = self._collect([in_], [out])
        n = self.dcnt[q]
        k = n % self.K
        semkey = ('d', q, k)
        prev = (n // self.K) * 16
        if prev > 0:
            deps[semkey] = max(deps.get(semkey, 0), prev)
        self._emit_waits(q, deps)
        inst = self.eng[q].dma_start(out=out, in_=in_, **kw)
        inst.then_inc(self.semobj[semkey], 16)
        self.dcnt[q] = n + 1
        self._record([in_], [out], semkey, prev + 16)
        self.n_inst += 1
        return inst

    def barrier(self):
        toks = {}
        for e in ('pe', 'act', 'dve', 'pool'):
            if self.cnt[e] > 0:
                toks[('c', e)] = self.cnt[e]
        for q in self.dsem:
            n = self.dcnt[q]
            for k in range(self.K):
                cntk = (n - k + self.K - 1) // self.K if n > k else 0
                if cntk > 0:
                    toks[('d', q, k)] = cntk * 16
        for e in self.eng:
            w = self.waited[e]
            for k, v in toks.items():
                if k == ('c', 'pe') and e == 'pe':
                    continue
                if w.get(k, 0) >= v:
                    continue
                self.eng[e].wait_ge(self.semobj[k], v)
                w[k] = v
                self.n_wait += 1
        self.W = {}
        self.R = {}


class B:
    def __init__(self, nc, st, n_layers=DEPTH, stage=99):
        self.nc = nc
        self.st = st
        self.S = Sync(nc, st)
        self.n_layers = n_layers
        self.stage = stage
        self.uid = 0

    def sb(self, name, shape, dt, st=None):
        self.uid += 1
        return (st or self.st).enter_context(self.nc.sbuf_tensor('%s_%d' % (name, self.uid), shape, dt))

    def ps(self, name, shape, dt, st=None):
        self.uid += 1
        return (st or self.st).enter_context(self.nc.psum_tensor('%s_%d' % (name, self.uid), shape, dt))

    def din(self, name, shape):
        return self.nc.dram_tensor(name, list(shape), F32, kind="ExternalInput").ap()

    def dout(self, name, shape):
        return self.nc.dram_tensor(name, list(shape), F32, kind="ExternalOutput").ap()

    def mm(self, out, lhsT, rhs, start=True, stop=True, **kw):
        nc = self.nc
        return self.S.op('pe', lambda: nc.tensor.matmul(out, lhsT=lhsT, rhs=rhs, start=start, stop=stop, **kw),
                         reads=[lhsT, rhs], writes=[out])

    def tr(self, out, in_, ident):
        nc = self.nc
        return self.S.op('pe', lambda: nc.tensor.transpose(out=out, in_=in_, identity=ident),
                         reads=[in_, ident], writes=[out])

    def act(self, out, in_, func, bias=None, scale=1.0):
        nc = self.nc
        reads = [in_]
        kw = {}
        if bias is not None:
            kw['bias'] = bias
            if not isinstance(bias, (int, float)):
                reads.append(bias)
        if not isinstance(scale, (int, float)):
            reads.append(scale)
        return self.S.op('act', lambda: nc.scalar.activation(out=out, in_=in_, func=func, scale=scale, **kw),
                         reads=reads, writes=[out])

    def _ve(self, e):
        return self.nc.vector if e == 'dve' else self.nc.gpsimd

    def tt(self, out, in0, in1, op, e='dve'):
        eng = self._ve(e)
        return self.S.op(e, lambda: eng.tensor_tensor(out=out, in0=in0, in1=in1, op=op),
                         reads=[in0, in1], writes=[out])

    def ts(self, out, in0, s1, op0, s2=None, op1=None, e='dve'):
        eng = self._ve(e)
        reads = [in0]
        if not isinstance(s1, (int, float)):
            reads.append(s1)
        if s2 is not None and not isinstance(s2, (int, float)):
            reads.append(s2)
        if op1 is None:
            f = lambda: eng.tensor_scalar(out=out, in0=in0, scalar1=s1, scalar2=None, op0=op0)
        else:
            f = lambda: eng.tensor_scalar(out=out, in0=in0, scalar1=s1, scalar2=s2, op0=op0, op1=op1)
        return self.S.op(e, f, reads=reads, writes=[out])

    def stt(self, out, in0, scalar, in1, op0, op1, e='dve'):
        eng = self._ve(e)
        reads = [in0, in1]
        if not isinstance(scalar, (int, float)):
            reads.append(scalar)
        return self.S.op(e, lambda: eng.scalar_tensor_tensor(out=out, in0=in0, scalar=scalar, in1=in1, op0=op0, op1=op1),
                         reads=reads, writes=[out])

    def cp(self, out, in_, e='dve'):
        if e == 'act':
            return self.act(out, in_, AF.Copy)
        eng = self._ve(e)
        return self.S.op(e, lambda: eng.tensor_copy(out=out, in_=in_), reads=[in_], writes=[out])

    def red(self, out, in_, op=ALU.add, e='dve'):
        eng = self._ve(e)
        return self.S.op(e, lambda: eng.tensor_reduce(out=out, in_=in_, axis=AX.X, op=op), reads=[in_], writes=[out])

    def recip(self, out, in_):
        nc = self.nc
        return self.S.op('dve', lambda: nc.vector.reciprocal(out=out, in_=in_), reads=[in_], writes=[out])

    def memset(self, ap, v, e='pool'):
        eng = self._ve(e)
        return self.S.op(e, lambda: eng.memset(ap, v), writes=[ap])

    def asel(self, out, in_, pattern, cmp, fill, base, cm):
        nc = self.nc
        return self.S.op('pool', lambda: nc.gpsimd.affine_select(out=out, in_=in_, pattern=pattern, compare_op=cmp,
                                                               fill=fill, base=base, channel_multiplier=cm),
                         reads=[in_], writes=[out])

    def dma(self, out, in_, q='sp', **kw):
        return self.S.dma(q, out, in_, **kw)

    def rsqrt(self, out, in_, scale, tmp=None):
        t = tmp if tmp is not None else out
        self.act(t, in_, AF.Ln, bias=self.eps_col[0:int(in_.shape[0]), :], scale=scale)
        self.act(out, t, AF.Exp, scale=-0.5)


def bc(ap, shape):
    return ap.unsqueeze(len(ap.shape)).broadcast_to(list(shape))


def pbc(dram_ap_1d, n, parts=128):
    return bass.AP(tensor=dram_ap_1d.tensor, offset=int(dram_ap_1d.offset), ap=[[0, parts], [1, n]])


class Rot:
    def __init__(self, bufs):
        self.bufs = bufs
        self.i = 0

    def next(self):
        t = self.bufs[self.i % len(self.bufs)]
        self.i += 1
        return t


def build(n_layers=DEPTH, stage=99):
    nc = bass.Bass("TRN2", target_bir_lowering=False)
    with contextlib.ExitStack() as st:
        b = B(nc, st, n_layers, stage)
        emit(b)
        b.S.barrier()
        print("built: inst", b.S.n_inst, "waits", b.S.n_wait)
    return nc


def emit(b):
    nc = b.nc
    L = DEPTH
    NL = b.n_layers
    d = {}
    d['xT'] = b.din('xT', [D, NTOK])
    d['cT'] = b.din('cT', [128, 8, 4])
    d['gconvT'] = b.din('gconvT', [L, 64, 12, 3])
    d['gstate'] = b.din('gstate', [L, 4, 64, 64])
    for n in ('sbk', 'sbv', 'fxk', 'fxv'):
        d[n] = b.din(n, [L, PAST, 256])
    d['fxlf'] = b.din('fxlf', [L, PAST, 4])
    d['bdk'] = b.din('bdk', [L, 512, 256])
    d['bdv'] = b.din('bdv', [L, 512, 256])
    d['fconvT'] = b.din('fconvT', [L, 128, 44, 2])
    d['ada_w'] = b.din('ada_w', [L, D, 6 * D])
    d['ada_bT'] = b.din('ada_bT', [128, L, 48])
    d['nmgT'] = b.din('nmgT', [128, L, 8])
    d['nfgT'] = b.din('nfgT', [128, L, 8])
    d['w_in'] = b.din('w_in', [L, D, INC])
    d['gcwT'] = b.din('gcwT', [64, L, 12, 4])
    d['a_log'] = b.din('a_log', [L, 4])
    d['dt_bias'] = b.din('dt_bias', [L, 4])
    d['gn_g'] = b.din('gn_g', [L, 64])
    d['fq_g'] = b.din('fq_g', [L, 64])
    d['fk_g'] = b.din('fk_g', [L, 64])
    d['fb_f'] = b.din('fb_f', [L, 4])
    d['bq_g'] = b.din('bq_g', [L, 64])
    d['bk_g'] = b.din('bk_g', [L, 64])
    d['rel'] = b.din('rel', [L, 4, 257])
    d['mg'] = b.din('mg', [L, 768])
    d['w_o'] = b.din('w_o', [L, D, D])
    d['w_up'] = b.din('w_up', [L, D, 2 * DFF])
    d['fcwT'] = b.din('fcwT', [128, L, 44, 3])
    d['w_down'] = b.din('w_down', [L, DFF, D])
    o = {}
    o['yT'] = b.dout('yT', [D, NTOK])
    o['p_gconv'] = b.dout('p_gconv', [L, 2, 3, 768])
    o['p_gstate'] = b.dout('p_gstate', [L, 2, 4, 64, 64])
    for n in ('p_sbk', 'p_sbv', 'p_fxk', 'p_fxv'):
        o[n] = b.dout(n, [L, 2, T, 256])
    o['p_fxlf'] = b.dout('p_fxlf', [L, 2, T, 4])
    o['p_bdk'] = b.dout('p_bdk', [L, 2, 512, 256])
    o['p_bdv'] = b.dout('p_bdv', [L, 2, 512, 256])
    o['p_fconv'] = b.dout('p_fconv', [L, 2, 2, 2 * DFF])
    o['s_gconv'] = b.dout('s_gconv', [L, 3, 768])
    o['s_gstate'] = b.dout('s_gstate', [L, 4, 64, 64])
    for n in ('s_sbk', 's_sbv', 's_fxk', 's_fxv'):
        o[n] = b.dout(n, [L, TS, 256])
    o['s_fxlf'] = b.dout('s_fxlf', [L, TS, 4])
    o['s_bdk'] = b.dout('s_bdk', [L, 512, 256])
    o['s_bdv'] = b.dout('s_bdv', [L, 512, 256])
    o['s_fconv'] = b.dout('s_fconv', [L, 2, 2 * DFF])
    b.d, b.o = d, o

    identb = b.sb('identb', [128, 128], BF16)
    identf = b.sb('identf', [128, 128], F32)
    onesb = b.sb('onesb', [128, 128], BF16)
    onesf = b.sb('onesf', [128, 128], F32)
    triUb = b.sb('triUb', [128, 128], BF16)
    triIncf = b.sb('triIncf', [128, 128], F32)
    sutf = b.sb('sutf', [64, 64], F32)
    antiJ = b.sb('antiJ', [128, 128], F32)
    blkb = b.sb('blkb', [128, 128], BF16)
    mask0 = b.sb('mask0', [128, 4, 128], BF16)
    mask4 = b.sb('mask4', [128, 4, 128], BF16)
    b.eps_col = b.sb('eps_col', [128, 1], F32)
    b.memset(b.eps_col[:], EPS)
    b.memset(identf[:], 0.0)
    b.asel(identf[:], identf[:], [[-1, 128]], ALU.not_equal, 1.0, 0, 1)
    b.cp(identb[:], identf[:])
    b.memset(onesb[:], 1.0)
    b.memset(onesf[:], 1.0)
    b.asel(triUb[:], onesb[:], [[-1, 128]], ALU.is_ge, 0.0, 0, 1)
    b.asel(triIncf[:], onesf[:], [[1, 128]], ALU.is_ge, 0.0, 0, -1)
    b.asel(sutf[:], onesf[0:64, 0:64], [[1, 64]], ALU.is_gt, 0.0, 0, -1)
    b.memset(antiJ[:], 0.0)
    b.asel(antiJ[:], antiJ[:], [[1, 128]], ALU.not_equal, 1.0, -127, 1)
    b.memset(blkb[:], 0.0)
    b.memset(blkb[0:64, 0:64], 1.0)
    b.memset(blkb[64:128, 64:128], 1.0)
    b.memset(mask0[:], 1.0)
    b.memset(mask0[0:64, :, 64:128], 0.0)
    b.memset(mask4[:], 1.0)
    b.memset(mask4[64:128, :, 0:64], 0.0)
    C = dict(identb=identb, identf=identf, onesb=onesb, onesf=onesf, triUb=triUb, triIncf=triIncf, sutf=sutf,
             antiJ=antiJ, blkb=blkb, mask0=mask0, mask4=mask4)
    b.C = C

    b.P = [b.ps('pb%d' % i, [128, 512], F32) for i in range(7)]
    b.PB = b.ps('pbb', [128, 1024], BF16)

    nmg = b.sb('nmg', [128, L, 8], F32)
    nfg = b.sb('nfg', [128, L, 8], F32)
    adab = b.sb('adab', [128, L, 48], F32)
    gcw = b.sb('gcw', [64, L, 12, 4], F32)
    fcw = b.sb('fcw', [128, L, 44, 3], F32)
    b.dma(nmg[:], d['nmgT'])
    b.dma(nfg[:], d['nfgT'])
    b.dma(adab[:], d['ada_bT'])
    b.dma(gcw[:], d['gcwT'])
    b.dma(fcw[:], d['fcwT'])
    b.gcw, b.fcw = gcw, fcw

    mod = b.sb('mod', [128, L, 48, 4], F32)
    modA = b.sb('modA', [128, L, 2, 8, 4], F32)
    b.mod, b.modA = mod, modA
    with contextlib.ExitStack() as s1:
        cT = b.sb('cT', [128, 8, 4], F32, s1)
        sc = b.sb('sc', [128, 8, 4], F32, s1)
        tmp = b.sb('sctmp', [128, 8, 4], F32, s1)
        b.dma(cT[:], d['cT'])
        b.act(tmp[:], cT[:], AF.Exp, scale=-1.0)
        b.ts(tmp[:], tmp[:], 1.0, ALU.add)
        b.recip(tmp[:], tmp[:])
        b.tt(sc[:], cT[:], tmp[:], ALU.mult)
        awr = Rot([b.sb('aw%d' % i, [128, 8, 768], F32, s1) for i in range(2)])
        for l in range(NL):
            pm = b.P[l % 2]
            for pc in range(8):
                aw = awr.next()
                b.dma(aw[:], d['ada_w'][l, :, pc * 768:(pc + 1) * 768].rearrange("(c p) n -> p c n", p=128))
                for j in range(6):
                    oc = pc * 6 + j
                    for c in range(8):
                        b.mm(pm[:, oc * 4:oc * 4 + 4], aw[:, c, j * 128:(j + 1) * 128], sc[:, c, :],
                             start=(c == 0), stop=(c == 7))
            pmv = pm[:, 0:192].rearrange("p (a s) -> p a s", s=4)
            b.tt(mod[:, l, :, :], pmv, bc(adab[:, l, :], [128, 48, 4]), ALU.add)
            for w, g in ((0, nmg), (1, nfg)):
                scv = mod[:, l, (1 + 3 * w) * 8:(2 + 3 * w) * 8, :]
                b.ts(modA[:, l, w, :, :], scv, 1.0, ALU.add)
                b.tt(modA[:, l, w, :, :], modA[:, l, w, :, :], bc(g[:, l, :], [128, 8, 4]), ALU.mult)
    b.S.barrier()

    for l in range(NL):
        layer(b, l)


def rmsnorm_tok(b, out, src, n, np_, gains, wk, den=None):
    sq, ss, y = wk
    x = src
    if den is not None:
        b.recip(ss[0:np_, 0:n], den)
        b.tt(y[0:np_, 0:n, :], src, bc(ss[0:np_, 0:n], [np_, n, 64]), ALU.mult)
        x = y[0:np_, 0:n, :]
    b.act(sq[0:np_, 0:n, :], x, AF.Square)
    b.red(ss[0:np_, 0:n], sq[0:np_, 0:n, :])
    b.rsqrt(ss[0:np_, 0:n], ss[0:np_, 0:n], 1.0 / 64.0)
    if gains is None:
        b.tt(out, x, bc(ss[0:np_, 0:n], [np_, n, 64]), ALU.mult)
    else:
        b.tt(y[0:np_, 0:n, :], x, bc(ss[0:np_, 0:n], [np_, n, 64]), ALU.mult)
        b.tt(out, y[0:np_, 0:n, :], gains[0:np_, 0:n, :], ALU.mult)


def layer(b, l):
    nc = b.nc
    d, o, C, P = b.d, b.o, b.C, b.P
    src_x = d['xT'] if l == 0 else o['yT']
    with contextlib.ExitStack() as s:
        win = b.sb('win', [128, 8, INC], BF16, s)
        wo = b.sb('wo', [128, 8, D], BF16, s)
        for c in range(8):
            for hh in range(2):
                b.dma(win[:, c, hh * 1670:(hh + 1) * 1670], d['w_in'][l, c * 128:(c + 1) * 128, hh * 1670:(hh + 1) * 1670], q='pool')
        for c in range(8):
            b.dma(wo[:, c, :], d['w_o'][l, c * 128:(c + 1) * 128, :], q='pool')
        gC = b.sb('gC', [128, 8, 64], F32, s)
        gD = b.sb('gD', [128, 8, 64], F32, s)
        gG = b.sb('gG', [128, 4, 64], F32, s)
        mg = b.sb('mg', [128, 12, 64], F32, s)
        dtb = b.sb('dtb', [128, 4], F32, s)
        nexpA = b.sb('nexpA', [128, 4], F32, s)
        fbf = b.sb('fbf', [128, 4], F32, s)

        def bcl(t, n, rep):
            return bass.AP(tensor=t.tensor, offset=int(t.offset), ap=[[0, 128], [0, rep], [1, n]])
        b.dma(gC[:, 0:4, :], bcl(d['fq_g'][l], 64, 4))
        b.dma(gC[:, 4:8, :], bcl(d['fk_g'][l], 64, 4))
        b.dma(gD[:, 0:4, :], bcl(d['bq_g'][l], 64, 4))
        b.dma(gD[:, 4:8, :], bcl(d['bk_g'][l], 64, 4))
        b.dma(gG[:], bcl(d['gn_g'][l], 64, 4))
        b.dma(mg[:].rearrange("p a b -> p (a b)"), pbc(d['mg'][l], 768))
        b.dma(dtb[:], pbc(d['dt_bias'][l], 4))
        b.dma(nexpA[:], pbc(d['a_log'][l], 4))
        b.dma(fbf[:], pbc(d['fb_f'][l], 4))
        b.act(nexpA[:], nexpA[:], AF.Exp)
        b.ts(nexpA[:], nexpA[:], -1.0, ALU.mult)
        relx = nc.dram_tensor('relx%d' % l, [4, 520], F32, kind="Internal").ap()
        with contextlib.ExitStack() as s2:
            rl = b.sb('rl', [4, 257], F32, s2)
            rx = b.sb('rx', [4, 520], F32, s2)
            b.dma(rl[:], d['rel'][l])
            b.memset(rx[:], 0.0)
            b.cp(rx[:, 128:385], rl[:])
            b.cp(rx[:, 0:128], rl[:, 0:1].broadcast_to([4, 128]))
            b.cp(rx[:, 385:513], rl[:, 256:257].broadcast_to([4, 128]))
            b.dma(relx, rx[:])
        Bt = b.sb('Bt', [128, 3, 4, 128], F32, s)
        with contextlib.ExitStack() as s2:
            tz = b.sb('tz', [128, 4, 128], F32, s2)
            for kind, delta in ((0, -384), (1, -128), (2, 0)):
                for h in range(4):
                    if delta <= -384:
                        src = bass.AP(tensor=relx.tensor, offset=int(relx[h, 385:386].offset), ap=[[0, 128], [1, 128]])
                    else:
                        src = bass.AP(tensor=relx.tensor, offset=int(relx[h, 0:1].offset) + (129 - delta), ap=[[1, 128], [1, 128]])
                    b.dma(tz[:, h, :], src)
                pz = P[0]
                b.mm(pz[:, 0:512], C['antiJ'][:], tz[:].rearrange("p a b -> p (a b)"))
                b.cp(Bt[:, kind, :, :].rearrange("p a b -> p (a b)"), pz[:, 0:512])
        KT = [b.sb('KT%d' % i, [128, 2, T if i < 2 else 1024], BF16, s) for i in range(3)]
        V = [b.sb('V%d' % i, [128, 16 if i < 2 else 8, 4, 65], BF16, s) for i in range(3)]
        for i in range(3):
            b.memset(V[i][:], 1.0)
        W = dict(win=win, wo=wo, gC=gC, gD=gD, gG=gG, mg=mg, dtb=dtb, nexpA=nexpA, fbf=fbf, Bt=Bt, KT=KT, V=V)
        alloc_work(b, W, s)
        import os
        for si in [int(c) for c in os.environ.get('KDBG_SEQS', '012')]:
            phaseAB(b, l, si, W, src_x)
    b.S.barrier()
    phaseC(b, l)
    b.S.barrier()


GP = 256
SKIP_D = False
OFFS = [1032, 1800, 2572]


def mid_bc(ap2d, n):
    a = [list(x) for x in ap2d.ap]
    return bass.AP(tensor=ap2d.tensor, offset=int(ap2d.offset), ap=[a[0], [0, n], a[1]])


def alloc_work(b, W, s):
    G = GP
    W['xg'] = b.sb('xg', [128, 8, G], F32, s)
    W['sqr'] = Rot([b.sb('sqr%d' % i, [128, G], BF16, s) for i in range(2)])
    W['rstd'] = b.sb('rstd', [128, G], F32, s)
    W['hT'] = b.sb('hT', [128, 8, G], BF16, s)
    W['QT'] = [b.sb('QT%d' % i, [128, 2, G], BF16, s) for i in range(3)]
    W['stg'] = Rot([b.sb('stg%d' % i, [128, 512], F32, s) for i in range(3)])
    W['qkb'] = Rot([b.sb('qkb%d' % i, [128, 512], BF16, s) for i in range(2)])
    W['nwk'] = (b.sb('nsq', [128, 8, 64], F32, s), b.sb('nss', [128, 8], F32, s), b.sb('ny', [128, 8, 64], F32, s))
    W['lf'] = b.sb('lf', [128, 4], F32, s)
    W['lfc'] = b.sb('lfc', [128, 8, 4], F32, s)
    W['negF'] = b.sb('negF', [128, 17, 4], F32, s)
    W['carry'] = b.sb('carry', [128, 4], F32, s)
    W['fbias'] = b.sb('fbias', [128, 17, 4], F32, s)
    W['mixed'] = b.sb('mixed', [128, G // 128, 1024], BF16, s)
    W['mixedT'] = b.sb('mixedT', [128, 8, G], BF16, s)
    X = [b.sb('X%d' % i, [128, 512], F32, s) for i in range(6)]
    W['X'] = X
    W['E'] = Rot([X[0], X[1]])
    W['zr'] = Rot([X[2], X[3]])
    W['Rr'] = [X[4], X[5]]
    W['htmp'] = Rot([X[1][:, 0:256], X[1][:, 256:512]])
    W['bt'] = Rot([X[4]])
    W['SP'] = Rot([b.sb('SP%d' % i, [128, 512], BF16, s) for i in range(2)])
    W['Wt'] = Rot([b.sb('Wt%d' % i, [128, 512], BF16, s) for i in range(3)])
    alloc_gdn(b, W, s)
    print('sbuf remaining after alloc', b.nc.sbuf_bytes_remaining)


def norm_mod(b, l, si, which, xg, G, W, hT):
    C, P = b.C, b.P
    ss = P[0][:, 0:G]
    for c in range(8):
        sq = W['sqr'].next()
        b.act(sq[:, 0:G], xg[:, c, 0:G], AF.Square)
        b.mm(ss, C['onesb'][:], sq[:, 0:G], start=(c == 0), stop=(c == 7))
    b.rsqrt(W['rstd'][:, 0:G], ss, 1.0 / D)
    for c in range(8):
        ht = W['htmp'].next()
        b.stt(ht[:, 0:G], xg[:, c, 0:G], b.modA[:, l, which, c, si:si + 1], W['rstd'][:, 0:G], ALU.mult, ALU.mult)
        b.act(hT[:, c, 0:G], ht[:, 0:G], AF.Identity, bias=b.mod[:, l, which * 24 + c, si:si + 1])


def phaseAB(b, l, si, W, src_x):
    nc = b.nc
    d, o, C, P, PB = b.d, b.o, b.C, b.P, b.PB
    prompt = si < 2
    Tq = T if prompt else TS
    tok0 = si * T if prompt else 2 * T
    G = GP if prompt else TS
    ngroups = Tq // G
    npast = 0 if prompt else PAST
    npastD = 0 if prompt else 512
    KT, V, win, wo, mg = W['KT'], W['V'], W['win'], W['wo'], W['mg']
    xg, hT, QT, mixed, mixedT = W['xg'], W['hT'], W['QT'], W['mixed'], W['mixedT']
    identb = C['identb']
    pbv = PB[:, 0:1024].rearrange("p (j t) -> p j t", t=128)
    pf = P[3]
    carry, negF, lf = W['carry'], W['negF'], W['lf']
    b.memset(carry[:], 0.0)
    b.memset(negF[:], 0.0)
    if not prompt:
        for i, (kn, vn, nrow) in enumerate((('sbk', 'sbv', 1024), ('fxk', 'fxv', 1024), ('bdk', 'bdv', 512))):
            for tl in range(nrow // 128):
                b.dma(V[i][:, tl, :, 0:64], d[vn][l, tl * 128:(tl + 1) * 128, :].rearrange("p (h e) -> p h e", e=64), q='pool')
                kb = W['qkb'].next()
                b.dma(kb[:, 0:256], d[kn][l, tl * 128:(tl + 1) * 128, :], q='pool')
                for j in range(2):
                    b.tr(pbv[:, j, :], kb[:, j * 128:(j + 1) * 128], identb[:])
                b.cp(KT[i][:, :, tl * 128:(tl + 1) * 128], pbv[:, 0:2, :])
        lfc = W['lfc']
        b.dma(lfc[:], d['fxlf'][l].rearrange("(t p) h -> p t h", p=128))
        for tl in range(8):
            b.mm(pf[:, 0:4], C['triIncf'][:], lfc[:, tl, :])
            b.stt(negF[:, tl, :], pf[:, 0:4], -1.0, carry[:], ALU.mult, ALU.subtract)
            b.mm(pf[:, 4:8], C['onesf'][:], lfc[:, tl, :])
            b.tt(carry[:], carry[:], pf[:, 4:8], ALU.add)
        b.dma(o['s_bdk'][l, 0:496, :], d['bdk'][l, 16:512, :])
        b.dma(o['s_bdv'][l, 0:496, :], d['bdv'][l, 16:512, :])
    gdn_seq_init(b, l, si, W)

    import os
    for g in range(min(ngroups, int(os.environ.get('KDBG_MAXG', '99')))):
        t0 = g * G
        b.dma(xg[:, :, 0:G], src_x[:, tok0 + t0:tok0 + t0 + G].rearrange("(c p) t -> p c t", p=128))
        norm_mod(b, l, si, 0, xg, G, W, hT)
        ntile = max(1, G // 128)
        nt = min(128, G)
        for tt in range(ntile):
            cols = slice(tt * 128, tt * 128 + nt)
            for i in range(3):
                c0 = OFFS[i]
                pqk, pv = P[1], P[2]
                nv = 260 if i == 1 else 256
                for c in range(8):
                    b.mm(pqk[0:nt, 0:512], hT[:, c, cols], win[:, c, c0:c0 + 512], start=(c == 0), stop=(c == 7))
                for c in range(8):
                    b.mm(pv[0:nt, 0:nv], hT[:, c, cols], win[:, c, c0 + 512:c0 + 512 + nv], start=(c == 0), stop=(c == 7))
                kbase = (npast if i < 2 else npastD) + t0 + tt * 128
                kt_i = kbase // 128
                if i == 2:
                    kbase = kbase % 1024
                    kt_i = kt_i % 8
                sv = W['stg'].next()
                b.cp(sv[0:nt, 0:256], pv[0:nt, 0:256], e='act')
                b.cp(V[i][0:nt, kt_i, :, 0:64], sv[0:nt, 0:256].rearrange("p (h e) -> p h e", e=64))
                sk = W['stg'].next()
                if i == 0:
                    b.cp(sk[0:nt, 0:512], pqk[0:nt, 0:512])
                else:
                    rmsnorm_tok(b, sk[0:nt, 0:512].rearrange("p (h e) -> p h e", e=64),
                                pqk[0:nt, 0:512].rearrange("p (h e) -> p h e", e=64), 8, nt,
                                W['gC'] if i == 1 else W['gD'], W['nwk'])
                tloc = t0 + tt * 128
                if prompt:
                    if i == 0:
                        b.dma(o['p_sbk'][l, si, tloc:tloc + nt, :], sk[0:nt, 256:512])
                        b.dma(o['p_sbv'][l, si, tloc:tloc + nt, :], sv[0:nt, 0:256])
                    elif i == 1:
                        b.dma(o['p_fxk'][l, si, tloc:tloc + nt, :], sk[0:nt, 256:512])
                        b.dma(o['p_fxv'][l, si, tloc:tloc + nt, :], sv[0:nt, 0:256])
                    elif tloc >= T - 512:
                        b.dma(o['p_bdk'][l, si, tloc - (T - 512):tloc - (T - 512) + nt, :], sk[0:nt, 256:512])
                        b.dma(o['p_bdv'][l, si, tloc - (T - 512):tloc - (T - 512) + nt, :], sv[0:nt, 0:256])
                else:
                    if i == 0:
                        b.dma(o['s_sbk'][l, 0:nt, :], sk[0:nt, 256:512])
                        b.dma(o['s_sbv'][l, 0:nt, :], sv[0:nt, 0:256])
                    elif i == 1:
                        b.dma(o['s_fxk'][l, 0:nt, :], sk[0:nt, 256:512])
                        b.dma(o['s_fxv'][l, 0:nt, :], sv[0:nt, 0:256])
                    else:
                        b.dma(o['s_bdk'][l, 496:512, :], sk[0:nt, 256:512])
                        b.dma(o['s_bdv'][l, 496:512, :], sv[0:nt, 0:256])
                qb = W['qkb'].next()
                b.cp(qb[0:nt, :], sk[0:nt, 0:512], e='act')
                for j in range(4):
                    b.tr(pbv[:, j, 0:nt], qb[0:nt, j * 128:(j + 1) * 128], identb[0:nt, 0:nt])
                b.cp(QT[i][:, :, tt * 128:tt * 128 + nt], pbv[:, 0:2, 0:nt])
                b.cp(KT[i][:, :, kbase:kbase + nt], pbv[:, 2:4, 0:nt], e='act')
                if i == 1:
                    b.tt(lf[0:nt, :], pv[0:nt, 256:260], W['fbf'][0:nt, :], ALU.add)
                    b.act(lf[0:nt, :], lf[0:nt, :], AF.Exp, scale=-1.0)
                    b.act(lf[0:nt, :], lf[0:nt, :], AF.Ln, bias=1.0)
                    b.ts(lf[0:nt, :], lf[0:nt, :], -1.0, ALU.mult)
                    if prompt:
                        b.dma(o['p_fxlf'][l, si, tloc:tloc + nt, :], lf[0:nt, :])
                    else:
                        b.dma(o['s_fxlf'][l, 0:nt, :], lf[0:nt, :])
                    b.mm(pf[0:nt, 0:4], C['triIncf'][0:nt, 0:nt], lf[0:nt, :])
                    b.stt(negF[0:nt, kt_i, :], pf[0:nt, 0:4], -1.0, carry[0:nt, :], ALU.mult, ALU.subtract)
                    b.mm(pf[:, 4:8], C['onesf'][0:nt, :], lf[0:nt, :])
                    b.tt(carry[:], carry[:], pf[:, 4:8], ALU.add)
        if b.stage < 2:
            continue
        nsb = ntile
        nq = nt
        qpos0 = npast + t0
        jmax = (qpos0 + G - 1) // 128
        Ops = P[4]
        zc = 0
        GW = 2 * G
        if b.stage >= 2 and not os.environ.get('KDBG_NOB'):
            for p in range(2):
                b.memset(W['Rr'][p][:, 0:GW], 0.0)
            firstB = True
            for j in range(jmax, -1, -1):
                nk = min(128, npast + Tq - j * 128)
                m = j * 128 - qpos0
                diag = m >= 0
                for p in range(2):
                    po = p * 64
                    Rr = W['Rr'][p]
                    pz = P[5 + zc % 2]
                    zc += 1
                    for hl in range(2):
                        b.mm(pz[0:nk, hl * G:(hl + 1) * G], KT[0][po:po + 64, hl, j * 128:j * 128 + nk], QT[0][po:po + 64, hl, 0:G])
                    E = W['E'].next()
                    b.act(E[0:nk, 0:GW], pz[0:nk, 0:GW], AF.Exp, scale=0.125)
                    sp = W['SP'].next()
                    b.act(sp[0:nk, 0:GW], E[0:nk, 0:GW], AF.Ln, bias=1.0)
                    if diag:
                        spv = sp[0:nk, 0:GW].rearrange("p (a q) -> p a q", a=2)
                        b.asel(spv, spv, [[0, 2], [1, G]], ALU.is_gt, 0.0, -m, -1)
                    b.mm(P[3][0:nk, 0:GW], C['triUb'][0:nk, 0:nk], sp[0:nk, 0:GW])
                    if j > 0:
                        b.mm(P[2][:, 0:GW], C['onesb'][0:nk, :], sp[0:nk, 0:GW])
                    zr = W['zr'].next()
                    b.act(zr[0:nk, 0:GW], pz[0:nk, 0:GW], AF.Identity, scale=0.125)
                    b.tt(zr[0:nk, 0:GW], zr[0:nk, 0:GW], Rr[0:nk, 0:GW], ALU.subtract)
                    b.tt(zr[0:nk, 0:GW], zr[0:nk, 0:GW], P[3][0:nk, 0:GW], ALU.subtract)
                    wt = W['Wt'].next()
                    b.act(wt[0:nk, 0:GW], zr[0:nk, 0:GW], AF.Exp)
                    if diag:
                        wtv = wt[0:nk, 0:GW].rearrange("p (a q) -> p a q", a=2)
                        b.asel(wtv, wtv, [[0, 2], [1, G]], ALU.is_gt, 0.0, -m, -1)
                    if j > 0:
                        b.tt(Rr[:, 0:GW], Rr[:, 0:GW], P[2][:, 0:GW], ALU.add)
                    for hl in range(2):
                        h = p + 2 * hl
                        for sb_ in range(nsb):
                            if diag and m >= sb_ * 128 + nq:
                                continue
                            oc0 = ((p * 2 + hl) * nsb + sb_) * 64
                            b.mm(Ops[0:nq, oc0:oc0 + 64], wt[0:nk, hl * G + sb_ * 128:hl * G + sb_ * 128 + nq],
                                 V[0][0:nk, j, h, 0:64], start=firstB, stop=(j == 0), skip_group_check=True)
                            firstB = False
            for p in range(2):
                for hl in range(2):
                    h = p + 2 * hl
                    oc0 = (p * 2 + hl) * nsb * 64
                    rmsnorm_tok(b, mixed[0:nq, 0:nsb, 256 + h * 64:256 + (h + 1) * 64],
                                Ops[0:nq, oc0:oc0 + nsb * 64].rearrange("p (s e) -> p s e", e=64), nsb, nq,
                                mid_bc(mg[:, h, :], nsb), W['nwk'])
        nj = jmax + 1
        b.tt(W['fbias'][:, 0:nj, :], negF[:, 0:nj, :], mid_bc(carry[:, :], nj), ALU.add)
        FQ = W['X'][0]
        for h in range(4 if b.stage >= 3 else 0):
            hp, po = h // 2, (h % 2) * 64
            first = True
            pfq = P[3]
            dg = W['nwk'][0][:, 0:2, :].rearrange("p a e -> p (a e)")
            for sb_ in range(nsb):
                jq_ = qpos0 // 128 + sb_
                b.ts(dg[0:nq, 0:nq], C['identf'][0:nq, 0:nq], W['fbias'][0:nq, jq_, h:h + 1], ALU.mult)
                b.mm(pfq[:, sb_ * 128:sb_ * 128 + nq], C['onesf'][0:nq, :], dg[0:nq, 0:nq])
            b.cp(FQ[:, 0:G], pfq[:, 0:G])
            for j in range(0, jmax + 1):
                nk = min(128, npast + Tq - j * 128)
                m = j * 128 - qpos0
                diag = m >= 0
                pz = P[5 + zc % 2]
                zc += 1
                b.mm(pz[0:nk, 0:G], KT[1][po:po + 64, hp, j * 128:j * 128 + nk], QT[1][po:po + 64, hp, 0:G])
                wt = W['Wt'].next()
                s_ = W['zr'].next()
                b.stt(s_[0:nk, 0:G], pz[0:nk, 0:G], 0.125, FQ[0:nk, 0:G], ALU.mult, ALU.subtract)
                if diag:
                    b.ts(s_[0:nk, 0:G], s_[0:nk, 0:G], W['fbias'][0:nk, j, h:h + 1], ALU.add, 30.0, ALU.min)
                    b.act(wt[0:nk, 0:G], s_[0:nk, 0:G], AF.Exp)
                    b.asel(wt[0:nk, 0:G], wt[0:nk, 0:G], [[1, G]], ALU.is_ge, 0.0, -m, -1)
                else:
                    b.act(wt[0:nk, 0:G], s_[0:nk, 0:G], AF.Exp, bias=W['fbias'][0:nk, j, h:h + 1])
                for sb_ in range(nsb):
                    if diag and m >= sb_ * 128 + nq:
                        continue
                    b.mm(Ops[0:nq, sb_ * 65:(sb_ + 1) * 65], wt[0:nk, sb_ * 128:sb_ * 128 + nq], V[1][0:nk, j, h, 0:65],
                         start=first, stop=(j == jmax), skip_group_check=True)
                    first = False
            ov = Ops[0:nq, 0:nsb * 65].rearrange("p (s e) -> p s e", e=65)
            rmsnorm_tok(b, mixed[0:nq, 0:nsb, 512 + h * 64:512 + (h + 1) * 64], ov[:, :, 0:64], nsb, nq,
                        mid_bc(mg[:, 4 + h, :], nsb), W['nwk'], den=ov[:, :, 64])
        if SKIP_D and b.stage >= 5:
            b.memset(mixed[:, :, 768:1024], 0.0)
        for tq in range(ntile if (b.stage >= 4 and not SKIP_D) else 0):
            qk0 = npastD + t0 + tq * 128
            jq = qk0 // 128
            tiles = list(range(max(0, jq - 4), jq + 1))
            first = True
            for j in tiles:
                nk = min(128, npastD + Tq - j * 128)
                kind = 2 if j == jq else (1 if j == jq - 1 else 0)
                bt = W['bt'].next()
                btv = bt[0:nk, 0:512].rearrange("p (h q) -> p h q", q=128)[:, :, 0:nq]
                for h in range(4):
                    hp, po = h // 2, (h % 2) * 64
                    b.mm(P[5 + h % 2][0:nk, hp * 128:hp * 128 + nq], KT[2][po:po + 64, hp, (j % 8) * 128:(j % 8) * 128 + nk],
                         QT[2][po:po + 64, hp, tq * 128:tq * 128 + nq])
                for par in range(2):
                    src = P[5 + par][0:nk, 0:256].rearrange("p (a q) -> p a q", q=128)[:, :, 0:nq]
                    dst = bt[0:nk, 0:512].rearrange("p (a c q) -> p a c q", a=2, c=2, q=128)[:, :, par, 0:nq]
                    b.act(dst, src, AF.Identity, scale=0.125)
                b.tt(btv, btv, W['Bt'][0:nk, kind, :, 0:nq], ALU.add)
                wt = W['Wt'].next()
                wtv = wt[0:nk, 0:512].rearrange("p (h q) -> p h q", q=128)[:, :, 0:nq]
                b.act(wtv, btv, AF.Exp)
                if prompt and j == jq:
                    b.tt(wtv, wtv, C['mask4'][0:nk, :, 0:nq], ALU.mult)
                if prompt and j == jq - 4:
                    b.tt(wtv, wtv, C['mask0'][0:nk, :, 0:nq], ALU.mult)
                for h in range(4):
                    b.mm(Ops[0:nq, h * 65:(h + 1) * 65], wt[0:nk, h * 128:h * 128 + nq], V[2][0:nk, j % 8, h, 0:65],
                         start=first, stop=(j == tiles[-1]), skip_group_check=True)
                    first = False
            ov = Ops[0:nq, 0:260].rearrange("p (s e) -> p s e", e=65)
            rmsnorm_tok(b, mixed[0:nq, tq, 768:1024].rearrange("p (h e) -> p h e", e=64), ov[:, :, 0:64], 4, nq,
                        mg[:, 8:12, :], W['nwk'], den=ov[:, :, 64])
        gdn_group(b, l, si, W, g, G, t0)
        if b.stage < 5:
            continue
        for tt in range(ntile):
            for cc in range(2, 8):
                b.tr(pbv[:, cc, 0:nt], mixed[0:nt, tt, cc * 128:(cc + 1) * 128], identb[0:nt, 0:nt])
            b.cp(mixedT[:, 2:8, tt * 128:tt * 128 + nt], pbv[:, 2:8, 0:nt])
        for oc in range(8):
            pso = P[oc % 2]
            for c in range(8):
                b.mm(pso[:, 0:G], wo[:, c, oc * 128:(oc + 1) * 128], mixedT[:, c, 0:G], start=(c == 0), stop=(c == 7))
            b.stt(xg[:, oc, 0:G], pso[:, 0:G], b.mod[:, l, 16 + oc, si:si + 1], xg[:, oc, 0:G], ALU.mult, ALU.add)
        b.dma(o['yT'][:, tok0 + t0:tok0 + t0 + G].rearrange("(c p) t -> p c t", p=128), xg[:, :, 0:G])


def phaseC(b, l):
    nc = b.nc
    d, o, C, P = b.d, b.o, b.C, b.P
    if b.stage < 6:
        return
    with contextlib.ExitStack() as s:
        wup = b.sb('wup', [128, 8, 2 * DFF], BF16, s)
        wdn = b.sb('wdn', [128, 22, D], BF16, s)
        for c in range(8):
            for q4 in range(4):
                b.dma(wup[:, c, q4 * 1408:(q4 + 1) * 1408], d['w_up'][l, c * 128:(c + 1) * 128, q4 * 1408:(q4 + 1) * 1408], q='pool')
        for j in range(22):
            b.dma(wdn[:, j, :], d['w_down'][l, j * 128:(j + 1) * 128, :], q='pool')
        GC = 512
        W = {}
        xgr = Rot([b.sb('cxg%d' % i, [128, 8, GC], F32, s) for i in range(1)])
        W['sqr'] = Rot([b.sb('csqr%d' % i, [128, GC], BF16, s) for i in range(2)])
        W['rstd'] = b.sb('crstd', [128, GC], F32, s)
        h2r = Rot([b.sb('h2T%d' % i, [128, 8, GC], BF16, s) for i in range(1)])
        ur = Rot([b.sb('ur%d' % i, [128, GC + 2], F32, s) for i in range(2)])
        ctr = Rot([b.sb('ctr%d' % i, [128, GC], F32, s) for i in range(2)])
        W['htmp'] = ctr
        hist = b.sb('hist', [128, 44, 2], F32, s)
        actr = Rot([b.sb('actT%d' % i, [128, 22, GC], BF16, s) for i in range(1)])
        for_ = Rot([b.sb('fo%d' % i, [2, 512], F32, s) for i in range(2)])
        fcw = b.fcw
        pc_ = 0
        for si in range(3):
            prompt = si < 2
            Tq = T if prompt else TS
            tok0 = si * T if prompt else 2 * T
            G = GC if prompt else TS
            ngroups = Tq // G
            if prompt:
                b.memset(hist[:], 0.0)
            else:
                b.dma(hist[:], d['fconvT'][l])
            for g in range(ngroups):
                t0 = g * G
                xg, h2T, actT = xgr.next(), h2r.next(), actr.next()
                ycols = o['yT'][:, tok0 + t0:tok0 + t0 + G].rearrange("(c p) t -> p c t", p=128)
                b.dma(xg[:, :, 0:G], ycols)
                norm_mod(b, l, si, 1, xg, G, W, h2T)
                for j in range(22):
                    cv = []
                    for half in range(2):
                        jj = half * 22 + j
                        colb = half * DFF + j * 128
                        pu = P[1 + pc_ % 4]
                        pc_ += 1
                        for c in range(8):
                            b.mm(pu[:, 0:G], wup[:, c, colb:colb + 128], h2T[:, c, 0:G], start=(c == 0), stop=(c == 7))
                        u = ur.next()
                        b.cp(u[:, 0:2], hist[:, jj, :], e='act')
                        b.cp(u[:, 2:2 + G], pu[:, 0:G], e='act')
                        b.cp(hist[:, jj, :], u[:, G:G + 2], e='act')
                        ct = ctr.next()
                        b.act(ct[:, 0:G], u[:, 0:G], AF.Identity, scale=fcw[:, l, jj, 0:1])
                        b.stt(ct[:, 0:G], u[:, 1:1 + G], fcw[:, l, jj, 1:2], ct[:, 0:G], ALU.mult, ALU.add)
                        b.stt(ct[:, 0:G], u[:, 2:2 + G], fcw[:, l, jj, 2:3], ct[:, 0:G], ALU.mult, ALU.add)
                        cv.append(ct)
                    sil = ctr.next()
                    b.act(sil[:, 0:G], cv[0][:, 0:G], AF.Silu)
                    b.tt(actT[:, j, 0:G], sil[:, 0:G], cv[1][:, 0:G], ALU.mult)
                for oc in range(8):
                    pd = P[5 + oc % 2]
                    for j in range(22):
                        b.mm(pd[:, 0:G], wdn[:, j, oc * 128:(oc + 1) * 128], actT[:, j, 0:G], start=(j == 0), stop=(j == 21))
                    b.stt(xg[:, oc, 0:G], pd[:, 0:G], b.mod[:, l, 40 + oc, si:si + 1], xg[:, oc, 0:G], ALU.mult, ALU.add)
                b.dma(ycols, xg[:, :, 0:G])
                if g == ngroups - 1:
                    for q11 in range(11):
                        pcx = P[0]
                        for c in range(8):
                            b.mm(pcx[0:2, 0:512], h2T[:, c, G - 2:G], wup[:, c, q11 * 512:(q11 + 1) * 512], start=(c == 0), stop=(c == 7))
                        fo = for_.next()
                        b.cp(fo[0:2, 0:512], pcx[0:2, 0:512])
                        if prompt:
                            b.dma(o['p_fconv'][l, si, :, q11 * 512:(q11 + 1) * 512], fo[:])
                        else:
                            b.dma(o['s_fconv'][l, :, q11 * 512:(q11 + 1) * 512], fo[:])


def alloc_gdn(b, W, s):
    G = GP
    W['ghist'] = b.sb('ghist', [64, 12, 3], F32, s)
    W['pst'] = Rot([b.sb('pst%d' % i, [64, G + 3], F32, s) for i in range(2)])
    W['cv'] = Rot([b.sb('cv%d' % i, [64, G], F32, s) for i in range(2)])
    W['gsq'] = b.sb('gsq', [64, G], BF16, s)
    W['grn'] = b.sb('grn', [64, G], F32, s)
    W['qkvn'] = b.sb('qkvn', [64, 12, G], BF16, s)
    W['Sf'] = b.sb('Sf', [64, 4, 64], F32, s)
    W['Sb'] = b.sb('Sb', [64, 4, 64], BF16, s)
    W['g3'] = b.sb('g3', [3, 768], F32, s)
    v64 = lambda t, c0: t[0:64, c0:c0 + 256].rearrange("p (h e) -> p h e", e=64)
    hosts32 = [(W['X'][i], c0) for i in range(4) for c0 in (0, 256)]
    f32n = ['zsil', 'dec', 'decI', 'decS', 'Uf', 'gb', 'vtok', 'of']
    for n, (t, c0) in zip(f32n, hosts32):
        W[n] = v64(t, c0)
    W['on'] = b.sb('on', [64, 4, 64], F32, s)[:]
    hosts16 = [(W['Wt'].bufs[i], c0) for i in range(3) for c0 in (0, 256)] + [(W['SP'].bufs[i], 0) for i in range(2)] + \
              [(W['qkb'].bufs[i], c0) for i in range(2) for c0 in (0, 256)]
    for n, (t, c0) in zip(['ktok', 'kd', 'vn', 'AT', 'om'], hosts16):
        W[n] = v64(t, c0)
    hosts32b = [(W['stg'].bufs[i], c0) for i in range(3) for c0 in (0, 256)]
    for n, (t, c0) in zip(['Pa', 'Qa', 'Qb', 'Ya', 'Yb'], hosts32b):
        W[n] = v64(t, c0)
    W['gsm'] = b.sb('gsm', [64, 10, 4], F32, s)


def gdn_seq_init(b, l, si, W):
    d = b.d
    if si < 2:
        b.memset(W['ghist'][:], 0.0)
        b.memset(W['Sf'][:], 0.0)
        b.memset(W['Sb'][:], 0.0)
    else:
        b.dma(W['ghist'][:], d['gconvT'][l])
        b.dma(W['Sf'][:], d['gstate'][l].rearrange("h k v -> k h v"))
        b.cp(W['Sb'][:], W['Sf'][:])


def gdn_group(b, l, si, W, g, G, t0):
    import os
    if os.environ.get('KDBG_NOGDN'):
        b.memset(W['mixedT'][:, 0:2, 0:G], 0.0)
        return
    nc = b.nc
    d, o, C, P, PB = b.d, b.o, b.C, b.P, b.PB
    prompt = si < 2
    Tq = T if prompt else TS
    win, hT = W['win'], W['hT']
    qkvn = W['qkvn']
    gcw = b.gcw
    for hc in range(12):
        pg = P[hc % 2]
        for c in range(8):
            b.mm(pg[0:64, 0:G], win[:, c, hc * 64:(hc + 1) * 64], hT[:, c, 0:G], start=(c == 0), stop=(c == 7))
        pst = W['pst'].next()
        b.cp(pst[:, 0:3], W['ghist'][:, hc, :], e='act')
        b.cp(pst[:, 3:3 + G], pg[0:64, 0:G], e='act')
        b.cp(W['ghist'][:, hc, :], pst[:, G:G + 3], e='act')
        cv = W['cv'].next()
        b.ts(cv[:, 0:G], pst[:, 0:G], gcw[:, l, hc, 0:1], ALU.mult)
        for i in range(1, 4):
            b.stt(cv[:, 0:G], pst[:, i:i + G], gcw[:, l, hc, i:i + 1], cv[:, 0:G], ALU.mult, ALU.add)
        b.act(cv[:, 0:G], cv[:, 0:G], AF.Silu)
        if hc < 8:
            b.act(W['gsq'][:, 0:G], cv[:, 0:G], AF.Square)
            pn = P[2]
            b.mm(pn[0:64, 0:G], C['onesb'][0:64, 0:64], W['gsq'][:, 0:G])
            b.rsqrt(W['grn'][:, 0:G], pn[0:64, 0:G], 1.0)
            if hc < 4:
                b.stt(qkvn[:, hc, 0:G], cv[:, 0:G], 0.125, W['grn'][:, 0:G], ALU.mult, ALU.mult)
            else:
                b.tt(qkvn[:, hc, 0:G], cv[:, 0:G], W['grn'][:, 0:G], ALU.mult)
        else:
            b.cp(qkvn[:, hc, 0:G], cv[:, 0:G], e='act')
    if t0 + G == Tq:
        p3 = P[3]
        for hc in range(8):
            b.tr(p3[0:3, hc * 64:(hc + 1) * 64], W['ghist'][:, hc, :], C['identf'][0:64, 0:64])
        b.cp(W['g3'][:, 0:512], p3[0:3, 0:512])
        p3b = P[2]
        for hc in range(8, 12):
            b.tr(p3b[0:3, (hc - 8) * 64:(hc - 7) * 64], W['ghist'][:, hc, :], C['identf'][0:64, 0:64])
        b.cp(W['g3'][:, 512:768], p3b[0:3, 0:256])
        if prompt:
            b.dma(o['p_gconv'][l, si, :, :], W['g3'][:])
        else:
            b.dma(o['s_gconv'][l, :, :], W['g3'][:])
    Lc = min(64, G)
    nlev = 5 if Lc == 64 else 3
    gsm = W['gsm']
    v3 = lambda t: t[0:Lc, :, 0:Lc]
    pv3 = lambda p, c0: p[0:Lc, c0:c0 + 256].rearrange("p (h e) -> p h e", e=64)
    tri = C['triIncf'][0:Lc, 0:Lc]
    for ch in range(G // Lc):
        cols = slice(ch * Lc, (ch + 1) * Lc)
        pzb = P[2]
        for c in range(8):
            b.mm(pzb[0:Lc, 0:264], hT[:, c, cols], win[:, c, 768:1032], start=(c == 0), stop=(c == 7))
        beta, gg, Gc, eG, neG, eGl, ekd, tmp4 = (gsm[:, i, :] for i in range(8))
        Sf, Sb = W['Sf'], W['Sb']
        pS = P[0]
        for h in range(4):
            b.mm(pS[0:Lc, h * 64:(h + 1) * 64], qkvn[:, 4 + h, cols], Sb[:, h, :])
        for h in range(4):
            b.mm(pS[0:Lc, 256 + h * 64:256 + (h + 1) * 64], qkvn[:, h, cols], Sb[:, h, :])
        pbk = PB[:, 0:512].rearrange("p (a h e) -> p a h e", a=2, e=64)
        for h in range(4):
            b.tr(pbk[0:Lc, 0, h, :], qkvn[:, 4 + h, cols], C['identb'][0:64, 0:64])
            b.tr(pbk[0:Lc, 1, h, :], qkvn[:, 8 + h, cols], C['identb'][0:64, 0:64])
        ktok, vtok, kd = W['ktok'], W['vtok'], W['kd']
        b.cp(ktok[0:Lc, :, :], pbk[0:Lc, 0, :, :])
        b.cp(vtok[0:Lc, :, :], pbk[0:Lc, 1, :, :])
        pKK = P[4]
        for h in range(4):
            b.mm(pKK[0:Lc, h * 64:h * 64 + Lc], qkvn[:, 4 + h, cols], qkvn[:, 4 + h, cols])
        for h in range(4):
            b.mm(pKK[0:Lc, 256 + h * 64:256 + h * 64 + Lc], qkvn[:, 4 + h, cols], qkvn[:, h, cols])
        zsil = W['zsil']
        pzv = pzb[0:Lc, 0:256].rearrange("p (h e) -> p h e", e=64)
        b.act(zsil[0:Lc, :, :], pzv, AF.Exp, scale=-1.0)
        b.act(beta[0:Lc, :], pzb[0:Lc, 256:260], AF.Exp, scale=-1.0)
        b.ts(zsil[0:Lc, :, :], zsil[0:Lc, :, :], 1.0, ALU.add)
        b.recip(zsil[0:Lc, :, :], zsil[0:Lc, :, :])
        b.tt(zsil[0:Lc, :, :], zsil[0:Lc, :, :], pzv, ALU.mult)
        b.ts(beta[0:Lc, :], beta[0:Lc, :], 1.0, ALU.add)
        b.recip(beta[0:Lc, :], beta[0:Lc, :])
        b.tt(tmp4[0:Lc, :], pzb[0:Lc, 260:264], W['dtb'][0:Lc, :], ALU.add)
        b.act(tmp4[0:Lc, :], tmp4[0:Lc, :], AF.Exp)
        b.act(tmp4[0:Lc, :], tmp4[0:Lc, :], AF.Ln, bias=1.0)
        b.tt(gg[0:Lc, :], tmp4[0:Lc, :], W['nexpA'][0:Lc, :], ALU.mult)
        pG = P[3]
        b.mm(pG[0:Lc, 0:4], tri, gg[0:Lc, :])
        b.mm(pG[0:64, 4:8], C['onesf'][0:Lc, 0:64], gg[0:Lc, :])
        b.cp(Gc[0:Lc, :], pG[0:Lc, 0:4])
        b.act(eGl[:, :], pG[0:64, 4:8], AF.Exp)
        b.tt(ekd[0:Lc, :], pG[0:Lc, 4:8], Gc[0:Lc, :], ALU.subtract)
        b.act(ekd[0:Lc, :], ekd[0:Lc, :], AF.Exp)
        b.act(eG[0:Lc, :], Gc[0:Lc, :], AF.Exp)
        b.tt(W['kd'][0:Lc, :, :], W['ktok'][0:Lc, :, :], bc(ekd[0:Lc, :], [Lc, 4, 64]), ALU.mult)
        b.ts(neG[0:Lc, :], eG[0:Lc, :], -1.0, ALU.mult)
        gb = W['gb']
        for h in range(4):
            b.ts(gb[0:Lc, h, 0:Lc], C['onesf'][0:Lc, 0:Lc], gg[0:Lc, h:h + 1], ALU.mult)
        pGr = P[3]
        for h in range(4):
            b.mm(pGr[0:Lc, 256 + h * 64:256 + h * 64 + Lc], gb[0:Lc, h, 0:Lc], tri)
        dec, decI, decS = W['dec'], W['decI'], W['decS']
        for h in range(4):
            b.ts(dec[0:Lc, h, 0:Lc], pGr[0:Lc, 256 + h * 64:256 + h * 64 + Lc], Gc[0:Lc, h:h + 1], ALU.subtract, 0.0, ALU.min)
        b.act(v3(dec), v3(dec), AF.Exp)
        b.tt(v3(decI), v3(dec), mid_bc(tri, 4), ALU.mult)
        b.tt(v3(decS), v3(dec), mid_bc(C['sutf'][0:Lc, 0:Lc], 4), ALU.mult)
        Uf = W['Uf']
        b.tt(v3(Uf), pv3(pKK, 0)[:, :, 0:Lc], v3(decS), ALU.mult)
        b.tt(v3(Uf), v3(Uf), bc(beta[0:Lc, :], [Lc, 4, Lc]), ALU.mult)
        AT = W['AT']
        b.tt(v3(AT), pv3(pKK, 256)[:, :, 0:Lc], v3(decI), ALU.mult)
        Pc, Pn_, Qc, Qn_, Yc, Yn_ = Uf, W['Pa'], W['Qa'], W['Qb'], W['Ya'], W['Yb']
        pq = P[5]
        for h in range(4):
            b.tr(pq[0:Lc, h * 64:h * 64 + Lc], Uf[0:Lc, h, 0:Lc], C['identf'][0:Lc, 0:Lc])
        b.cp(v3(Qc), pv3(pq, 0)[:, :, 0:Lc])
        b.tt(v3(Yc), mid_bc(C['identf'][0:Lc, 0:Lc], 4), v3(Uf), ALU.subtract)
        for k in range(1, nlev + 1):
            pI = P[5]
            for h in range(4):
                b.mm(pI[0:Lc, h * 64:h * 64 + Lc], Pc[0:Lc, h, 0:Lc], Qc[0:Lc, h, 0:Lc])
            if k < nlev:
                for h in range(4):
                    b.mm(pI[0:Lc, 256 + h * 64:256 + h * 64 + Lc], Qc[0:Lc, h, 0:Lc], Pc[0:Lc, h, 0:Lc])
            b.cp(v3(Qn_), pv3(pI, 0)[:, :, 0:Lc], e='act')
            if k < nlev:
                b.cp(v3(Pn_), pv3(pI, 256)[:, :, 0:Lc])
            pY = P[6]
            if k > 1:
                for h in range(4):
                    b.mm(pY[0:Lc, h * 64:h * 64 + Lc], Qc[0:Lc, h, 0:Lc], Yc[0:Lc, h, 0:Lc])
                b.tt(v3(Yn_), v3(Yc), pv3(pY, 0)[:, :, 0:Lc], ALU.add)
                Yc, Yn_ = Yn_, Yc
            Pc, Pn_ = Pn_, Pc
            Qc, Qn_ = Qn_, Qc
        pY = P[6]
        for h in range(4):
            b.mm(pY[0:Lc, h * 64:h * 64 + Lc], Qc[0:Lc, h, 0:Lc], Yc[0:Lc, h, 0:Lc])
        b.tt(v3(Yn_), v3(Yc), pv3(pY, 0)[:, :, 0:Lc], ALU.add)
        Yc, Yn_ = Yn_, Yc
        Rm, vn, of = W['gb'], W['vn'], W['of']
        b.tt(of[0:Lc, :, :], pv3(pS, 0), bc(neG[0:Lc, :], [Lc, 4, 64]), ALU.mult)
        b.tt(Rm[0:Lc, :, :], of[0:Lc, :, :], vtok[0:Lc, :, :], ALU.add)
        b.tt(of[0:Lc, :, :], pv3(pS, 256), bc(eG[0:Lc, :], [Lc, 4, 64]), ALU.mult)
        pX = P[1]
        for h in range(4):
            b.mm(pX[0:Lc, h * 64:(h + 1) * 64], Yc[0:Lc, h, 0:Lc], Rm[0:Lc, h, :])
        b.tt(vn[0:Lc, :, :], pv3(pX, 0), bc(beta[0:Lc, :], [Lc, 4, 64]), ALU.mult)
        for h in range(4):
            b.mm(pX[0:Lc, 256 + h * 64:256 + (h + 1) * 64], AT[0:Lc, h, 0:Lc], vn[0:Lc, h, :])
        b.tt(of[0:Lc, :, :], of[0:Lc, :, :], pv3(pX, 256), ALU.add)
        pSn = P[4]
        for h in range(4):
            b.mm(pSn[0:64, h * 64:(h + 1) * 64], kd[0:Lc, h, :], vn[0:Lc, h, :])
        b.tt(Sf[:, :, :], Sf[:, :, :], bc(eGl[:, :], [64, 4, 64]), ALU.mult)
        b.tt(Sf[:, :, :], Sf[:, :, :], pSn[0:64, 0:256].rearrange("p (h e) -> p h e", e=64), ALU.add)
        b.cp(Sb[:, :, :], Sf[:, :, :], e='act')
        on, om = W['on'], W['om']
        rmsnorm_tok(b, on[0:Lc, :, :], of[0:Lc, :, :], 4, Lc, W['gG'], W['nwk'])
        b.tt(om[0:Lc, :, :], on[0:Lc, :, :], zsil[0:Lc, :, :], ALU.mult)
        pm = PB[:, 768:1024].rearrange("p (a t) -> p a t", t=128)
        omf = om[0:Lc, :, :].rearrange("p h e -> p (h e)")
        for cc in range(2):
            b.tr(pm[:, cc, 0:Lc], omf[:, cc * 128:(cc + 1) * 128], C['identb'][0:Lc, 0:Lc])
        b.cp(W['mixedT'][:, 0:2, ch * Lc:(ch + 1) * Lc], pm[:, 0:2, 0:Lc])
    if t0 + G == Tq:
        if prompt:
            b.dma(o['p_gstate'][l, si].rearrange("h k v -> k h v"), W['Sf'][:])
        else:
            b.dma(o['s_gstate'][l].rearrange("h k v -> k h v"), W['Sf'][:])


_NC_CACHE = {}


def _prep_inputs(inp):
    f = lambda a: np.ascontiguousarray(np.asarray(a, dtype=np.float32))
    L = DEPTH
    shared = {
        'ada_w': f(inp['ada_w']),
        'ada_bT': f(np.asarray(inp['ada_b']).reshape(L, 48, 128).transpose(2, 0, 1)),
        'nmgT': f(np.asarray(inp['norm_mix_g']).reshape(L, 8, 128).transpose(2, 0, 1)),
        'nfgT': f(np.asarray(inp['norm_ffn_g']).reshape(L, 8, 128).transpose(2, 0, 1)),
        'w_in': f(inp['w_in']),
        'gcwT': f(np.asarray(inp['gdn_conv_w']).reshape(L, 4, 12, 64).transpose(3, 0, 2, 1)),
        'a_log': f(inp['gdn_a_log']), 'dt_bias': f(inp['gdn_dt_bias']), 'gn_g': f(inp['gdn_norm_g']),
        'fq_g': f(inp['fox_q_g']), 'fk_g': f(inp['fox_k_g']), 'fb_f': f(inp['fox_b_f']),
        'bq_g': f(inp['band_q_g']), 'bk_g': f(inp['band_k_g']), 'rel': f(inp['band_rel_bias']),
        'mg': f(np.asarray(inp['merge_g']).reshape(L, 768)),
        'w_o': f(inp['w_o']), 'w_up': f(inp['w_up']),
        'fcwT': f(np.asarray(inp['ffn_conv_w']).reshape(L, 3, 44, 128).transpose(3, 0, 2, 1)),
        'w_down': f(inp['w_down']),
    }
    xp = np.asarray(inp['x_prompt'], dtype=np.float32)
    xs = np.asarray(inp['x_sample'], dtype=np.float32)
    cp_ = np.asarray(inp['c_prompt'], dtype=np.float32)
    cs = np.asarray(inp['c_sample'], dtype=np.float32)
    maps = []
    for k in range(8):
        m = dict(shared)
        xT = np.empty((D, NTOK), np.float32)
        xT[:, 0:T] = xp[2 * k].T
        xT[:, T:2 * T] = xp[2 * k + 1].T
        xT[:, 2 * T:] = xs[k].T
        m['xT'] = xT
        cc = np.zeros((4, D), np.float32)
        cc[0], cc[1], cc[2] = cp_[2 * k], cp_[2 * k + 1], cs[k]
        m['cT'] = f(cc.reshape(4, 8, 128).transpose(2, 1, 0))
        m['gconvT'] = f(np.asarray(inp['state_gdn_conv'])[:, k].reshape(L, 3, 12, 64).transpose(0, 3, 2, 1))
        m['gstate'] = f(np.asarray(inp['state_gdn'])[:, k])
        m['sbk'] = f(np.asarray(inp['cache_sb_k'])[:, k].reshape(L, PAST, 256))
        m['sbv'] = f(np.asarray(inp['cache_sb_v'])[:, k].reshape(L, PAST, 256))
        m['fxk'] = f(np.asarray(inp['cache_fox_k'])[:, k].reshape(L, PAST, 256))
        m['fxv'] = f(np.asarray(inp['cache_fox_v'])[:, k].reshape(L, PAST, 256))
        m['fxlf'] = f(np.asarray(inp['cache_fox_logf'])[:, k])
        m['bdk'] = f(np.asarray(inp['cache_band_k'])[:, k].reshape(L, 512, 256))
        m['bdv'] = f(np.asarray(inp['cache_band_v'])[:, k].reshape(L, 512, 256))
        m['fconvT'] = f(np.asarray(inp['state_ffn_conv'])[:, k].reshape(L, 2, 44, 128).transpose(0, 3, 2, 1))
        maps.append(m)
    return maps


def _assemble(res):
    L = DEPTH
    r = res
    yp = np.empty((16, T, D), np.float32)
    ys = np.empty((8, TS, D), np.float32)
    for k in range(8):
        yT = r[k]['yT']
        yp[2 * k] = yT[:, 0:T].T
        yp[2 * k + 1] = yT[:, T:2 * T].T
        ys[k] = yT[:, 2 * T:].T
    cat_p = lambda n, shp: np.concatenate([r[k][n] for k in range(8)], axis=1).reshape(shp)
    stk_s = lambda n, shp: np.stack([r[k][n] for k in range(8)], axis=1).reshape(shp)
    outs = [yp, ys,
            cat_p('p_gconv', (L, 16, 3, 768)), cat_p('p_gstate', (L, 16, 4, 64, 64)),
            cat_p('p_sbk', (L, 16, T, 4, 64)), cat_p('p_sbv', (L, 16, T, 4, 64)),
            cat_p('p_fxk', (L, 16, T, 4, 64)), cat_p('p_fxv', (L, 16, T, 4, 64)),
            cat_p('p_fxlf', (L, 16, T, 4)),
            cat_p('p_bdk', (L, 16, 512, 4, 64)), cat_p('p_bdv', (L, 16, 512, 4, 64)),
            cat_p('p_fconv', (L, 16, 2, 2 * DFF)),
            stk_s('s_gconv', (L, 8, 3, 768)), stk_s('s_gstate', (L, 8, 4, 64, 64)),
            stk_s('s_sbk', (L, 8, TS, 4, 64)), stk_s('s_sbv', (L, 8, TS, 4, 64)),
            stk_s('s_fxk', (L, 8, TS, 4, 64)), stk_s('s_fxv', (L, 8, TS, 4, 64)),
            stk_s('s_fxlf', (L, 8, TS, 4)),
            stk_s('s_bdk', (L, 8, 512, 4, 64)), stk_s('s_bdv', (L, 8, 512, 4, 64)),
            stk_s('s_fconv', (L, 8, 2, 2 * DFF))]
    return tuple(np.ascontiguousarray(a, dtype=np.float32) for a in outs)


def kernel(**inputs):
    maps = _prep_inputs(inputs)
    if 'nc' not in _NC_CACHE:
        _NC_CACHE['nc'] = build()
    res = run_bass_kernel_spmd(_NC_CACHE['nc'], maps, core_ids=list(range(8)))
    return _assemble(res.results)
```

```python
import contextlib
import numpy as np
import concourse.bass as bass
import concourse.mybir as mybir
from concourse.bass_utils import run_bass_kernel_spmd

F32 = mybir.dt.float32
BF16 = mybir.dt.bfloat16
AF = mybir.ActivationFunctionType
ALU = mybir.AluOpType
AX = mybir.AxisListType

D = 1024
T = 2048
DEPTH = 4
TS = 16
PAST = 1024
NTOK = 2 * T + TS
INC = 3340
DFF = 2816
EPS = 1e-6


def _box(ap):
    t = ap.tensor
    name = t.name
    dims = [(int(s), int(c)) for s, c in ap.ap]
    off = int(ap.offset)
    if 'DRAM' in str(ap.space).upper():
        lo = hi = off
        for s, c in dims:
            if s >= 0:
                hi += s * (c - 1)
            else:
                lo += s * (c - 1)
        return (name, 0, 1, lo, hi + 1)
    import os
    if 'PSUM' in str(ap.space).upper() and (os.environ.get('KDBG_PSUMBOX', '1') == '1' or (os.environ.get('KDBG_PSUMBOX') == '2' and name.startswith('pbb'))):
        return (name, 0, 128, 0, 1 << 30)
    psize = 1
    for d in list(t.shape)[1:]:
        psize *= int(d)
    p0 = off // psize
    f0 = off % psize
    pstep, pcnt = dims[0]
    if pstep == psize or pcnt == 1:
        np_ = pcnt
    elif pstep == 0:
        np_ = 1
    else:
        np_ = 128 - p0
    lo = hi = f0
    for s, c in dims[1:]:
        if s >= 0:
            hi += s * (c - 1)
        else:
            lo += s * (c - 1)
    return (name, p0, p0 + np_, lo, hi + 1)


class Sync:
    def __init__(self, nc, stack, n_dma_sems=8):
        self.nc = nc
        self.eng = {'pe': nc.tensor, 'act': nc.scalar, 'dve': nc.vector, 'pool': nc.gpsimd, 'sp': nc.sync}
        self.sem = {}
        self.cnt = {}
        for e in ('pe', 'act', 'dve', 'pool'):
            self.sem[e] = stack.enter_context(nc.semaphore('s_' + e))
            self.cnt[e] = 0
        self.dsem = {}
        self.dcnt = {}
        for q in ('sp', 'pool'):
            self.dsem[q] = [stack.enter_context(nc.semaphore('d_%s%d' % (q, i))) for i in range(n_dma_sems)]
            self.dcnt[q] = 0
        self.K = n_dma_sems
        self.semobj = {}
        for e, s in self.sem.items():
            self.semobj[('c', e)] = s
        for q, l in self.dsem.items():
            for i, s in enumerate(l):
                self.semobj[('d', q, i)] = s
        self.waited = {e: {} for e in self.eng}
        self.W = {}
        self.R = {}
        self.n_wait = 0
        self.n_inst = 0

    def _collect(self, reads, writes):
        deps = {}
        for ap in reads:
            b = _box(ap)
            for r in self.W.get(b[0], ()):
                if r[0] < b[2] and b[1] < r[1] and r[2] < b[4] and b[3] < r[3]:
                    if deps.get(r[4], 0) < r[5]:
                        deps[r[4]] = r[5]
        for ap in writes:
            b = _box(ap)
            for r in self.W.get(b[0], ()):
                if r[0] < b[2] and b[1] < r[1] and r[2] < b[4] and b[3] < r[3]:
                    if deps.get(r[4], 0) < r[5]:
                        deps[r[4]] = r[5]
            rd = self.R.get(b[0])
            if rd:
                for k, v in rd.items():
                    if k[0] < b[2] and b[1] < k[1] and k[2] < b[4] and b[3] < k[3]:
                        if deps.get(k[4], 0) < v:
                            deps[k[4]] = v
        return deps

    def _record(self, reads, writes, semkey, val):
        for ap in writes:
            b = _box(ap)
            lst = self.W.setdefault(b[0], [])
            lst[:] = [r for r in lst if not (b[1] <= r[0] and r[1] <= b[2] and b[3] <= r[2] and r[3] <= b[4])]
            lst.append([b[1], b[2], b[3], b[4], semkey, val])
            rd = self.R.get(b[0])
            if rd:
                for k in [k for k in rd if b[1] <= k[0] and k[1] <= b[2] and b[3] <= k[2] and k[3] <= b[4]]:
                    del rd[k]
        for ap in reads:
            b = _box(ap)
            self.R.setdefault(b[0], {})[(b[1], b[2], b[3], b[4], semkey)] = val

    def _emit_waits(self, e, deps):
        w = self.waited[e]
        for k, v in deps.items():
            if k == ('c', 'pe') and e == 'pe':
                continue
            if w.get(k, 0) >= v:
                continue
            self.eng[e].wait_ge(self.semobj[k], v)
            w[k] = v
            self.n_wait += 1

    def op(self, e, fn, reads=(), writes=()):
        px = [a for a in reads if 'PSUM' in str(a.space).upper()]
        if px:
            writes = list(writes) + px
        deps = self._collect(reads, writes)
        self._emit_waits(e, deps)
        inst = fn()
        self.cnt[e] += 1
        inst.then_inc(self.sem[e], 1)
        self._record(reads, writes, ('c', e), self.cnt[e])
        self.n_inst += 1
        return inst

    def dma(self, q, out, in_, **kw):
        deps = self._collect([in_], [out])
        n = self.dcnt[q]
        k = n % self.K
        semkey = ('d', q, k)
        prev = (n // self.K) * 16
        if prev > 0:
            deps[semkey] = max(deps.get(semkey, 0), prev)
        self._emit_waits(q, deps)
        inst = self.eng[q].dma_start(out=out, in_=in_, **kw)
        inst.then_inc(self.semobj[semkey], 16)
        self.dcnt[q] = n + 1
        self._record([in_], [out], semkey, prev + 16)
        self.n_inst += 1
        return inst

    def barrier(self):
        toks = {}
        for e in ('pe', 'act', 'dve', 'pool'):
            if self.cnt[e] > 0:
                toks[('c', e)] = self.cnt[e]
        for q in self.dsem:
            n = self.dcnt[q]
            for k in range(self.K):
                cntk = (n - k + self.K - 1) // self.K if n > k else 0
                if cntk > 0:
                    toks[('d', q, k)] = cntk * 16
        for e in self.eng:
            w = self.waited[e]
            for k, v in toks.items():
                if k == ('c', 'pe') and e == 'pe':
                    continue
                if w.get(k, 0) >= v:
                    continue
                self.eng[e].wait_ge(self.semobj[k], v)
                w[k] = v
                self.n_wait += 1
        self.W = {}
        self.R = {}


class B:
    def __init__(self, nc, st, n_layers=DEPTH, stage=99):
        self.nc = nc
        self.st = st
        self.S = Sync(nc, st)
        self.n_layers = n_layers
        self.stage = stage
        self.uid = 0

    def sb(self, name, shape, dt, st=None):
        self.uid += 1
        return (st or self.st).enter_context(self.nc.sbuf_tensor('%s_%d' % (name, self.uid), shape, dt))

    def ps(self, name, shape, dt, st=None):
        self.uid += 1
        return (st or self.st).enter_context(self.nc.psum_tensor('%s_%d' % (name, self.uid), shape, dt))

    def din(self, name, shape):
        return self.nc.dram_tensor(name, list(shape), F32, kind="ExternalInput").ap()

    def dout(self, name, shape):
        return self.nc.dram_tensor(name, list(shape), F32, kind="ExternalOutput").ap()

    def mm(self, out, lhsT, rhs, start=True, stop=True, **kw):
        nc = self.nc
        return self.S.op('pe', lambda: nc.tensor.matmul(out, lhsT=lhsT, rhs=rhs, start=start, stop=stop, **kw),
                         reads=[lhsT, rhs], writes=[out])

    def tr(self, out, in_, ident):
        nc = self.nc
        return self.S.op('pe', lambda: nc.tensor.transpose(out=out, in_=in_, identity=ident),
                         reads=[in_, ident], writes=[out])

    def act(self, out, in_, func, bias=None, scale=1.0):
        nc = self.nc
        reads = [in_]
        kw = {}
        if bias is not None:
            kw['bias'] = bias
            if not isinstance(bias, (int, float)):
                reads.append(bias)
        if not isinstance(scale, (int, float)):
            reads.append(scale)
        return self.S.op('act', lambda: nc.scalar.activation(out=out, in_=in_, func=func, scale=scale, **kw),
                         reads=reads, writes=[out])

    def _ve(self, e):
        return self.nc.vector if e == 'dve' else self.nc.gpsimd

    def tt(self, out, in0, in1, op, e='dve'):
        eng = self._ve(e)
        return self.S.op(e, lambda: eng.tensor_tensor(out=out, in0=in0, in1=in1, op=op),
                         reads=[in0, in1], writes=[out])

    def ts(self, out, in0, s1, op0, s2=None, op1=None, e='dve'):
        eng = self._ve(e)
        reads = [in0]
        if not isinstance(s1, (int, float)):
            reads.append(s1)
        if s2 is not None and not isinstance(s2, (int, float)):
            reads.append(s2)
        if op1 is None:
            f = lambda: eng.tensor_scalar(out=out, in0=in0, scalar1=s1, scalar2=None, op0=op0)
        else:
            f = lambda: eng.tensor_scalar(out=out, in0=in0, scalar1=s1, scalar2=s2, op0=op0, op1=op1)
        return self.S.op(e, f, reads=reads, writes=[out])

    def stt(self, out, in0, scalar, in1, op0, op1, e='dve'):
        eng = self._ve(e)
        reads = [in0, in1]
        if not isinstance(scalar, (int, float)):
            reads.append(scalar)
        return self.S.op(e, lambda: eng.scalar_tensor_tensor(out=out, in0=in0, scalar=scalar, in1=in1, op0=op0, op1=op1),
                         reads=reads, writes=[out])

    def cp(self, out, in_, e='dve'):
        if e == 'act':
            return self.act(out, in_, AF.Copy)
        eng = self._ve(e)
        return self.S.op(e, lambda: eng.tensor_copy(out=out, in_=in_), reads=[in_], writes=[out])

    def red(self, out, in_, op=ALU.add, e='dve'):
        eng = self._ve(e)
        return self.S.op(e, lambda: eng.tensor_reduce(out=out, in_=in_, axis=AX.X, op=op), reads=[in_], writes=[out])

    def recip(self, out, in_):
        nc = self.nc
        return self.S.op('dve', lambda: nc.vector.reciprocal(out=out, in_=in_), reads=[in_], writes=[out])

    def memset(self, ap, v, e='pool'):
        eng = self._ve(e)
        return self.S.op(e, lambda: eng.memset(ap, v), writes=[ap])

    def asel(self, out, in_, pattern, cmp, fill, base, cm):
        nc = self.nc
        return self.S.op('pool', lambda: nc.gpsimd.affine_select(out=out, in_=in_, pattern=pattern, compare_op=cmp,
                                                               fill=fill, base=base, channel_multiplier=cm),
                         reads=[in_], writes=[out])

    def dma(self, out, in_, q='sp', **kw):
        return self.S.dma(q, out, in_, **kw)

    def rsqrt(self, out, in_, scale, tmp=None):
        t = tmp if tmp is not None else out
        self.act(t, in_, AF.Ln, bias=self.eps_col[0:int(in_.shape[0]), :], scale=scale)
        self.act(out, t, AF.Exp, scale=-0.5)


def bc(ap, shape):
    return ap.unsqueeze(len(ap.shape)).broadcast_to(list(shape))


def pbc(dram_ap_1d, n, parts=128):
    return bass.AP(tensor=dram_ap_1d.tensor, offset=int(dram_ap_1d.offset), ap=[[0, parts], [1, n]])


class Rot:
    def __init__(self, bufs):
        self.bufs = bufs
        self.i = 0

    def next(self):
        t = self.bufs[self.i % len(self.bufs)]
        self.i += 1
        return t


def build(n_layers=DEPTH, stage=99):
    nc = bass.Bass("TRN2", target_bir_lowering=False)
    with contextlib.ExitStack() as st:
        b = B(nc, st, n_layers, stage)
        emit(b)
        b.S.barrier()
        print("built: inst", b.S.n_inst, "waits", b.S.n_wait)
    return nc


def emit(b):
    nc = b.nc
    L = DEPTH
    NL = b.n_layers
    d = {}
    d['xT'] = b.din('xT', [D, NTOK])
    d['cT'] = b.din('cT', [128, 8, 4])
    d['gconvT'] = b.din('gconvT', [L, 64, 12, 3])
    d['gstate'] = b.din('gstate', [L, 4, 64, 64])
    for n in ('sbk', 'sbv', 'fxk', 'fxv'):
        d[n] = b.din(n, [L, PAST, 256])
    d['fxlf'] = b.din('fxlf', [L, PAST, 4])
    d['bdk'] = b.din('bdk', [L, 512, 256])
    d['bdv'] = b.din('bdv', [L, 512, 256])
    d['fconvT'] = b.din('fconvT', [L, 128, 44, 2])
    d['ada_w'] = b.din('ada_w', [L, D, 6 * D])
    d['ada_bT'] = b.din('ada_bT', [128, L, 48])
    d['nmgT'] = b.din('nmgT', [128, L, 8])
    d['nfgT'] = b.din('nfgT', [128, L, 8])
    d['w_in'] = b.din('w_in', [L, D, INC])
    d['gcwT'] = b.din('gcwT', [64, L, 12, 4])
    d['a_log'] = b.din('a_log', [L, 4])
    d['dt_bias'] = b.din('dt_bias', [L, 4])
    d['gn_g'] = b.din('gn_g', [L, 64])
    d['fq_g'] = b.din('fq_g', [L, 64])
    d['fk_g'] = b.din('fk_g', [L, 64])
    d['fb_f'] = b.din('fb_f', [L, 4])
    d['bq_g'] = b.din('bq_g', [L, 64])
    d['bk_g'] = b.din('bk_g', [L, 64])
    d['rel'] = b.din('rel', [L, 4, 257])
    d['mg'] = b.din('mg', [L, 768])
    d['w_o'] = b.din('w_o', [L, D, D])
    d['w_up'] = b.din('w_up', [L, D, 2 * DFF])
    d['fcwT'] = b.din('fcwT', [128, L, 44, 3])
    d['w_down'] = b.din('w_down', [L, DFF, D])
    o = {}
    o['yT'] = b.dout('yT', [D, NTOK])
    o['p_gconv'] = b.dout('p_gconv', [L, 2, 3, 768])
    o['p_gstate'] = b.dout('p_gstate', [L, 2, 4, 64, 64])
    for n in ('p_sbk', 'p_sbv', 'p_fxk', 'p_fxv'):
        o[n] = b.dout(n, [L, 2, T, 256])
    o['p_fxlf'] = b.dout('p_fxlf', [L, 2, T, 4])
    o['p_bdk'] = b.dout('p_bdk', [L, 2, 512, 256])
    o['p_bdv'] = b.dout('p_bdv', [L, 2, 512, 256])
    o['p_fconv'] = b.dout('p_fconv', [L, 2, 2, 2 * DFF])
    o['s_gconv'] = b.dout('s_gconv', [L, 3, 768])
    o['s_gstate'] = b.dout('s_gstate', [L, 4, 64, 64])
    for n in ('s_sbk', 's_sbv', 's_fxk', 's_fxv'):
        o[n] = b.dout(n, [L, TS, 256])
    o['s_fxlf'] = b.dout('s_fxlf', [L, TS, 4])
    o['s_bdk'] = b.dout('s_bdk', [L, 512, 256])
    o['s_bdv'] = b.dout('s_bdv', [L, 512, 256])
    o['s_fconv'] = b.dout('s_fconv', [L, 2, 2 * DFF])
    b.d, b.o = d, o

    identb = b.sb('identb', [128, 128], BF16)
    identf = b.sb('identf', [128, 128], F32)
    onesb = b.sb('onesb', [128, 128], BF16)
    onesf = b.sb('onesf', [128, 128], F32)
    triUb = b.sb('triUb', [128, 128], BF16)
    triIncf = b.sb('triIncf', [128, 128], F32)
    sutf = b.sb('sutf', [64, 64], F32)
    antiJ = b.sb('antiJ', [128, 128], F32)
    blkb = b.sb('blkb', [128, 128], BF16)
    mask0 = b.sb('mask0', [128, 4, 128], BF16)
    mask4 = b.sb('mask4', [128, 4, 128], BF16)
    b.eps_col = b.sb('eps_col', [128, 1], F32)
    b.memset(b.eps_col[:], EPS)
    b.memset(identf[:], 0.0)
    b.asel(identf[:], identf[:], [[-1, 128]], ALU.not_equal, 1.0, 0, 1)
    b.cp(identb[:], identf[:])
    b.memset(onesb[:], 1.0)
    b.memset(onesf[:], 1.0)
    b.asel(triUb[:], onesb[:], [[-1, 128]], ALU.is_ge, 0.0, 0, 1)
    b.asel(triIncf[:], onesf[:], [[1, 128]], ALU.is_ge, 0.0, 0, -1)
    b.asel(sutf[:], onesf[0:64, 0:64], [[1, 64]], ALU.is_gt, 0.0, 0, -1)
    b.memset(antiJ[:], 0.0)
    b.asel(antiJ[:], antiJ[:], [[1, 128]], ALU.not_equal, 1.0, -127, 1)
    b.memset(blkb[:], 0.0)
    b.memset(blkb[0:64, 0:64], 1.0)
    b.memset(blkb[64:128, 64:128], 1.0)
    b.memset(mask0[:], 1.0)
    b.memset(mask0[0:64, :, 64:128], 0.0)
    b.memset(mask4[:], 1.0)
    b.memset(mask4[64:128, :, 0:64], 0.0)
    C = dict(identb=identb, identf=identf, onesb=onesb, onesf=onesf, triUb=triUb, triIncf=triIncf, sutf=sutf,
             antiJ=antiJ, blkb=blkb, mask0=mask0, mask4=mask4)
    b.C = C

    b.P = [b.ps('pb%d' % i, [128, 512], F32) for i in range(7)]
    b.PB = b.ps('pbb', [128, 1024], BF16)

    nmg = b.sb('nmg', [128, L, 8], F32)
    nfg = b.sb('nfg', [128, L, 8], F32)
    adab = b.sb('adab', [128, L, 48], F32)
    gcw = b.sb('gcw', [64, L, 12, 4], F32)
    fcw = b.sb('fcw', [128, L, 44, 3], F32)
    b.dma(nmg[:], d['nmgT'])
    b.dma(nfg[:], d['nfgT'])
    b.dma(adab[:], d['ada_bT'])
    b.dma(gcw[:], d['gcwT'])
    b.dma(fcw[:], d['fcwT'])
    b.gcw, b.fcw = gcw, fcw

    mod = b.sb('mod', [128, L, 48, 4], F32)
    modA = b.sb('modA', [128, L, 2, 8, 4], F32)
    b.mod, b.modA = mod, modA
    with contextlib.ExitStack() as s1:
        cT = b.sb('cT', [128, 8, 4], F32, s1)
        sc = b.sb('sc', [128, 8, 4], F32, s1)
        tmp = b.sb('sctmp', [128, 8, 4], F32, s1)
        b.dma(cT[:], d['cT'])
        b.act(tmp[:], cT[:], AF.Exp, scale=-1.0)
        b.ts(tmp[:], tmp[:], 1.0, ALU.add)
        b.recip(tmp[:], tmp[:])
        b.tt(sc[:], cT[:], tmp[:], ALU.mult)
        awr = Rot([b.sb('aw%d' % i, [128, 8, 768], F32, s1) for i in range(2)])
        for l in range(NL):
            pm = b.P[l % 2]
            for pc in range(8):
                aw = awr.next()
                b.dma(aw[:], d['ada_w'][l, :, pc * 768:(pc + 1) * 768].rearrange("(c p) n -> p c n", p=128))
                for j in range(6):
                    oc = pc * 6 + j
                    for c in range(8):
                        b.mm(pm[:, oc * 4:oc * 4 + 4], aw[:, c, j * 128:(j + 1) * 128], sc[:, c, :],
                             start=(c == 0), stop=(c == 7))
            pmv = pm[:, 0:192].rearrange("p (a s) -> p a s", s=4)
            b.tt(mod[:, l, :, :], pmv, bc(adab[:, l, :], [128, 48, 4]), ALU.add)
            for w, g in ((0, nmg), (1, nfg)):
                scv = mod[:, l, (1 + 3 * w) * 8:(2 + 3 * w) * 8, :]
                b.ts(modA[:, l, w, :, :], scv, 1.0, ALU.add)
                b.tt(modA[:, l, w, :, :], modA[:, l, w, :, :], bc(g[:, l, :], [128, 8, 4]), ALU.mult)
    b.S.barrier()

    for l in range(NL):
        layer(b, l)


def rmsnorm_tok(b, out, src, n, np_, gains, wk, den=None):
    sq, ss, y = wk
    x = src
    if den is not None:
        b.recip(ss[0:np_, 0:n], den)
        b.tt(y[0:np_, 0:n, :], src, bc(ss[0:np_, 0:n], [np_, n, 64]), ALU.mult)
        x = y[0:np_, 0:n, :]
    b.act(sq[0:np_, 0:n, :], x, AF.Square)
    b.red(ss[0:np_, 0:n], sq[0:np_, 0:n, :])
    b.rsqrt(ss[0:np_, 0:n], ss[0:np_, 0:n], 1.0 / 64.0)
    if gains is None:
        b.tt(out, x, bc(ss[0:np_, 0:n], [np_, n, 64]), ALU.mult)
    else:
        b.tt(y[0:np_, 0:n, :], x, bc(ss[0:np_, 0:n], [np_, n, 64]), ALU.mult)
        b.tt(out, y[0:np_, 0:n, :], gains[0:np_, 0:n, :], ALU.mult)


def layer(b, l):
    nc = b.nc
    d, o, C, P = b.d, b.o, b.C, b.P
    src_x = d['xT'] if l == 0 else o['yT']
    with contextlib.ExitStack() as s:
        win = b.sb('win', [128, 8, INC], BF16, s)
        wo = b.sb('wo', [128, 8, D], BF16, s)
        for c in range(8):
            for hh in range(2):
                b.dma(win[:, c, hh * 1670:(hh + 1) * 1670], d['w_in'][l, c * 128:(c + 1) * 128, hh * 1670:(hh + 1) * 1670], q='pool')
        for c in range(8):
            b.dma(wo[:, c, :], d['w_o'][l, c * 128:(c + 1) * 128, :], q='pool')
        gC = b.sb('gC', [128, 8, 64], F32, s)
        gD = b.sb('gD', [128, 8, 64], F32, s)
        gG = b.sb('gG', [128, 4, 64], F32, s)
        mg = b.sb('mg', [128, 12, 64], F32, s)
        dtb = b.sb('dtb', [128, 4], F32, s)
        nexpA = b.sb('nexpA', [128, 4], F32, s)
        fbf = b.sb('fbf', [128, 4], F32, s)

        def bcl(t, n, rep):
            return bass.AP(tensor=t.tensor, offset=int(t.offset), ap=[[0, 128], [0, rep], [1, n]])
        b.dma(gC[:, 0:4, :], bcl(d['fq_g'][l], 64, 4))
        b.dma(gC[:, 4:8, :], bcl(d['fk_g'][l], 64, 4))
        b.dma(gD[:, 0:4, :], bcl(d['bq_g'][l], 64, 4))
        b.dma(gD[:, 4:8, :], bcl(d['bk_g'][l], 64, 4))
        b.dma(gG[:], bcl(d['gn_g'][l], 64, 4))
        b.dma(mg[:].rearrange("p a b -> p (a b)"), pbc(d['mg'][l], 768))
        b.dma(dtb[:], pbc(d['dt_bias'][l], 4))
        b.dma(nexpA[:], pbc(d['a_log'][l], 4))
        b.dma(fbf[:], pbc(d['fb_f'][l], 4))
        b.act(nexpA[:], nexpA[:], AF.Exp)
        b.ts(nexpA[:], nexpA[:], -1.0, ALU.mult)
        relx = nc.dram_tensor('relx%d' % l, [4, 520], F32, kind="Internal").ap()
        with contextlib.ExitStack() as s2:
            rl = b.sb('rl', [4, 257], F32, s2)
            rx = b.sb('rx', [4, 520], F32, s2)
            b.dma(rl[:], d['rel'][l])
            b.memset(rx[:], 0.0)
            b.cp(rx[:, 128:385], rl[:])
            b.cp(rx[:, 0:128], rl[:, 0:1].broadcast_to([4, 128]))
            b.cp(rx[:, 385:513], rl[:, 256:257].broadcast_to([4, 128]))
            b.dma(relx, rx[:])
        Bt = b.sb('Bt', [128, 3, 4, 128], F32, s)
        with contextlib.ExitStack() as s2:
            tz = b.sb('tz', [128, 4, 128], F32, s2)
            for kind, delta in ((0, -384), (1, -128), (2, 0)):
                for h in range(4):
                    if delta <= -384:
                        src = bass.AP(tensor=relx.tensor, offset=int(relx[h, 385:386].offset), ap=[[0, 128], [1, 128]])
                    else:
                        src = bass.AP(tensor=relx.tensor, offset=int(relx[h, 0:1].offset) + (129 - delta), ap=[[1, 128], [1, 128]])
                    b.dma(tz[:, h, :], src)
                pz = P[0]
                b.mm(pz[:, 0:512], C['antiJ'][:], tz[:].rearrange("p a b -> p (a b)"))
                b.cp(Bt[:, kind, :, :].rearrange("p a b -> p (a b)"), pz[:, 0:512])
        KT = [b.sb('KT%d' % i, [128, 2, T if i < 2 else 1024], BF16, s) for i in range(3)]
        V = [b.sb('V%d' % i, [128, 16 if i < 2 else 8, 4, 65], BF16, s) for i in range(3)]
        for i in range(3):
            b.memset(V[i][:], 1.0)
        W = dict(win=win, wo=wo, gC=gC, gD=gD, gG=gG, mg=mg, dtb=dtb, nexpA=nexpA, fbf=fbf, Bt=Bt, KT=KT, V=V)
        alloc_work(b, W, s)
        import os
        for si in [int(c) for c in os.environ.get('KDBG_SEQS', '012')]:
            phaseAB(b, l, si, W, src_x)
    b.S.barrier()
    phaseC(b, l)
    b.S.barrier()


GP = 256
SKIP_D = False
OFFS = [1032, 1800, 2572]


def mid_bc(ap2d, n):
    a = [list(x) for x in ap2d.ap]
    return bass.AP(tensor=ap2d.tensor, offset=int(ap2d.offset), ap=[a[0], [0, n], a[1]])


def alloc_work(b, W, s):
    G = GP
    W['xg'] = b.sb('xg', [128, 8, G], F32, s)
    W['sqr'] = Rot([b.sb('sqr%d' % i, [128, G], BF16, s) for i in range(2)])
    W['rstd'] = b.sb('rstd', [128, G], F32, s)
    W['hT'] = b.sb('hT', [128, 8, G], BF16, s)
    W['QT'] = [b.sb('QT%d' % i, [128, 2, G], BF16, s) for i in range(3)]
    W['stg'] = Rot([b.sb('stg%d' % i, [128, 512], F32, s) for i in range(3)])
    W['qkb'] = Rot([b.sb('qkb%d' % i, [128, 512], BF16, s) for i in range(2)])
    W['nwk'] = (b.sb('nsq', [128, 8, 64], F32, s), b.sb('nss', [128, 8], F32, s), b.sb('ny', [128, 8, 64], F32, s))
    W['lf'] = b.sb('lf', [128, 4], F32, s)
    W['lfc'] = b.sb('lfc', [128, 8, 4], F32, s)
    W['negF'] = b.sb('negF', [128, 17, 4], F32, s)
    W['carry'] = b.sb('carry', [128, 4], F32, s)
    W['fbias'] = b.sb('fbias', [128, 17, 4], F32, s)
    W['mixed'] = b.sb('mixed', [128, G // 128, 1024], BF16, s)
    W['mixedT'] = b.sb('mixedT', [128, 8, G], BF16, s)
    X = [b.sb('X%d' % i, [128, 512], F32, s) for i in range(6)]
    W['X'] = X
    W['E'] = Rot([X[0], X[1]])
    W['zr'] = Rot([X[2], X[3]])
    W['Rr'] = [X[4], X[5]]
    W['htmp'] = Rot([X[1][:, 0:256], X[1][:, 256:512]])
    W['bt'] = Rot([X[4]])
    W['SP'] = Rot([b.sb('SP%d' % i, [128, 512], BF16, s) for i in range(2)])
    W['Wt'] = Rot([b.sb('Wt%d' % i, [128, 512], BF16, s) for i in range(3)])
    alloc_gdn(b, W, s)
    print('sbuf remaining after alloc', b.nc.sbuf_bytes_remaining)


def norm_mod(b, l, si, which, xg, G, W, hT):
    C, P = b.C, b.P
    ss = P[0][:, 0:G]
    for c in range(8):
        sq = W['sqr'].next()
        b.act(sq[:, 0:G], xg[:, c, 0:G], AF.Square)
        b.mm(ss, C['onesb'][:], sq[:, 0:G], start=(c == 0), stop=(c == 7))
    b.rsqrt(W['rstd'][:, 0:G], ss, 1.0 / D)
    for c in range(8):
        ht = W['htmp'].next()
        b.stt(ht[:, 0:G], xg[:, c, 0:G], b.modA[:, l, which, c, si:si + 1], W['rstd'][:, 0:G], ALU.mult, ALU.mult)
        b.act(hT[:, c, 0:G], ht[:, 0:G], AF.Identity, bias=b.mod[:, l, which * 24 + c, si:si + 1])


def phaseAB(b, l, si, W, src_x):
    nc = b.nc
    d, o, C, P, PB = b.d, b.o, b.C, b.P, b.PB
    prompt = si < 2
    Tq = T if prompt else TS
    tok0 = si * T if prompt else 2 * T
    G = GP if prompt else TS
    ngroups = Tq // G
    npast = 0 if prompt else PAST
    npastD = 0 if prompt else 512
    KT, V, win, wo, mg = W['KT'], W['V'], W['win'], W['wo'], W['mg']
    xg, hT, QT, mixed, mixedT = W['xg'], W['hT'], W['QT'], W['mixed'], W['mixedT']
    identb = C['identb']
    pbv = PB[:, 0:1024].rearrange("p (j t) -> p j t", t=128)
    pf = P[3]
    carry, negF, lf = W['carry'], W['negF'], W['lf']
    b.memset(carry[:], 0.0)
    b.memset(negF[:], 0.0)
    if not prompt:
        for i, (kn, vn, nrow) in enumerate((('sbk', 'sbv', 1024), ('fxk', 'fxv', 1024), ('bdk', 'bdv', 512))):
            for tl in range(nrow // 128):
                b.dma(V[i][:, tl, :, 0:64], d[vn][l, tl * 128:(tl + 1) * 128, :].rearrange("p (h e) -> p h e", e=64), q='pool')
                kb = W['qkb'].next()
                b.dma(kb[:, 0:256], d[kn][l, tl * 128:(tl + 1) * 128, :], q='pool')
                for j in range(2):
                    b.tr(pbv[:, j, :], kb[:, j * 128:(j + 1) * 128], identb[:])
                b.cp(KT[i][:, :, tl * 128:(tl + 1) * 128], pbv[:, 0:2, :])
        lfc = W['lfc']
        b.dma(lfc[:], d['fxlf'][l].rearrange("(t p) h -> p t h", p=128))
        for tl in range(8):
            b.mm(pf[:, 0:4], C['triIncf'][:], lfc[:, tl, :])
            b.stt(negF[:, tl, :], pf[:, 0:4], -1.0, carry[:], ALU.mult, ALU.subtract)
            b.mm(pf[:, 4:8], C['onesf'][:], lfc[:, tl, :])
            b.tt(carry[:], carry[:], pf[:, 4:8], ALU.add)
        b.dma(o['s_bdk'][l, 0:496, :], d['bdk'][l, 16:512, :])
        b.dma(o['s_bdv'][l, 0:496, :], d['bdv'][l, 16:512, :])
    gdn_seq_init(b, l, si, W)

    import os
    for g in range(min(ngroups, int(os.environ.get('KDBG_MAXG', '99')))):
        t0 = g * G
        b.dma(xg[:, :, 0:G], src_x[:, tok0 + t0:tok0 + t0 + G].rearrange("(c p) t -> p c t", p=128))
        norm_mod(b, l, si, 0, xg, G, W, hT)
        ntile = max(1, G // 128)
        nt = min(128, G)
        for tt in range(ntile):
            cols = slice(tt * 128, tt * 128 + nt)
            for i in range(3):
                c0 = OFFS[i]
                pqk, pv = P[1], P[2]
                nv = 260 if i == 1 else 256
                for c in range(8):
                    b.mm(pqk[0:nt, 0:512], hT[:, c, cols], win[:, c, c0:c0 + 512], start=(c == 0), stop=(c == 7))
                for c in range(8):
                    b.mm(pv[0:nt, 0:nv], hT[:, c, cols], win[:, c, c0 + 512:c0 + 512 + nv], start=(c == 0), stop=(c == 7))
                kbase = (npast if i < 2 else npastD) + t0 + tt * 128
                kt_i = kbase // 128
                if i == 2:
                    kbase = kbase % 1024
                    kt_i = kt_i % 8
                sv = W['stg'].next()
                b.cp(sv[0:nt, 0:256], pv[0:nt, 0:256], e='act')
                b.cp(V[i][0:nt, kt_i, :, 0:64], sv[0:nt, 0:256].rearrange("p (h e) -> p h e", e=64))
                sk = W['stg'].next()
                if i == 0:
                    b.cp(sk[0:nt, 0:512], pqk[0:nt, 0:512])
                else:
                    rmsnorm_tok(b, sk[0:nt, 0:512].rearrange("p (h e) -> p h e", e=64),
                                pqk[0:nt, 0:512].rearrange("p (h e) -> p h e", e=64), 8, nt,
                                W['gC'] if i == 1 else W['gD'], W['nwk'])
                tloc = t0 + tt * 128
                if prompt:
                    if i == 0:
                        b.dma(o['p_sbk'][l, si, tloc:tloc + nt, :], sk[0:nt, 256:512])
                        b.dma(o['p_sbv'][l, si, tloc:tloc + nt, :], sv[0:nt, 0:256])
                    elif i == 1:
                        b.dma(o['p_fxk'][l, si, tloc:tloc + nt, :], sk[0:nt, 256:512])
                        b.dma(o['p_fxv'][l, si, tloc:tloc + nt, :], sv[0:nt, 0:256])
                    elif tloc >= T - 512:
                        b.dma(o['p_bdk'][l, si, tloc - (T - 512):tloc - (T - 512) + nt, :], sk[0:nt, 256:512])
                        b.dma(o['p_bdv'][l, si, tloc - (T - 512):tloc - (T - 512) + nt, :], sv[0:nt, 0:256])
                else:
                    if i == 0:
                        b.dma(o['s_sbk'][l, 0:nt, :], sk[0:nt, 256:512])
                        b.dma(o['s_sbv'][l, 0:nt, :], sv[0:nt, 0:256])
                    elif i == 1:
                        b.dma(o['s_fxk'][l, 0:nt, :], sk[0:nt, 256:512])
                        b.dma(o['s_fxv'][l, 0:nt, :], sv[0:nt, 0:256])
                    else:
                        b.dma(o['s_bdk'][l, 496:512, :], sk[0:nt, 256:512])
                        b.dma(o['s_bdv'][l, 496:512, :], sv[0:nt, 0:256])
                qb = W['qkb'].next()
                b.cp(qb[0:nt, :], sk[0:nt, 0:512], e='act')
                for j in range(4):
                    b.tr(pbv[:, j, 0:nt], qb[0:nt, j * 128:(j + 1) * 128], identb[0:nt, 0:nt])
                b.cp(QT[i][:, :, tt * 128:tt * 128 + nt], pbv[:, 0:2, 0:nt])
                b.cp(KT[i][:, :, kbase:kbase + nt], pbv[:, 2:4, 0:nt], e='act')
                if i == 1:
                    b.tt(lf[0:nt, :], pv[0:nt, 256:260], W['fbf'][0:nt, :], ALU.add)
                    b.act(lf[0:nt, :], lf[0:nt, :], AF.Exp, scale=-1.0)
                    b.act(lf[0:nt, :], lf[0:nt, :], AF.Ln, bias=1.0)
                    b.ts(lf[0:nt, :], lf[0:nt, :], -1.0, ALU.mult)
                    if prompt:
                        b.dma(o['p_fxlf'][l, si, tloc:tloc + nt, :], lf[0:nt, :])
                    else:
                        b.dma(o['s_fxlf'][l, 0:nt, :], lf[0:nt, :])
                    b.mm(pf[0:nt, 0:4], C['triIncf'][0:nt, 0:nt], lf[0:nt, :])
                    b.stt(negF[0:nt, kt_i, :], pf[0:nt, 0:4], -1.0, carry[0:nt, :], ALU.mult, ALU.subtract)
                    b.mm(pf[:, 4:8], C['onesf'][0:nt, :], lf[0:nt, :])
                    b.tt(carry[:], carry[:], pf[:, 4:8], ALU.add)
        if b.stage < 2:
            continue
        nsb = ntile
        nq = nt
        qpos0 = npast + t0
        jmax = (qpos0 + G - 1) // 128
        Ops = P[4]
        zc = 0
        GW = 2 * G
        if b.stage >= 2 and not os.environ.get('KDBG_NOB'):
            for p in range(2):
                b.memset(W['Rr'][p][:, 0:GW], 0.0)
            firstB = True
            for j in range(jmax, -1, -1):
                nk = min(128, npast + Tq - j * 128)
                m = j * 128 - qpos0
                diag = m >= 0
                for p in range(2):
                    po = p * 64
                    Rr = W['Rr'][p]
                    pz = P[5 + zc % 2]
                    zc += 1
                    for hl in range(2):
                        b.mm(pz[0:nk, hl * G:(hl + 1) * G], KT[0][po:po + 64, hl, j * 128:j * 128 + nk], QT[0][po:po + 64, hl, 0:G])
                    E = W['E'].next()
                    b.act(E[0:nk, 0:GW], pz[0:nk, 0:GW], AF.Exp, scale=0.125)
                    sp = W['SP'].next()
                    b.act(sp[0:nk, 0:GW], E[0:nk, 0:GW], AF.Ln, bias=1.0)
                    if diag:
                        spv = sp[0:nk, 0:GW].rearrange("p (a q) -> p a q", a=2)
                        b.asel(spv, spv, [[0, 2], [1, G]], ALU.is_gt, 0.0, -m, -1)
                    b.mm(P[3][0:nk, 0:GW], C['triUb'][0:nk, 0:nk], sp[0:nk, 0:GW])
                    if j > 0:
                        b.mm(P[2][:, 0:GW], C['onesb'][0:nk, :], sp[0:nk, 0:GW])
                    zr = W['zr'].next()
                    b.stt(zr[0:nk, 0:GW], pz[0:nk, 0:GW], 0.125, Rr[0:nk, 0:GW], ALU.mult, ALU.subtract)
                    b.tt(zr[0:nk, 0:GW], zr[0:nk, 0:GW], P[3][0:nk, 0:GW], ALU.subtract)
                    wt = W['Wt'].next()
                    b.act(wt[0:nk, 0:GW], zr[0:nk, 0:GW], AF.Exp)
                    if diag:
                        wtv = wt[0:nk, 0:GW].rearrange("p (a q) -> p a q", a=2)
                        b.asel(wtv, wtv, [[0, 2], [1, G]], ALU.is_gt, 0.0, -m, -1)
                    if j > 0:
                        b.tt(Rr[:, 0:GW], Rr[:, 0:GW], P[2][:, 0:GW], ALU.add)
                    for hl in range(2):
                        h = p + 2 * hl
                        for sb_ in range(nsb):
                            if diag and m >= sb_ * 128 + nq:
                                continue
                            oc0 = ((p * 2 + hl) * nsb + sb_) * 64
                            b.mm(Ops[0:nq, oc0:oc0 + 64], wt[0:nk, hl * G + sb_ * 128:hl * G + sb_ * 128 + nq],
                                 V[0][0:nk, j, h, 0:64], start=firstB, stop=(j == 0), skip_group_check=True)
                            firstB = False
            for p in range(2):
                for hl in range(2):
                    h = p + 2 * hl
                    oc0 = (p * 2 + hl) * nsb * 64
                    rmsnorm_tok(b, mixed[0:nq, 0:nsb, 256 + h * 64:256 + (h + 1) * 64],
                                Ops[0:nq, oc0:oc0 + nsb * 64].rearrange("p (s e) -> p s e", e=64), nsb, nq,
                                mid_bc(mg[:, h, :], nsb), W['nwk'])
        nj = jmax + 1
        b.tt(W['fbias'][:, 0:nj, :], negF[:, 0:nj, :], mid_bc(carry[:, :], nj), ALU.add)
        FQ = W['X'][0]
        for h in range(4 if b.stage >= 3 else 0):
            hp, po = h // 2, (h % 2) * 64
            first = True
            pfq = P[3]
            dg = W['nwk'][0][:, 0:2, :].rearrange("p a e -> p (a e)")
            for sb_ in range(nsb):
                jq_ = qpos0 // 128 + sb_
                b.ts(dg[0:nq, 0:nq], C['identf'][0:nq, 0:nq], W['fbias'][0:nq, jq_, h:h + 1], ALU.mult)
                b.mm(pfq[:, sb_ * 128:sb_ * 128 + nq], C['onesf'][0:nq, :], dg[0:nq, 0:nq])
            b.cp(FQ[:, 0:G], pfq[:, 0:G])
            for j in range(0, jmax + 1):
                nk = min(128, npast + Tq - j * 128)
                m = j * 128 - qpos0
                diag = m >= 0
                pz = P[5 + zc % 2]
                zc += 1
                b.mm(pz[0:nk, 0:G], KT[1][po:po + 64, hp, j * 128:j * 128 + nk], QT[1][po:po + 64, hp, 0:G])
                wt = W['Wt'].next()
                s_ = W['zr'].next()
                b.stt(s_[0:nk, 0:G], pz[0:nk, 0:G], 0.125, FQ[0:nk, 0:G], ALU.mult, ALU.subtract)
                if diag:
                    b.ts(s_[0:nk, 0:G], s_[0:nk, 0:G], W['fbias'][0:nk, j, h:h + 1], ALU.add, 30.0, ALU.min)
                    b.act(wt[0:nk, 0:G], s_[0:nk, 0:G], AF.Exp)
                    b.asel(wt[0:nk, 0:G], wt[0:nk, 0:G], [[1, G]], ALU.is_ge, 0.0, -m, -1)
                else:
                    b.act(wt[0:nk, 0:G], s_[0:nk, 0:G], AF.Exp, bias=W['fbias'][0:nk, j, h:h + 1])
                for sb_ in range(nsb):
                    if diag and m >= sb_ * 128 + nq:
                        continue
                    b.mm(Ops[0:nq, sb_ * 65:(sb_ + 1) * 65], wt[0:nk, sb_ * 128:sb_ * 128 + nq], V[1][0:nk, j, h, 0:65],
                         start=first, stop=(j == jmax), skip_group_check=True)
                    first = False
            ov = Ops[0:nq, 0:nsb * 65].rearrange("p (s e) -> p s e", e=65)
            rmsnorm_tok(b, mixed[0:nq, 0:nsb, 512 + h * 64:512 + (h + 1) * 64], ov[:, :, 0:64], nsb, nq,
                        mid_bc(mg[:, 4 + h, :], nsb), W['nwk'], den=ov[:, :, 64])
        if SKIP_D and b.stage >= 5:
            b.memset(mixed[:, :, 768:1024], 0.0)
        for tq in range(ntile if (b.stage >= 4 and not SKIP_D) else 0):
            qk0 = npastD + t0 + tq * 128
            jq = qk0 // 128
            tiles = list(range(max(0, jq - 4), jq + 1))
            first = True
            for j in tiles:
                nk = min(128, npastD + Tq - j * 128)
                kind = 2 if j == jq else (1 if j == jq - 1 else 0)
                bt = W['bt'].next()
                btv = bt[0:nk, 0:512].rearrange("p (h q) -> p h q", q=128)[:, :, 0:nq]
                for h in range(4):
                    hp, po = h // 2, (h % 2) * 64
                    b.mm(P[5 + h % 2][0:nk, hp * 128:hp * 128 + nq], KT[2][po:po + 64, hp, (j % 8) * 128:(j % 8) * 128 + nk],
                         QT[2][po:po + 64, hp, tq * 128:tq * 128 + nq])
                for par in range(2):
                    src = P[5 + par][0:nk, 0:256].rearrange("p (a q) -> p a q", q=128)[:, :, 0:nq]
                    dst = bt[0:nk, 0:512].rearrange("p (a c q) -> p a c q", a=2, c=2, q=128)[:, :, par, 0:nq]
                    b.act(dst, src, AF.Identity, scale=0.125)
                b.tt(btv, btv, W['Bt'][0:nk, kind, :, 0:nq], ALU.add)
                wt = W['Wt'].next()
                wtv = wt[0:nk, 0:512].rearrange("p (h q) -> p h q", q=128)[:, :, 0:nq]
                b.act(wtv, btv, AF.Exp)
                if prompt and j == jq:
                    b.tt(wtv, wtv, C['mask4'][0:nk, :, 0:nq], ALU.mult)
                if prompt and j == jq - 4:
                    b.tt(wtv, wtv, C['mask0'][0:nk, :, 0:nq], ALU.mult)
                for h in range(4):
                    b.mm(Ops[0:nq, h * 65:(h + 1) * 65], wt[0:nk, h * 128:h * 128 + nq], V[2][0:nk, j % 8, h, 0:65],
                         start=first, stop=(j == tiles[-1]), skip_group_check=True)
                    first = False
            ov = Ops[0:nq, 0:260].rearrange("p (s e) -> p s e", e=65)
            rmsnorm_tok(b, mixed[0:nq, tq, 768:1024].rearrange("p (h e) -> p h e", e=64), ov[:, :, 0:64], 4, nq,
                        mg[:, 8:12, :], W['nwk'], den=ov[:, :, 64])
        gdn_group(b, l, si, W, g, G, t0)
        if b.stage < 5:
            continue
        for tt in range(ntile):
            for cc in range(2, 8):
                b.tr(pbv[:, cc, 0:nt], mixed[0:nt, tt, cc * 128:(cc + 1) * 128], identb[0:nt, 0:nt])
            b.cp(mixedT[:, 2:8, tt * 128:tt * 128 + nt], pbv[:, 2:8, 0:nt])
        for oc in range(8):
            pso = P[oc % 2]
            for c in range(8):
                b.mm(pso[:, 0:G], wo[:, c, oc * 128:(oc + 1) * 128], mixedT[:, c, 0:G], start=(c == 0), stop=(c == 7))
            b.stt(xg[:, oc, 0:G], pso[:, 0:G], b.mod[:, l, 16 + oc, si:si + 1], xg[:, oc, 0:G], ALU.mult, ALU.add)
        b.dma(o['yT'][:, tok0 + t0:tok0 + t0 + G].rearrange("(c p) t -> p c t", p=128), xg[:, :, 0:G])


def phaseC(b, l):
    nc = b.nc
    d, o, C, P = b.d, b.o, b.C, b.P
    if b.stage < 6:
        return
    with contextlib.ExitStack() as s:
        wup = b.sb('wup', [128, 8, 2 * DFF], BF16, s)
        wdn = b.sb('wdn', [128, 22, D], BF16, s)
        for c in range(8):
            for q4 in range(4):
                b.dma(wup[:, c, q4 * 1408:(q4 + 1) * 1408], d['w_up'][l, c * 128:(c + 1) * 128, q4 * 1408:(q4 + 1) * 1408], q='pool')
        for j in range(22):
            b.dma(wdn[:, j, :], d['w_down'][l, j * 128:(j + 1) * 128, :], q='pool')
        GC = 512
        W = {}
        xgr = Rot([b.sb('cxg%d' % i, [128, 8, GC], F32, s) for i in range(1)])
        W['sqr'] = Rot([b.sb('csqr%d' % i, [128, GC], BF16, s) for i in range(2)])
        W['rstd'] = b.sb('crstd', [128, GC], F32, s)
        h2r = Rot([b.sb('h2T%d' % i, [128, 8, GC], BF16, s) for i in range(1)])
        ur = Rot([b.sb('ur%d' % i, [128, GC + 2], F32, s) for i in range(2)])
        ctr = Rot([b.sb('ctr%d' % i, [128, GC], F32, s) for i in range(2)])
        W['htmp'] = ctr
        hist = b.sb('hist', [128, 44, 2], F32, s)
        actr = Rot([b.sb('actT%d' % i, [128, 22, GC], BF16, s) for i in range(1)])
        for_ = Rot([b.sb('fo%d' % i, [2, 512], F32, s) for i in range(2)])
        fcw = b.fcw
        pc_ = 0
        for si in range(3):
            prompt = si < 2
            Tq = T if prompt else TS
            tok0 = si * T if prompt else 2 * T
            G = GC if prompt else TS
            ngroups = Tq // G
            if prompt:
                b.memset(hist[:], 0.0)
            else:
                b.dma(hist[:], d['fconvT'][l])
            for g in range(ngroups):
                t0 = g * G
                xg, h2T, actT = xgr.next(), h2r.next(), actr.next()
                ycols = o['yT'][:, tok0 + t0:tok0 + t0 + G].rearrange("(c p) t -> p c t", p=128)
                b.dma(xg[:, :, 0:G], ycols)
                norm_mod(b, l, si, 1, xg, G, W, h2T)
                for j in range(22):
                    cv = []
                    for half in range(2):
                        jj = half * 22 + j
                        colb = half * DFF + j * 128
                        pu = P[1 + pc_ % 4]
                        pc_ += 1
                        for c in range(8):
                            b.mm(pu[:, 0:G], wup[:, c, colb:colb + 128], h2T[:, c, 0:G], start=(c == 0), stop=(c == 7))
                        u = ur.next()
                        b.cp(u[:, 0:2], hist[:, jj, :], e='act')
                        b.cp(u[:, 2:2 + G], pu[:, 0:G], e='act')
                        b.cp(hist[:, jj, :], u[:, G:G + 2], e='act')
                        ct = ctr.next()
                        b.act(ct[:, 0:G], u[:, 0:G], AF.Identity, scale=fcw[:, l, jj, 0:1])
                        b.stt(ct[:, 0:G], u[:, 1:1 + G], fcw[:, l, jj, 1:2], ct[:, 0:G], ALU.mult, ALU.add)
                        b.stt(ct[:, 0:G], u[:, 2:2 + G], fcw[:, l, jj, 2:3], ct[:, 0:G], ALU.mult, ALU.add)
                        cv.append(ct)
                    sil = ctr.next()
                    b.act(sil[:, 0:G], cv[0][:, 0:G], AF.Silu)
                    b.tt(actT[:, j, 0:G], sil[:, 0:G], cv[1][:, 0:G], ALU.mult)
                for oc in range(8):
                    pd = P[5 + oc % 2]
                    for j in range(22):
                        b.mm(pd[:, 0:G], wdn[:, j, oc * 128:(oc + 1) * 128], actT[:, j, 0:G], start=(j == 0), stop=(j == 21))
                    b.stt(xg[:, oc, 0:G], pd[:, 0:G], b.mod[:, l, 40 + oc, si:si + 1], xg[:, oc, 0:G], ALU.mult, ALU.add)
                b.dma(ycols, xg[:, :, 0:G])
                if g == ngroups - 1:
                    for q11 in range(11):
                        pcx = P[0]
                        for c in range(8):
                            b.mm(pcx[0:2, 0:512], h2T[:, c, G - 2:G], wup[:, c, q11 * 512:(q11 + 1) * 512], start=(c == 0), stop=(c == 7))
                        fo = for_.next()
                        b.cp(fo[0:2, 0:512], pcx[0:2, 0:512])
                        if prompt:
                            b.dma(o['p_fconv'][l, si, :, q11 * 512:(q11 + 1) * 512], fo[:])
                        else:
                            b.dma(o['s_fconv'][l, :, q11 * 512:(q11 + 1) * 512], fo[:])


def alloc_gdn(b, W, s):
    G = GP
    W['ghist'] = b.sb('ghist', [64, 12, 3], F32, s)
    W['pst'] = Rot([b.sb('pst%d' % i, [64, G + 3], F32, s) for i in range(2)])
    W['cv'] = Rot([b.sb('cv%d' % i, [64, G], F32, s) for i in range(2)])
    W['gsq'] = b.sb('gsq', [64, G], BF16, s)
    W['grn'] = b.sb('grn', [64, G], F32, s)
    W['qkvn'] = b.sb('qkvn', [64, 12, G], BF16, s)
    W['Sf'] = b.sb('Sf', [64, 4, 64], F32, s)
    W['Sb'] = b.sb('Sb', [64, 4, 64], BF16, s)
    W['g3'] = b.sb('g3', [3, 768], F32, s)
    v64 = lambda t, c0: t[0:64, c0:c0 + 256].rearrange("p (h e) -> p h e", e=64)
    hosts32 = [(W['X'][i], c0) for i in range(4) for c0 in (0, 256)]
    f32n = ['zsil', 'dec', 'decI', 'decS', 'Uf', 'gb', 'vtok', 'of']
    for n, (t, c0) in zip(f32n, hosts32):
        W[n] = v64(t, c0)
    W['on'] = b.sb('on', [64, 4, 64], F32, s)[:]
    hosts16 = [(W['Wt'].bufs[i], c0) for i in range(3) for c0 in (0, 256)] + [(W['SP'].bufs[i], 0) for i in range(2)] + \
              [(W['qkb'].bufs[i], c0) for i in range(2) for c0 in (0, 256)]
    for n, (t, c0) in zip(['ktok', 'kd', 'vn', 'AT', 'om'], hosts16):
        W[n] = v64(t, c0)
    hosts32b = [(W['stg'].bufs[i], c0) for i in range(3) for c0 in (0, 256)]
    for n, (t, c0) in zip(['Pa', 'Qa', 'Qb', 'Ya', 'Yb'], hosts32b):
        W[n] = v64(t, c0)
    W['gsm'] = b.sb('gsm', [64, 10, 4], F32, s)


def gdn_seq_init(b, l, si, W):
    d = b.d
    if si < 2:
        b.memset(W['ghist'][:], 0.0)
        b.memset(W['Sf'][:], 0.0)
        b.memset(W['Sb'][:], 0.0)
    else:
        b.dma(W['ghist'][:], d['gconvT'][l])
        b.dma(W['Sf'][:], d['gstate'][l].rearrange("h k v -> k h v"))
        b.cp(W['Sb'][:], W['Sf'][:])


def gdn_group(b, l, si, W, g, G, t0):
    import os
    if os.environ.get('KDBG_NOGDN'):
        b.memset(W['mixedT'][:, 0:2, 0:G], 0.0)
        return
    nc = b.nc
    d, o, C, P, PB = b.d, b.o, b.C, b.P, b.PB
    prompt = si < 2
    Tq = T if prompt else TS
    win, hT = W['win'], W['hT']
    qkvn = W['qkvn']
    gcw = b.gcw
    for hc in range(12):
        pg = P[hc % 2]
        for c in range(8):
            b.mm(pg[0:64, 0:G], win[:, c, hc * 64:(hc + 1) * 64], hT[:, c, 0:G], start=(c == 0), stop=(c == 7))
        pst = W['pst'].next()
        b.cp(pst[:, 0:3], W['ghist'][:, hc, :], e='act')
        b.cp(pst[:, 3:3 + G], pg[0:64, 0:G], e='act')
        b.cp(W['ghist'][:, hc, :], pst[:, G:G + 3], e='act')
        cv = W['cv'].next()
        b.ts(cv[:, 0:G], pst[:, 0:G], gcw[:, l, hc, 0:1], ALU.mult)
        for i in range(1, 4):
            b.stt(cv[:, 0:G], pst[:, i:i + G], gcw[:, l, hc, i:i + 1], cv[:, 0:G], ALU.mult, ALU.add)
        b.act(cv[:, 0:G], cv[:, 0:G], AF.Silu)
        if hc < 8:
            b.act(W['gsq'][:, 0:G], cv[:, 0:G], AF.Square)
            pn = P[2]
            b.mm(pn[0:64, 0:G], C['onesb'][0:64, 0:64], W['gsq'][:, 0:G])
            b.rsqrt(W['grn'][:, 0:G], pn[0:64, 0:G], 1.0)
            if hc < 4:
                b.stt(qkvn[:, hc, 0:G], cv[:, 0:G], 0.125, W['grn'][:, 0:G], ALU.mult, ALU.mult)
            else:
                b.tt(qkvn[:, hc, 0:G], cv[:, 0:G], W['grn'][:, 0:G], ALU.mult)
        else:
            b.cp(qkvn[:, hc, 0:G], cv[:, 0:G], e='act')
    if t0 + G == Tq:
        p3 = P[3]
        for hc in range(8):
            b.tr(p3[0:3, hc * 64:(hc + 1) * 64], W['ghist'][:, hc, :], C['identf'][0:64, 0:64])
        b.cp(W['g3'][:, 0:512], p3[0:3, 0:512])
        p3b = P[2]
        for hc in range(8, 12):
            b.tr(p3b[0:3, (hc - 8) * 64:(hc - 7) * 64], W['ghist'][:, hc, :], C['identf'][0:64, 0:64])
        b.cp(W['g3'][:, 512:768], p3b[0:3, 0:256])
        if prompt:
            b.dma(o['p_gconv'][l, si, :, :], W['g3'][:])
        else:
            b.dma(o['s_gconv'][l, :, :], W['g3'][:])
    Lc = min(64, G)
    nlev = 5 if Lc == 64 else 3
    gsm = W['gsm']
    v3 = lambda t: t[0:Lc, :, 0:Lc]
    pv3 = lambda p, c0: p[0:Lc, c0:c0 + 256].rearrange("p (h e) -> p h e", e=64)
    tri = C['triIncf'][0:Lc, 0:Lc]
    for ch in range(G // Lc):
        cols = slice(ch * Lc, (ch + 1) * Lc)
        pzb = P[2]
        for c in range(8):
            b.mm(pzb[0:Lc, 0:264], hT[:, c, cols], win[:, c, 768:1032], start=(c == 0), stop=(c == 7))
        beta, gg, Gc, eG, neG, eGl, ekd, tmp4 = (gsm[:, i, :] for i in range(8))
        Sf, Sb = W['Sf'], W['Sb']
        pS = P[0]
        for h in range(4):
            b.mm(pS[0:Lc, h * 64:(h + 1) * 64], qkvn[:, 4 + h, cols], Sb[:, h, :])
        for h in range(4):
            b.mm(pS[0:Lc, 256 + h * 64:256 + (h + 1) * 64], qkvn[:, h, cols], Sb[:, h, :])
        pbk = PB[:, 0:512].rearrange("p (a h e) -> p a h e", a=2, e=64)
        for h in range(4):
            b.tr(pbk[0:Lc, 0, h, :], qkvn[:, 4 + h, cols], C['identb'][0:64, 0:64])
            b.tr(pbk[0:Lc, 1, h, :], qkvn[:, 8 + h, cols], C['identb'][0:64, 0:64])
        ktok, vtok, kd = W['ktok'], W['vtok'], W['kd']
        b.cp(ktok[0:Lc, :, :], pbk[0:Lc, 0, :, :])
        b.cp(vtok[0:Lc, :, :], pbk[0:Lc, 1, :, :])
        pKK = P[4]
        for h in range(4):
            b.mm(pKK[0:Lc, h * 64:h * 64 + Lc], qkvn[:, 4 + h, cols], qkvn[:, 4 + h, cols])
        for h in range(4):
            b.mm(pKK[0:Lc, 256 + h * 64:256 + h * 64 + Lc], qkvn[:, 4 + h, cols], qkvn[:, h, cols])
        zsil = W['zsil']
        pzv = pzb[0:Lc, 0:256].rearrange("p (h e) -> p h e", e=64)
        b.act(zsil[0:Lc, :, :], pzv, AF.Exp, scale=-1.0)
        b.act(beta[0:Lc, :], pzb[0:Lc, 256:260], AF.Exp, scale=-1.0)
        b.ts(zsil[0:Lc, :, :], zsil[0:Lc, :, :], 1.0, ALU.add)
        b.recip(zsil[0:Lc, :, :], zsil[0:Lc, :, :])
        b.tt(zsil[0:Lc, :, :], zsil[0:Lc, :, :], pzv, ALU.mult)
        b.ts(beta[0:Lc, :], beta[0:Lc, :], 1.0, ALU.add)
        b.recip(beta[0:Lc, :], beta[0:Lc, :])
        b.tt(tmp4[0:Lc, :], pzb[0:Lc, 260:264], W['dtb'][0:Lc, :], ALU.add)
        b.act(tmp4[0:Lc, :], tmp4[0:Lc, :], AF.Exp)
        b.act(tmp4[0:Lc, :], tmp4[0:Lc, :], AF.Ln, bias=1.0)
        b.tt(gg[0:Lc, :], tmp4[0:Lc, :], W['nexpA'][0:Lc, :], ALU.mult)
        pG = P[3]
        b.mm(pG[0:Lc, 0:4], tri, gg[0:Lc, :])
        b.mm(pG[0:64, 4:8], C['onesf'][0:Lc, 0:64], gg[0:Lc, :])
        b.cp(Gc[0:Lc, :], pG[0:Lc, 0:4])
        b.act(eGl[:, :], pG[0:64, 4:8], AF.Exp)
        b.tt(ekd[0:Lc, :], pG[0:Lc, 4:8], Gc[0:Lc, :], ALU.subtract)
        b.act(ekd[0:Lc, :], ekd[0:Lc, :], AF.Exp)
        b.act(eG[0:Lc, :], Gc[0:Lc, :], AF.Exp)
        b.tt(W['kd'][0:Lc, :, :], W['ktok'][0:Lc, :, :], bc(ekd[0:Lc, :], [Lc, 4, 64]), ALU.mult)
        b.ts(neG[0:Lc, :], eG[0:Lc, :], -1.0, ALU.mult)
        gb = W['gb']
        for h in range(4):
            b.ts(gb[0:Lc, h, 0:Lc], C['onesf'][0:Lc, 0:Lc], gg[0:Lc, h:h + 1], ALU.mult)
        pGr = P[3]
        for h in range(4):
            b.mm(pGr[0:Lc, 256 + h * 64:256 + h * 64 + Lc], gb[0:Lc, h, 0:Lc], tri)
        dec, decI, decS = W['dec'], W['decI'], W['decS']
        for h in range(4):
            b.ts(dec[0:Lc, h, 0:Lc], pGr[0:Lc, 256 + h * 64:256 + h * 64 + Lc], Gc[0:Lc, h:h + 1], ALU.subtract, 0.0, ALU.min)
        b.act(v3(dec), v3(dec), AF.Exp)
        b.tt(v3(decI), v3(dec), mid_bc(tri, 4), ALU.mult)
        b.tt(v3(decS), v3(dec), mid_bc(C['sutf'][0:Lc, 0:Lc], 4), ALU.mult)
        Uf = W['Uf']
        b.tt(v3(Uf), pv3(pKK, 0)[:, :, 0:Lc], v3(decS), ALU.mult)
        b.tt(v3(Uf), v3(Uf), bc(beta[0:Lc, :], [Lc, 4, Lc]), ALU.mult)
        AT = W['AT']
        b.tt(v3(AT), pv3(pKK, 256)[:, :, 0:Lc], v3(decI), ALU.mult)
        Pc, Pn_, Qc, Qn_, Yc, Yn_ = Uf, W['Pa'], W['Qa'], W['Qb'], W['Ya'], W['Yb']
        pq = P[5]
        for h in range(4):
            b.tr(pq[0:Lc, h * 64:h * 64 + Lc], Uf[0:Lc, h, 0:Lc], C['identf'][0:Lc, 0:Lc])
        b.cp(v3(Qc), pv3(pq, 0)[:, :, 0:Lc])
        b.tt(v3(Yc), mid_bc(C['identf'][0:Lc, 0:Lc], 4), v3(Uf), ALU.subtract)
        for k in range(1, nlev + 1):
            pI = P[5]
            for h in range(4):
                b.mm(pI[0:Lc, h * 64:h * 64 + Lc], Pc[0:Lc, h, 0:Lc], Qc[0:Lc, h, 0:Lc])
            if k < nlev:
                for h in range(4):
                    b.mm(pI[0:Lc, 256 + h * 64:256 + h * 64 + Lc], Qc[0:Lc, h, 0:Lc], Pc[0:Lc, h, 0:Lc])
            b.cp(v3(Qn_), pv3(pI, 0)[:, :, 0:Lc], e='act')
            if k < nlev:
                b.cp(v3(Pn_), pv3(pI, 256)[:, :, 0:Lc])
            pY = P[6]
            if k > 1:
                for h in range(4):
                    b.mm(pY[0:Lc, h * 64:h * 64 + Lc], Qc[0:Lc, h, 0:Lc], Yc[0:Lc, h, 0:Lc])
                b.tt(v3(Yn_), v3(Yc), pv3(pY, 0)[:, :, 0:Lc], ALU.add)
                Yc, Yn_ = Yn_, Yc
            Pc, Pn_ = Pn_, Pc
            Qc, Qn_ = Qn_, Qc
        pY = P[6]
        for h in range(4):
            b.mm(pY[0:Lc, h * 64:h * 64 + Lc], Qc[0:Lc, h, 0:Lc], Yc[0:Lc, h, 0:Lc])
        b.tt(v3(Yn_), v3(Yc), pv3(pY, 0)[:, :, 0:Lc], ALU.add)
        Yc, Yn_ = Yn_, Yc
        Rm, vn, of = W['gb'], W['vn'], W['of']
        b.tt(of[0:Lc, :, :], pv3(pS, 0), bc(neG[0:Lc, :], [Lc, 4, 64]), ALU.mult)
        b.tt(Rm[0:Lc, :, :], of[0:Lc, :, :], vtok[0:Lc, :, :], ALU.add)
        b.tt(of[0:Lc, :, :], pv3(pS, 256), bc(eG[0:Lc, :], [Lc, 4, 64]), ALU.mult)
        pX = P[1]
        for h in range(4):
            b.mm(pX[0:Lc, h * 64:(h + 1) * 64], Yc[0:Lc, h, 0:Lc], Rm[0:Lc, h, :])
        b.tt(vn[0:Lc, :, :], pv3(pX, 0), bc(beta[0:Lc, :], [Lc, 4, 64]), ALU.mult)
        for h in range(4):
            b.mm(pX[0:Lc, 256 + h * 64:256 + (h + 1) * 64], AT[0:Lc, h, 0:Lc], vn[0:Lc, h, :])
        b.tt(of[0:Lc, :, :], of[0:Lc, :, :], pv3(pX, 256), ALU.add)
        pSn = P[4]
        for h in range(4):
            b.mm(pSn[0:64, h * 64:(h + 1) * 64], kd[0:Lc, h, :], vn[0:Lc, h, :])
        b.tt(Sf[:, :, :], Sf[:, :, :], bc(eGl[:, :], [64, 4, 64]), ALU.mult)
        b.tt(Sf[:, :, :], Sf[:, :, :], pSn[0:64, 0:256].rearrange("p (h e) -> p h e", e=64), ALU.add)
        b.cp(Sb[:, :, :], Sf[:, :, :], e='act')
        on, om = W['on'], W['om']
        rmsnorm_tok(b, on[0:Lc, :, :], of[0:Lc, :, :], 4, Lc, W['gG'], W['nwk'])
        b.tt(om[0:Lc, :, :], on[0:Lc, :, :], zsil[0:Lc, :, :], ALU.mult)
        pm = PB[:, 768:1024].rearrange("p (a t) -> p a t", t=128)
        omf = om[0:Lc, :, :].rearrange("p h e -> p (h e)")
        for cc in range(2):
            b.tr(pm[:, cc, 0:Lc], omf[:, cc * 128:(cc + 1) * 128], C['identb'][0:Lc, 0:Lc])
        b.cp(W['mixedT'][:, 0:2, ch * Lc:(ch + 1) * Lc], pm[:, 0:2, 0:Lc])
    if t0 + G == Tq:
        if prompt:
            b.dma(o['p_gstate'][l, si].rearrange("h k v -> k h v"), W['Sf'][:])
        else:
            b.dma(o['s_gstate'][l].rearrange("h k v -> k h v"), W['Sf'][:])


_NC_CACHE = {}


def _prep_inputs(inp):
    f = lambda a: np.ascontiguousarray(np.asarray(a, dtype=np.float32))
    L = DEPTH
    shared = {
        'ada_w': f(inp['ada_w']),
        'ada_bT': f(np.asarray(inp['ada_b']).reshape(L, 48, 128).transpose(2, 0, 1)),
        'nmgT': f(np.asarray(inp['norm_mix_g']).reshape(L, 8, 128).transpose(2, 0, 1)),
        'nfgT': f(np.asarray(inp['norm_ffn_g']).reshape(L, 8, 128).transpose(2, 0, 1)),
        'w_in': f(inp['w_in']),
        'gcwT': f(np.asarray(inp['gdn_conv_w']).reshape(L, 4, 12, 64).transpose(3, 0, 2, 1)),
        'a_log': f(inp['gdn_a_log']), 'dt_bias': f(inp['gdn_dt_bias']), 'gn_g': f(inp['gdn_norm_g']),
        'fq_g': f(inp['fox_q_g']), 'fk_g': f(inp['fox_k_g']), 'fb_f': f(inp['fox_b_f']),
        'bq_g': f(inp['band_q_g']), 'bk_g': f(inp['band_k_g']), 'rel': f(inp['band_rel_bias']),
        'mg': f(np.asarray(inp['merge_g']).reshape(L, 768)),
        'w_o': f(inp['w_o']), 'w_up': f(inp['w_up']),
        'fcwT': f(np.asarray(inp['ffn_conv_w']).reshape(L, 3, 44, 128).transpose(3, 0, 2, 1)),
        'w_down': f(inp['w_down']),
    }
    xp = np.asarray(inp['x_prompt'], dtype=np.float32)
    xs = np.asarray(inp['x_sample'], dtype=np.float32)
    cp_ = np.asarray(inp['c_prompt'], dtype=np.float32)
    cs = np.asarray(inp['c_sample'], dtype=np.float32)
    maps = []
    for k in range(8):
        m = dict(shared)
        xT = np.empty((D, NTOK), np.float32)
        xT[:, 0:T] = xp[2 * k].T
        xT[:, T:2 * T] = xp[2 * k + 1].T
        xT[:, 2 * T:] = xs[k].T
        m['xT'] = xT
        cc = np.zeros((4, D), np.float32)
        cc[0], cc[1], cc[2] = cp_[2 * k], cp_[2 * k + 1], cs[k]
        m['cT'] = f(cc.reshape(4, 8, 128).transpose(2, 1, 0))
        m['gconvT'] = f(np.asarray(inp['state_gdn_conv'])[:, k].reshape(L, 3, 12, 64).transpose(0, 3, 2, 1))
        m['gstate'] = f(np.asarray(inp['state_gdn'])[:, k])
        m['sbk'] = f(np.asarray(inp['cache_sb_k'])[:, k].reshape(L, PAST, 256))
        m['sbv'] = f(np.asarray(inp['cache_sb_v'])[:, k].reshape(L, PAST, 256))
        m['fxk'] = f(np.asarray(inp['cache_fox_k'])[:, k].reshape(L, PAST, 256))
        m['fxv'] = f(np.asarray(inp['cache_fox_v'])[:, k].reshape(L, PAST, 256))
        m['fxlf'] = f(np.asarray(inp['cache_fox_logf'])[:, k])
        m['bdk'] = f(np.asarray(inp['cache_band_k'])[:, k].reshape(L, 512, 256))
        m['bdv'] = f(np.asarray(inp['cache_band_v'])[:, k].reshape(L, 512, 256))
        m['fconvT'] = f(np.asarray(inp['state_ffn_conv'])[:, k].reshape(L, 2, 44, 128).transpose(0, 3, 2, 1))
        maps.append(m)
    return maps


def _assemble(res):
    L = DEPTH
    r = res
    yp = np.empty((16, T, D), np.float32)
    ys = np.empty((8, TS, D), np.float32)
    for k in range(8):
        yT = r[k]['yT']
        yp[2 * k] = yT[:, 0:T].T
        yp[2 * k + 1] = yT[:, T:2 * T].T
        ys[k] = yT[:, 2 * T:].T
    cat_p = lambda n, shp: np.concatenate([r[k][n] for k in range(8)], axis=1).reshape(shp)
    stk_s = lambda n, shp: np.stack([r[k][n] for k in range(8)], axis=1).reshape(shp)
    outs = [yp, ys,
            cat_p('p_gconv', (L, 16, 3, 768)), cat_p('p_gstate', (L, 16, 4, 64, 64)),
            cat_p('p_sbk', (L, 16, T, 4, 64)), cat_p('p_sbv', (L, 16, T, 4, 64)),
            cat_p('p_fxk', (L, 16, T, 4, 64)), cat_p('p_fxv', (L, 16, T, 4, 64)),
            cat_p('p_fxlf', (L, 16, T, 4)),
            cat_p('p_bdk', (L, 16, 512, 4, 64)), cat_p('p_bdv', (L, 16, 512, 4, 64)),
            cat_p('p_fconv', (L, 16, 2, 2 * DFF)),
            stk_s('s_gconv', (L, 8, 3, 768)), stk_s('s_gstate', (L, 8, 4, 64, 64)),
            stk_s('s_sbk', (L, 8, TS, 4, 64)), stk_s('s_sbv', (L, 8, TS, 4, 64)),
            stk_s('s_fxk', (L, 8, TS, 4, 64)), stk_s('s_fxv', (L, 8, TS, 4, 64)),
            stk_s('s_fxlf', (L, 8, TS, 4)),
            stk_s('s_bdk', (L, 8, 512, 4, 64)), stk_s('s_bdv', (L, 8, 512, 4, 64)),
            stk_s('s_fconv', (L, 8, 2, 2 * DFF))]
    return tuple(np.ascontiguousarray(a, dtype=np.float32) for a in outs)


def kernel(**inputs):
    maps = _prep_inputs(inputs)
    if 'nc' not in _NC_CACHE:
        _NC_CACHE['nc'] = build()
    res = run_bass_kernel_spmd(_NC_CACHE['nc'], maps, core_ids=list(range(8)))
    return _assemble(res.results)
```
